# Optimizing a Trainium2 kernel written in Bass

```python
import jax, jax.numpy as jnp
from jax import lax
import numpy as np

D_MODEL = 1024
BATCH = 8
SEQ = 8192
DEPTH = 2
DEC_BATCH = 2
DEC_SEQ = 16384
PAST_LEN = 128

N_EVEN = (DEPTH + 1) // 2
N_ODD = DEPTH // 2
D_A = 512
D_B = 512
K_A = 3
K_B = 31
D_C = 768
H_C = 6
DH_C = D_C // H_C
CHUNK = 128
D_D = 256
G_D = 4
DG_D = D_D // G_D
D_FF = ((8 * D_MODEL // 3 + 255) // 256) * 256
N_MOD = 6
EPS = 1e-6

kernel_name = "hybrid_bidir_conv_gmlp_fnet_encoder"


def rmsnorm(x, g):
    x32 = x.astype(jnp.float32)
    y = x32 * lax.rsqrt(jnp.mean(x32 * x32, axis=-1, keepdims=True) + EPS)
    return y.astype(x.dtype) * g


def layernorm(x, g, b):
    x32 = x.astype(jnp.float32)
    mu = jnp.mean(x32, axis=-1, keepdims=True)
    var = jnp.mean(jnp.square(x32 - mu), axis=-1, keepdims=True)
    y = (x32 - mu) * lax.rsqrt(var + EPS)
    return y.astype(x.dtype) * g + b


def depthwise_conv(x, w):
    k, ch = w.shape
    return lax.conv_general_dilated(
        x, w[:, None, :].astype(x.dtype), window_strides=(1,), padding=[(k // 2, k // 2)],
        dimension_numbers=("NWC", "WIO", "NWC"), feature_group_count=ch)


def mixer_ab(h, w_in, conv_a, conv_b_w, conv_b_b, ln_g, ln_b, w_out):
    p = h @ w_in
    a_b, a_c, a_x, b_val, b_gate = jnp.split(
        p, [D_A, 2 * D_A, 3 * D_A, 3 * D_A + D_B], axis=-1)
    y_a = a_b * depthwise_conv(a_c * a_x, conv_a)
    g = b_val * jax.nn.sigmoid(b_gate)
    z = depthwise_conv(g, conv_b_w) + conv_b_b
    y_b = jax.nn.silu(layernorm(z, ln_g, ln_b))
    return jnp.concatenate([y_a, y_b], axis=-1) @ w_out


def mixer_cd(h, w_in, ln_g, ln_b, w_s, b_s, w_out):
    bsz, s, _ = h.shape
    p = h @ w_in
    u, v, f = jnp.split(p, [D_C, 2 * D_C], axis=-1)
    v = layernorm(v, ln_g, ln_b)
    vc = v.reshape(bsz, s // CHUNK, CHUNK, H_C, DH_C)
    sv = jnp.einsum('hpq,bnqhd->bnphd', w_s, vc) + b_s.T[None, None, :, :, None]
    y_c = u * sv.reshape(bsz, s, D_C)
    fg = f.reshape(bsz, s, G_D, DG_D).astype(jnp.float32)
    y_d = jnp.fft.fftn(fg, axes=(1, 3), norm="ortho").real.astype(h.dtype).reshape(bsz, s, D_D)
    return jnp.concatenate([y_c, y_d], axis=-1) @ w_out


def swiglu(h, w_in, w_out):
    gate, up = jnp.split(h @ w_in, 2, axis=-1)
    return (jax.nn.silu(gate) * up) @ w_out


def trunk(x, c, ada_w, ada_b, mix_norm_g, ffn_norm_g,
          ab_w_in, ab_conv_a, ab_conv_b_w, ab_conv_b_b, ab_ln_g, ab_ln_b, ab_w_out,
          cd_w_in, cd_ln_g, cd_ln_b, cd_w_s, cd_b_s, cd_w_out,
          ffn_w_in, ffn_w_out, final_g):
    sc = jax.nn.silu(c)
    for l in range(DEPTH):
        mod = (sc @ ada_w[l] + ada_b[l])[:, None, :]
        sh1, sc1, g1, sh2, sc2, g2 = jnp.split(mod, N_MOD, axis=-1)
        h = rmsnorm(x, mix_norm_g[l]) * (1 + sc1) + sh1
        if l % 2 == 0:
            i = l // 2
            m = mixer_ab(h, ab_w_in[i], ab_conv_a[i], ab_conv_b_w[i], ab_conv_b_b[i],
                         ab_ln_g[i], ab_ln_b[i], ab_w_out[i])
        else:
            i = l // 2
            m = mixer_cd(h, cd_w_in[i], cd_ln_g[i], cd_ln_b[i], cd_w_s[i], cd_b_s[i], cd_w_out[i])
        x = x + g1 * m
        h = rmsnorm(x, ffn_norm_g[l]) * (1 + sc2) + sh2
        x = x + g2 * swiglu(h, ffn_w_in[l], ffn_w_out[l])
    return rmsnorm(x, final_g)


def setup_inputs(seed: int = 0) -> dict:
    key = jax.random.key(seed)
    ks = jax.random.split(key, 32)
    f32 = jnp.float32
    D = D_MODEL

    def nrm(k, shape, scale):
        return jax.random.normal(k, shape, f32) * scale

    return {
        "x_prompt": nrm(ks[0], (BATCH, SEQ, D), 1.0),
        "x_sample": nrm(ks[1], (DEC_BATCH, DEC_SEQ, D), 1.0),
        "c_prompt": nrm(ks[2], (BATCH, D), 1.0),
        "c_sample": nrm(ks[3], (DEC_BATCH, D), 1.0),
        "ada_w": nrm(ks[4], (DEPTH, D, N_MOD * D), D ** -0.5),
        "ada_b": nrm(ks[5], (DEPTH, N_MOD * D), 0.02),
        "mix_norm_g": 1.0 + nrm(ks[6], (DEPTH, D), 0.02),
        "ffn_norm_g": 1.0 + nrm(ks[7], (DEPTH, D), 0.02),
        "ab_w_in": nrm(ks[8], (N_EVEN, D, 3 * D_A + 2 * D_B), D ** -0.5),
        "ab_conv_a": nrm(ks[9], (N_EVEN, K_A, D_A), K_A ** -0.5),
        "ab_conv_b_w": nrm(ks[10], (N_EVEN, K_B, D_B), K_B ** -0.5),
        "ab_conv_b_b": nrm(ks[11], (N_EVEN, D_B), 0.02),
        "ab_ln_g": 1.0 + nrm(ks[12], (N_EVEN, D_B), 0.02),
        "ab_ln_b": nrm(ks[13], (N_EVEN, D_B), 0.02),
        "ab_w_out": nrm(ks[14], (N_EVEN, D_A + D_B, D), (D_A + D_B) ** -0.5),
        "cd_w_in": nrm(ks[15], (N_ODD, D, 2 * D_C + D_D), D ** -0.5),
        "cd_ln_g": 1.0 + nrm(ks[16], (N_ODD, D_C), 0.02),
        "cd_ln_b": nrm(ks[17], (N_ODD, D_C), 0.02),
        "cd_w_s": nrm(ks[18], (N_ODD, H_C, CHUNK, CHUNK), CHUNK ** -0.5),
        "cd_b_s": nrm(ks[19], (N_ODD, H_C, CHUNK), 0.02),
        "cd_w_out": nrm(ks[20], (N_ODD, D_C + D_D, D), (D_C + D_D) ** -0.5),
        "ffn_w_in": nrm(ks[21], (DEPTH, D, 2 * D_FF), D ** -0.5),
        "ffn_w_out": nrm(ks[22], (DEPTH, D_FF, D), D_FF ** -0.5),
        "final_g": 1.0 + nrm(ks[23], (D,), 0.02),
    }


def reference(x_prompt, x_sample, c_prompt, c_sample, ada_w, ada_b, mix_norm_g, ffn_norm_g,
              ab_w_in, ab_conv_a, ab_conv_b_w, ab_conv_b_b, ab_ln_g, ab_ln_b, ab_w_out,
              cd_w_in, cd_ln_g, cd_ln_b, cd_w_s, cd_b_s, cd_w_out,
              ffn_w_in, ffn_w_out, final_g):
    y_prompt = trunk(x_prompt, c_prompt, ada_w, ada_b, mix_norm_g, ffn_norm_g,
                     ab_w_in, ab_conv_a, ab_conv_b_w, ab_conv_b_b, ab_ln_g, ab_ln_b, ab_w_out,
                     cd_w_in, cd_ln_g, cd_ln_b, cd_w_s, cd_b_s, cd_w_out,
                     ffn_w_in, ffn_w_out, final_g)
    y_sample = trunk(x_sample, c_sample, ada_w, ada_b, mix_norm_g, ffn_norm_g,
                     ab_w_in, ab_conv_a, ab_conv_b_w, ab_conv_b_b, ab_ln_g, ab_ln_b, ab_w_out,
                     cd_w_in, cd_ln_g, cd_ln_b, cd_w_s, cd_b_s, cd_w_out,
                     ffn_w_in, ffn_w_out, final_g)
    return (y_prompt, y_sample)
```

```python
import numpy as np
import ml_dtypes
from contextlib import ExitStack
import concourse.bass as bass
import concourse.mybir as mybir
from concourse.bass_utils import run_bass_kernel_spmd

F32 = mybir.dt.float32
BF16 = mybir.dt.bfloat16
AF = mybir.ActivationFunctionType
ALU = mybir.AluOpType

D = 1024
KC = 8
T = 512
HALO = 16
TW = T + 2 * HALO
HW_ = TW // 2
SP_LEN = 8192
SS_LEN = 4096
NT_S = SS_LEN // T
NT_P = SP_LEN // T
NTILES = NT_S + NT_P
NTOK = SS_LEN + SP_LEN
DFF = 2816
NFF = DFF // 128
EPS = 1e-6
NSLOT = 8

CB_ONES = 0
CB_D0 = 128
CB_W1P1 = 384
CB_W1P2 = 512
CB_W1S1 = 640
CB_W1S2 = 896
CB_W2P = 1152
CB_W2S = 1408
CB_N = 1472
CF_ID = 0
CF_ONES512 = 128
CF_TP = 256
CF_TS = 384
CF_MSK = 640
CF_N = 642


class Buf:
    __slots__ = ("name", "w", "r")

    def __init__(self, name):
        self.name = name
        self.w = None
        self.r = {}


class Tracker:
    ENGS = ("pe", "act", "dve", "pool", "sp")

    def __init__(self, nc, stack):
        self.nc = nc
        self.stack = stack
        self.q = {e: [] for e in self.ENGS}
        self.cnt = {}
        self.semh = {}
        self.isdma = {}
        self.seen = {e: {} for e in self.ENGS}
        for e in ("pe", "act", "dve", "pool"):
            self.newsem(e, False)
        self.nops = 0

    def newsem(self, key, isdma=True):
        self.semh[key] = self.stack.enter_context(self.nc.semaphore("s_" + key))
        self.cnt[key] = 0
        self.isdma[key] = isdma

    def wait(self, eng, dep):
        if dep is None:
            return
        k, v = dep
        if self.isdma[k]:
            v = self.cnt[k]
        elif k == "pe" and eng == "pe":
            return
        if self.seen[eng].get(k, 0) >= v:
            return
        self.seen[eng][k] = v
        sem = self.semh[k]
        self.q[eng].append(lambda e, sem=sem, v=v: e.wait_ge(sem, v))

    def _deps(self, eng, reads, writes):
        for b in reads:
            self.wait(eng, b.w)
        for b in writes:
            self.wait(eng, b.w)
            for k, v in list(b.r.items()):
                self.wait(eng, (k, v))

    def _post(self, dep, reads, writes):
        k, v = dep
        for b in reads:
            if b.r.get(k, 0) < v:
                b.r[k] = v
        for b in writes:
            b.w = dep
            b.r = {}

    def op(self, eng, fn, reads=(), writes=()):
        self._deps(eng, reads, writes)
        self.cnt[eng] += 1
        v = self.cnt[eng]
        sem = self.semh[eng]
        self.q[eng].append(lambda e, fn=fn, sem=sem: fn(e).then_inc(sem, 1))
        self._post((eng, v), reads, writes)
        self.nops += 1

    def dma(self, eng, key, out, in_, reads=(), writes=(), **kw):
        self._deps(eng, reads, writes)
        self.cnt[key] += 16
        v = self.cnt[key]
        sem = self.semh[key]
        self.q[eng].append(lambda e, out=out, in_=in_, sem=sem, kw=kw: e.dma_start(out=out, in_=in_, **kw).then_inc(sem, 16))
        self._post((key, v), reads, writes)

    def custom(self, eng, key, inc, fn, reads=(), writes=()):
        self._deps(eng, reads, writes)
        self.cnt[key] += inc
        v = self.cnt[key]
        sem = self.semh[key]
        self.q[eng].append(lambda e, fn=fn, sem=sem, inc=inc: fn(e).then_inc(sem, inc))
        self._post((key, v), reads, writes)

    def final_wait(self, eng, keys):
        for k in keys:
            if self.cnt[k] > 0:
                self.wait(eng, (k, self.cnt[k]))


def build_program(debug=None, plan_specs=None):
    nc = bass.Bass("TRN2", target_bir_lowering=False)
    stack = ExitStack()
    tr = Tracker(nc, stack)

    def din(name, shape, dt=F32):
        return nc.dram_tensor(name, list(shape), dt, kind="ExternalInput").ap()

    def dscr(name, shape, dt):
        return nc.dram_tensor(name, list(shape), dt).ap()

    xp = din("xp", [SP_LEN + 2 * HALO, D])
    xs = din("xs", [SS_LEN + 2 * HALO, D])
    cvec = din("cvec", [2, D])
    ada_w = din("ada_w", [2, D, 6 * D])
    ada_b = din("ada_b", [2, 6 * D])
    mix_norm_g = din("mix_norm_g", [2, D])
    ffn_norm_g = din("ffn_norm_g", [2, D])
    ab_w_in = din("ab_w_in", [D, 2560])
    ab_conv_a = din("ab_conv_a", [3, 512])
    ab_conv_b_w = din("ab_conv_b_w", [31, 512])
    ab_conv_b_b = din("ab_conv_b_b", [512])
    ab_ln_g = din("ab_ln_g", [512])
    ab_ln_b = din("ab_ln_b", [512])
    ab_w_out = din("ab_w_out", [D, D])
    cd_w_in = din("cd_w_in", [D, 1792])
    cd_ln_g = din("cd_ln_g", [768])
    cd_ln_b = din("cd_ln_b", [768])
    cd_w_s = din("cd_w_s", [6, 128, 128])
    cd_b_s = din("cd_b_s", [6, 128])
    cd_w_out = din("cd_w_out", [D, D])
    ffn_w_in = din("ffn_w_in", [2, D, 2 * DFF])
    ffn_w_out = din("ffn_w_out", [2, DFF, D])
    final_g = din("final_g", [D])
    cb_d = din("cb", [128, CB_N], BF16)
    cf_d = din("cf", [128, CF_N])
    yp = nc.dram_tensor("yp", [SP_LEN, D], F32, kind="ExternalOutput").ap()
    ys = nc.dram_tensor("ys", [SS_LEN, D], F32, kind="ExternalOutput").ap()

    w_in0_s = dscr("w_in0_s", [20, 128, KC, 128], BF16)
    w_out0_s = dscr("w_out0_s", [8, 128, KC, 128], BF16)
    ffi_s = [dscr("ffi0_s", [44, 128, KC, 128], BF16), dscr("ffi1_s", [44, 128, KC, 128], BF16)]
    ffo_s = [dscr("ffo0_s", [8, 128, NFF, 128], BF16), dscr("ffo1_s", [8, 128, NFF, 128], BF16)]
    cdu_s = dscr("cdu_s", [8, 128, KC, 128], BF16)
    cdv_s = dscr("cdv_s", [KC, 128, 768], BF16)
    w_out1_s = dscr("w_out1_s", [8, 128, KC, 128], BF16)
    X2 = dscr("X2", [NTILES, 128, KC, T], F32)
    YC = dscr("YC", [NTILES, 128, 6, T], BF16)
    G_P = dscr("G_P", [64, 4, 128, 128], BF16)
    G_S = [dscr("G_S%d" % j, [4, 4, 128, 128], BF16) for j in range(NT_S)]
    AG_all = dscr("AG_all", [NT_S, 4, 4, 4, 128, 128], BF16)
    AG = [AG_all[j] for j in range(NT_S)]
    YD = dscr("YD", [2, 128, NTOK], BF16)
    b_X2 = [Buf("X2_%d" % i) for i in range(NTILES)]
    b_YC = [Buf("YC_%d" % i) for i in range(NTILES)]
    b_GP, b_YD = Buf("GP"), Buf("YD")
    b_GS = [Buf("GS%d" % j) for j in range(NT_S)]
    b_AG = [Buf("AG%d" % j) for j in range(NT_S)]

    def sbg(name, shape, dt=F32):
        return nc.alloc_sbuf_tensor(name, list(shape), dt), Buf(name)

    cb, b_cb = sbg("cb_sb", [128, CB_N], BF16)
    cf, b_cf = sbg("cf_sb", [128, CF_N])
    ring = nc.alloc_sbuf_tensor("ring", [128, NSLOT, 1024], BF16)
    b_ring = [Buf("ring%d" % i) for i in range(NSLOT)]
    for i in range(NSLOT):
        tr.newsem("ring%d" % i)
    modr, b_modr = sbg("modr", [128, 2, 48, 2])
    gm, b_gm = sbg("gm", [128, 2, 2, KC, 2])
    adab, b_adab = sbg("adab", [128, 2, 48])
    ngm, b_ngm = sbg("ngm", [128, 2, 2, KC])
    fgm, b_fgm = sbg("fgm", [128, KC])
    cT, b_cT = sbg("cT", [128, KC, 2])
    scT, b_scT = sbg("scT", [128, KC, 2], BF16)
    wA, b_wA = sbg("wA", [128, 3, 4])
    wB, b_wB = sbg("wB", [128, 31, 4])
    bB, b_bB = sbg("bB", [128, 4])
    lngB, b_lngB = sbg("lngB", [128, 4])
    lnbB, b_lnbB = sbg("lnbB", [128, 4])
    lngC, b_lngC = sbg("lngC", [128, 6])
    wsT, b_wsT = sbg("wsT", [128, 6, 128], BF16)
    Cst, b_Cst = sbg("Cst", [128, 6, 128])
    st4s = [sbg("st4_%d" % i, [128, 8]) for i in range(4)]
    epsb, b_epsb = sbg("epsb", [128, 1])
    vres_scope = ExitStack()
    vres, b_vres = vres_scope.enter_context(nc.sbuf_tensor("vres", [128, KC, 768], BF16)), Buf("vres")

    pp = nc.alloc_psum_tensor("pp", [128, 8, 512], F32)
    b_pp = [Buf("bank%d" % i) for i in range(8)]
    bank_ctr = [0]

    def bank():
        i = bank_ctr[0] % 8
        bank_ctr[0] += 1
        return pp[:, i, :], b_pp[i]

    def bankpair():
        if bank_ctr[0] % 2:
            bank_ctr[0] += 1
        i = bank_ctr[0] % 8
        bank_ctr[0] += 2
        return pp[:, i:i + 2, :], [b_pp[i], b_pp[i + 1]]

    DMAKEYS = ["ld", "st", "xl0", "xl1", "xl2", "xl3", "xl4", "castA", "castB", "castC", "castD", "castE", "castF",
               "cc", "g", "ph2", "out", "p3a", "p3b", "vres"]
    for k in DMAKEYS:
        tr.newsem(k)
    DMAKEYS += ["ring%d" % i for i in range(NSLOT)]

    def barrier():
        for e in Tracker.ENGS:
            for k in ("pe", "act", "dve", "pool"):
                if tr.cnt[k] > 0:
                    tr.wait(e, (k, tr.cnt[k]))
            for k in DMAKEYS:
                if tr.cnt[k] > 0 and not k.startswith("cast") and k != "vres":
                    tr.wait(e, (k, tr.cnt[k]))

    ident = cf[:, CF_ID:CF_ID + 128]
    ones_bf = cb[:, CB_ONES:CB_ONES + 128]
    ones512 = cf[:, CF_ONES512:CF_ONES512 + 128]

    def mm_group(out_ap, pairs, reads, writes):
        def fn(e, out_ap=out_ap, pairs=pairs):
            n = len(pairs)
            last = None
            for i, (l, r) in enumerate(pairs):
                last = e.matmul(out_ap, l, r, start=(i == 0), stop=(i == n - 1))
            return last
        tr.op("pe", fn, reads, writes)

    tr.op("dve", lambda e: e.memset(epsb[:], EPS), writes=[b_epsb])
    tr.dma("sp", "ld", cb[:], cb_d, writes=[b_cb])
    tr.dma("sp", "ld", cf[:], cf_d, writes=[b_cf])

    SETUP_T_MARK = None

    with ExitStack() as s0:
        def sbt(name, shape, dt=F32):
            return s0.enter_context(nc.sbuf_tensor(name, list(shape), dt)), Buf(name)
        lnbC, b_lnbC = sbt("lnbC", [128, 768])
        ws_nat, b_wsnat = sbt("ws_nat", [128, 6, 128])
        wsTf, b_wsTf = sbt("wsTf", [128, 6, 128])
        bsr, b_bsr = sbt("bsr", [1, 6, 128])
        ones1, b_ones1 = sbt("ones1", [1, 128])
        tr.dma("sp", "ld", lnbC[:], cd_ln_b.partition_broadcast(128), writes=[b_lnbC])
        tr.dma("sp", "ld", ws_nat[:], cd_w_s.rearrange("h p q -> p h q"), writes=[b_wsnat])
        tr.dma("sp", "ld", bsr[:], cd_b_s.rearrange("(o h) p -> o h p", o=1), writes=[b_bsr])
        tr.op("dve", lambda e: e.memset(ones1[:], 1.0), writes=[b_ones1])
        stg, b_stg = sbt("stg", [48, 1024])

        def load_T(src2d, R, C, dst_of_c, b_dst):
            tr.dma("sp", "ld", stg[0:R, 0:C * 128], src2d, writes=[b_stg])
            for c in range(C):
                pb, b_pb = bank()
                tr.op("pe", lambda e, c=c, pb=pb, R=R: e.transpose(pb[:, 0:R], stg[0:R, c * 128:(c + 1) * 128], ident[0:R, 0:R]),
                      reads=[b_stg, b_cf], writes=[b_pb])
                tr.op("dve", lambda e, c=c, pb=pb, R=R: e.tensor_copy(out=dst_of_c(c), in_=pb[:, 0:R]), reads=[b_pb], writes=[b_dst])

        load_T(cvec, 2, 8, lambda c: cT[:, c, :], b_cT)
        for l in range(2):
            load_T(ada_b[l].rearrange("(n p) -> n p", p=128), 48, 1, lambda c, l=l: adab[:, l, :], b_adab)
            load_T(mix_norm_g[l].rearrange("(k p) -> k p", p=128), 8, 1, lambda c, l=l: ngm[:, l, 0, :], b_ngm)
            load_T(ffn_norm_g[l].rearrange("(k p) -> k p", p=128), 8, 1, lambda c, l=l: ngm[:, l, 1, :], b_ngm)
        load_T(final_g.rearrange("(k p) -> k p", p=128), 8, 1, lambda c: fgm[:], b_fgm)
        load_T(ab_conv_a, 3, 4, lambda c: wA[:, :, c], b_wA)
        load_T(ab_conv_b_w, 31, 4, lambda c: wB[:, :, c], b_wB)
        tr.op("dve", lambda e: e.tensor_scalar(out=wB[:], in0=wB[:], scalar1=0.5, scalar2=None, op0=ALU.mult), reads=[b_wB], writes=[b_wB])
        load_T(ab_conv_b_b.rearrange("(g p) -> g p", p=128), 4, 1, lambda c: bB[:], b_bB)
        load_T(ab_ln_g.rearrange("(g p) -> g p", p=128), 4, 1, lambda c: lngB[:], b_lngB)
        load_T(ab_ln_b.rearrange("(g p) -> g p", p=128), 4, 1, lambda c: lnbB[:], b_lnbB)
        load_T(cd_ln_g.rearrange("(g p) -> g p", p=128), 6, 1, lambda c: lngC[:], b_lngC)
        for h in range(6):
            pb, b_pb = bank()
            tr.op("pe", lambda e, h=h, pb=pb: e.transpose(pb[:, 0:128], ws_nat[:, h, :], ident),
                  reads=[b_wsnat, b_cf], writes=[b_pb])
            tr.op("dve", lambda e, h=h, pb=pb: e.tensor_copy(out=wsT[:, h, :], in_=pb[:, 0:128]), reads=[b_pb], writes=[b_wsT])
            tr.op("act", lambda e, h=h, pb=pb: e.activation(out=wsTf[:, h, :], in_=pb[:, 0:128], func=AF.Copy), reads=[b_pb], writes=[b_wsTf])
        for h in range(6):
            pb, b_pb = bank()
            mm_group(pb[:, 0:128], [(lnbC[:, h * 128:(h + 1) * 128], wsTf[:, h, :]), (ones1[:], bsr[:, h, :])],
                     reads=[b_lnbC, b_wsTf, b_ones1, b_bsr], writes=[b_pb])
            tr.op("dve", lambda e, h=h, pb=pb: e.tensor_copy(out=Cst[:, h, :], in_=pb[:, 0:128]), reads=[b_pb], writes=[b_Cst])
        barrier()


    b_cast = {k: Buf(k) for k in ("castA", "castB", "castC", "castD", "castE", "castF")}

    def cast_units(key, dst, src, ncols0, nunits):
        for u in range(nunits):
            c0 = ncols0 + u * 128
            tr.dma("pool", key, dst[u], src[:, c0:c0 + 128].rearrange("(k p) n -> p k n", p=128), writes=[b_cast[key]])

    cast_units("castA", w_in0_s, ab_w_in, 0, 20)
    cast_units("castB", w_out0_s, ab_w_out, 0, 8)
    cast_units("castC", ffi_s[0], ffn_w_in[0], 0, 44)
    cast_units("castD", ffo_s[0], ffn_w_out[0], 0, 8)
    cast_units("castE", cdu_s[0:6], cd_w_in, 0, 6)
    cast_units("castE", cdu_s[6:8], cd_w_in, 1536, 2)
    for k in range(KC):
        tr.dma("pool", "castE", cdv_s[k], cd_w_in[k * 128:(k + 1) * 128, 768:1536], writes=[b_cast["castE"]])
    tr.dma("pool", "vres", vres[:], cdv_s.rearrange("k p n -> p k n"), reads=[b_cast["castE"]], writes=[b_vres])
    cast_units("castF", w_out1_s, cd_w_out, 0, 8)
    cast_units("castF", ffi_s[1], ffn_w_in[1], 0, 44)
    cast_units("castF", ffo_s[1], ffn_w_out[1], 0, 8)

    def resolve(spec):
        kind = spec[0]
        if kind == "w_in0":
            return w_in0_s[spec[1]], b_cast["castA"]
        if kind == "w_out0":
            return w_out0_s[spec[1]], b_cast["castB"]
        if kind == "ffi":
            return ffi_s[spec[1]][spec[2]], b_cast["castC" if spec[1] == 0 else "castF"]
        if kind == "ffo":
            return ffo_s[spec[1]][spec[2]][:, spec[3]:spec[4], :], b_cast["castD" if spec[1] == 0 else "castF"]
        if kind == "cdu":
            return cdu_s[spec[1]], b_cast["castE"]
        if kind == "cdv":
            return cdv_s[spec[1]], b_cast["castE"]
        if kind == "w_out1":
            return w_out1_s[spec[1]], b_cast["castF"]
        raise ValueError(spec)

    recorded = []
    ring_state = {"issued": 0, "used": 0}

    def ring_issue():
        if plan_specs is None:
            return
        u = ring_state["issued"]
        if u >= len(plan_specs):
            return
        ring_state["issued"] += 1
        src, cbuf = resolve(plan_specs[u])
        s_ = u % NSLOT
        shp = list(src.shape)
        n = 1
        for d_ in shp[1:]:
            n *= d_
        dst = ring[:, s_, 0:n]
        if len(shp) == 3:
            dst = dst.rearrange("p (k n) -> p k n", n=shp[2])
        tr.dma("sp", "ring%d" % s_, dst, src, reads=[cbuf], writes=[b_ring[s_]])

    def ring_next(spec):
        u = ring_state["used"]
        ring_state["used"] += 1
        recorded.append(spec)
        if plan_specs is not None:
            assert plan_specs[u] == spec, (u, plan_specs[u], spec)
        s_ = u % NSLOT
        return ring[:, s_, :], b_ring[s_]

    def ring_done():
        ring_issue()

    tr.op("act", lambda e: e.activation(out=scT[:], in_=cT[:], func=AF.Silu), reads=[b_cT], writes=[b_scT])
    with ExitStack() as s1:
        adaf = [(s1.enter_context(nc.sbuf_tensor("adaf%d" % i, [128, KC, 512], F32)), Buf("adaf%d" % i)) for i in range(2)]
        adab16 = [(s1.enter_context(nc.sbuf_tensor("adab16_%d" % i, [128, KC, 512], BF16)), Buf("adab16_%d" % i)) for i in range(2)]
        for i in range(2):
            tr.newsem("ada%d" % i)
            DMAKEYS.append("ada%d" % i)
        for l in range(2):
            pb, b_pb = bank()
            for nb_ in range(12):
                i = (l * 12 + nb_) % 2
                af, b_af = adaf[i]
                a16, b_a16 = adab16[i]
                tr.dma("sp", "ada%d" % i, af[:], ada_w[l][:, nb_ * 512:(nb_ + 1) * 512].rearrange("(k p) n -> p k n", p=128), writes=[b_af])
                tr.op("act", lambda e, af=af, a16=a16: e.activation(out=a16[:, 0:4], in_=af[:, 0:4], func=AF.Copy), reads=[b_af], writes=[b_a16])
                tr.op("dve", lambda e, af=af, a16=a16: e.tensor_copy(out=a16[:, 4:8], in_=af[:, 4:8]), reads=[b_af], writes=[b_a16])
                for q_ in range(4):
                    n = nb_ * 4 + q_
                    mm_group(pb[:, 2 * n:2 * n + 2], [(a16[:, k, q_ * 128:(q_ + 1) * 128], scT[:, k, :]) for k in range(KC)],
                             reads=[b_a16, b_scT], writes=[b_pb])
            tr.op("dve", lambda e, l=l, pb=pb: e.tensor_tensor(
                out=modr[:, l], in0=pb[:, 0:96].rearrange("p (n s) -> p n s", s=2),
                in1=adab[:, l, :].unsqueeze(2).broadcast_to([128, 48, 2]), op=ALU.add),
                reads=[b_pb, b_adab], writes=[b_modr])
        barrier()
    for _ in range(NSLOT):
        ring_issue()

    for l in range(2):
        for wch, kind in ((0, 1), (1, 4)):
            tr.op("dve", lambda e, l=l, wch=wch, kind=kind: e.scalar_tensor_tensor(
                out=gm[:, l, wch], in0=modr[:, l, kind * 8:(kind + 1) * 8, :], scalar=1.0,
                in1=ngm[:, l, wch, :].unsqueeze(2).broadcast_to([128, KC, 2]), op0=ALU.add, op1=ALU.mult),
                reads=[b_modr, b_ngm], writes=[b_gm])

    def MOD(l, kind, kc, si):
        return modr[:, l, kind * 8 + kc, si:si + 1]

    def GM(l, wch, kc, si):
        return gm[:, l, wch, kc, si:si + 1]

    def make_env(phase):
        ph = ExitStack()
        P1 = (phase == 1)

        def sbp(name, shape, dt=F32):
            return ph.enter_context(nc.sbuf_tensor("%s_p%d" % (name, phase), list(shape), dt)), Buf(name)

        xTa, b_xTa = sbp("xTa", [128, KC, TW])
        sq, b_sq = sbp("sq", [128, KC, TW], BF16)
        hTa, b_hTa = sbp("hTa", [128, KC, TW], BF16)
        hTb, b_hTb = sbp("hTb", [128, KC, TW], BF16)
        rstd, b_rstd = sbp("rstd", [128, TW])
        hid, b_hid = sbp("hid", [128, NFF, T], BF16)
        tmpg = [sbp("tmpg%d" % i, [128, T]) for i in range(2)]
        xTb, b_xTb = sbp("xTb", [128, KC, TW])
        nts = [sbp("nt%d" % i, [128, TW]) for i in range(2)]
        if P1:
            xtok = [sbp("xtok%d" % i, [128, D]) for i in range(3)]
            mix, _ = sbp("mix", [128, 9984])
            cz, b_cz = sbp("cz", [128, 8, T])
            zsq, b_zsq = sbp("zsq", [128, 4, T])
        else:
            otok = [sbp("otok%d" % i, [128, 4, D]) for i in range(2)]
            yc3 = [sbp("yc3_%d" % i, [128, 6, T], BF16) for i in range(2)]
            ydt = [sbp("ydt%d" % i, [128, 2, T], BF16) for i in range(2)]
            mix, _ = sbp("mix", [128, 16])
            tbuf3, b_tbuf3 = sbp("tbuf3", [128, KC, TW])
            sq2, b_sq2 = sbp("sq2", [128, KC, TW], BF16)
            rstd2, b_rstd2 = sbp("rstd2", [128, TW])
        if P1:
            tbuf, b_tbuf = None, None
        else:
            tbuf, b_tbuf = tbuf3, b_tbuf3

        if P1:
            def mixv(a, b, dt=F32):
                v = mix[:, a:b]
                return v.bitcast(BF16) if dt is BF16 else v
            ab_t, b_ab = mixv(0, 2048).rearrange("p (g t) -> p g t", t=T), Buf("ab")
            ca, b_ca = mixv(2048, 3136, BF16).rearrange("p (g t) -> p g t", t=TW), Buf("ca")
            gB, b_gB = mixv(3136, 4224, BF16).rearrange("p (g t) -> p g t", t=TW), Buf("gB")
            tmpc = [(mixv(4224 + i * 544, 4768 + i * 544).rearrange("p (h c) -> p h c", c=HW_), Buf("tmpc%d" % i)) for i in range(2)]
            tmps = [(mixv(5312 + i * 544, 5856 + i * 544).rearrange("p (h c) -> p h c", c=HW_), Buf("tmps%d" % i)) for i in range(2)]
            lnt = [(mixv(6400 + i * 512, 6912 + i * 512), Buf("lnt%d" % i)) for i in range(3)]
            yT, b_yT = mixv(7936, 9984, BF16).rearrange("p (k t) -> p k t", t=T), Buf("yT")
            czf = cz[:].rearrange("p k t -> p (k t)")
            zsqf = zsq[:].rearrange("p g t -> p (g t)")
            uT, b_uT = czf[:, 0:3072].rearrange("p (k t) -> p k t", t=T), [b_cz]
            Gst, b_Gst = czf[:, 3072:4096].bitcast(BF16).rearrange("p (r t) -> p r t", t=T), [b_cz]
            ycT, b_ycT = zsqf[:, 0:1536].bitcast(BF16).rearrange("p (k t) -> p k t", t=T), [b_zsq]
            fT, b_fT = zsqf[:, 1536:2048].bitcast(BF16).rearrange("p (c t) -> p c t", t=T), [b_zsq]
            vfs = [(mixv(7936, 8704), [b_yT]), (mixv(7936, 8704), [b_yT])]
            t6, b_t6 = mixv(8704, 9472).rearrange("p (h c) -> p h c", c=128), [b_yT]
            vn = [(mixv(9472, 9856, BF16), [b_yT]), (mixv(7552, 7936, BF16), [lnt[2][1]])]
            vsq, b_vsq = mixv(6400, 7168), [lnt[0][1], lnt[1][1]]

        def rms_to_h(xT, b_xT, l, wch, si, c0, ncols, plain_gain=None, hT=None, b_hT=None, yielding=True, scr=None):
            sq_, b_sq_, rstd_, b_rstd_ = scr if scr is not None else (sq, b_sq, rstd, b_rstd)
            c1 = c0 + ncols
            tr.op("act", lambda e: e.activation(out=sq_[:, :, c0:c1], in_=xT[:, :, c0:c1], func=AF.Square),
                  reads=[b_xT], writes=[b_sq_])
            if yielding:
                yield 4.0
            pieces = [(c0, c1)] if ncols <= 512 else [(c0, c0 + ncols // 2), (c0 + ncols // 2, c1)]
            for (a0, a1) in pieces:
                pb, b_pb = bank()
                mm_group(pb[:, 0:a1 - a0], [(ones_bf, sq_[:, k, a0:a1]) for k in range(KC)], reads=[b_sq_, b_cb], writes=[b_pb])
                tr.op("act", lambda e, pb=pb, a0=a0, a1=a1: e.activation(
                    out=rstd_[:, a0:a1], in_=pb[:, 0:a1 - a0], func=AF.Ln, bias=epsb[:, 0:1], scale=1.0),
                    reads=[b_pb, b_epsb], writes=[b_rstd_])
                tr.op("act", lambda e, a0=a0, a1=a1: e.activation(
                    out=rstd_[:, a0:a1], in_=rstd_[:, a0:a1], func=AF.Exp, scale=-0.5),
                    reads=[b_rstd_], writes=[b_rstd_])
            if yielding:
                yield 3.0
            if plain_gain is not None:
                for k in range(KC):
                    tr.op("dve", lambda e, k=k: e.scalar_tensor_tensor(
                        out=tbuf[:, k, c0:c1], in0=xT[:, k, c0:c1], scalar=plain_gain[:, k:k + 1], in1=rstd_[:, c0:c1],
                        op0=ALU.mult, op1=ALU.mult), reads=[b_xT, b_rstd_, b_fgm], writes=[b_tbuf])
                if yielding:
                    yield 5.0
                return
            kind_sh = 0 if wch == 0 else 3
            for k in range(KC):
                nt_, b_nt = nts[k % 2]
                tr.op("dve", lambda e, k=k, nt_=nt_: e.scalar_tensor_tensor(
                    out=nt_[:, c0:c1], in0=xT[:, k, c0:c1], scalar=GM(l, wch, k, si), in1=rstd_[:, c0:c1],
                    op0=ALU.mult, op1=ALU.mult), reads=[b_xT, b_rstd_, b_gm], writes=[b_nt])
                tr.op("act", lambda e, k=k, nt_=nt_: e.activation(
                    out=hT[:, k, c0:c1], in_=nt_[:, c0:c1], func=AF.Identity, bias=MOD(l, kind_sh, k, si), scale=1.0),
                    reads=[b_nt, b_modr], writes=[b_hT])
            if yielding:
                yield 5.0

        hsl = slice(HALO, HALO + T)

        def ffn(xT, b_xT, l, si, hT, b_hT, do_norm=True):
            if do_norm:
                yield from rms_to_h(xT, b_xT, l, 1, si, HALO, T, hT=hT, b_hT=b_hT)
            for j in range(NFF):
                slot, b_slot = ring_next(("ffi", l, j))
                w = slot.rearrange("p (k n) -> p k n", n=128)
                pg, b_pg = bank()
                mm_group(pg, [(w[:, k, :], hT[:, k, hsl]) for k in range(KC)], reads=[b_slot, b_hT], writes=[b_pg])
                ring_done()
                tg, b_tg = tmpg[j % 2]
                tr.op("act", lambda e, pg=pg, tg=tg: e.activation(out=tg[:], in_=pg, func=AF.Silu), reads=[b_pg], writes=[b_tg])
                slot, b_slot = ring_next(("ffi", l, NFF + j))
                w = slot.rearrange("p (k n) -> p k n", n=128)
                pu, b_pu = bank()
                mm_group(pu, [(w[:, k, :], hT[:, k, hsl]) for k in range(KC)], reads=[b_slot, b_hT], writes=[b_pu])
                ring_done()
                tr.op("dve", lambda e, pu=pu, tg=tg, j=j: e.tensor_tensor(out=hid[:, j, :], in0=pu, in1=tg[:], op=ALU.mult),
                      reads=[b_pu, b_tg], writes=[b_hid])
                yield 2.4
            for n in range(8):
                po, b_po = bank()
                for (k0, k1) in ((0, 8), (8, 16), (16, 22)):
                    slot, b_slot = ring_next(("ffo", l, n, k0, k1))
                    w = slot[:, 0:(k1 - k0) * 128].rearrange("p (k n) -> p k n", n=128)

                    def fn(e, w=w, po=po, k0=k0, k1=k1):
                        last = None
                        for k in range(k0, k1):
                            last = e.matmul(po, w[:, k - k0, :], hid[:, k, :], start=(k == 0), stop=(k == NFF - 1))
                        return last
                    tr.op("pe", fn, reads=[b_slot, b_hid], writes=[b_po])
                    ring_done()
                tr.op("dve", lambda e, po=po, n=n: e.scalar_tensor_tensor(
                    out=xT[:, n, hsl], in0=po, scalar=MOD(l, 5, n, si), in1=xT[:, n, hsl], op0=ALU.mult, op1=ALU.add),
                    reads=[b_po, b_xT, b_modr], writes=[b_xT])
                yield 3.4

        xl_ctr = [0]

        def load_x_tile(src, t0):
            nblk = [(t0 + r * 128, 128) for r in range(4)] + [(t0 + 512, 32)]
            staged = []
            for (r0, nr) in nblk:
                i = xl_ctr[0] % 3
                xl_ctr[0] += 1
                xt, b_xt = xtok[i]
                tr.dma("sp", "xl%d" % i, xt[0:nr, :], src[r0:r0 + nr, :], writes=[b_xt])
                staged.append((xt, b_xt, nr))
            return staged

        x_prefetched = {}

        def transpose_x_tile(xT, b_xT, src, t0, nxt=None):
            nblk = [(t0 + r * 128, 128) for r in range(4)] + [(t0 + 512, 32)]
            cnt = 0
            stg_ = {}

            def load(r, src_=src, nblk_=nblk, dst_=stg_):
                r0, nr = nblk_[r]
                i = xl_ctr[0] % 3
                xl_ctr[0] += 1
                xt, b_xt = xtok[i]
                tr.dma("sp", "xl%d" % i, xt[0:nr, :], src_[r0:r0 + nr, :], writes=[b_xt])
                dst_[r] = (xt, b_xt)
            key = (id(src), t0)
            if key in x_prefetched:
                stg_.update(x_prefetched.pop(key))
            else:
                for r in range(3):
                    load(r)
            yield 2.0
            for r in range(5):
                xt, b_xt = stg_[r]
                if r < 4:
                    for h in range(2):
                        pb, b_pb = bank()

                        def fn(e, h=h, pb=pb, xt=xt):
                            last = None
                            for kk in range(4):
                                k = 4 * h + kk
                                last = e.transpose(pb[:, kk * 128:(kk + 1) * 128], xt[:, k * 128:(k + 1) * 128], ident)
                            return last
                        tr.op("pe", fn, reads=[b_xt, b_cf], writes=[b_pb])
                        dst = xT[:, 4 * h:4 * h + 4, r * 128:(r + 1) * 128]
                        srcv = pb.rearrange("p (k c) -> p k c", c=128)
                        if cnt % 2:
                            tr.op("act", lambda e, dst=dst, srcv=srcv: e.activation(out=dst, in_=srcv, func=AF.Copy), reads=[b_pb], writes=[b_xT])
                        else:
                            tr.op("dve", lambda e, dst=dst, srcv=srcv: e.tensor_copy(out=dst, in_=srcv), reads=[b_pb], writes=[b_xT])
                        cnt += 1
                else:
                    pb, b_pb = bank()

                    def fn2(e, pb=pb, xt=xt):
                        last = None
                        for k in range(KC):
                            last = e.transpose(pb[:, k * 32:(k + 1) * 32], xt[0:32, k * 128:(k + 1) * 128], ident[0:32, 0:32])
                        return last
                    tr.op("pe", fn2, reads=[b_xt, b_cf], writes=[b_pb])
                    tr.op("dve", lambda e, pb=pb: e.tensor_copy(out=xT[:, :, 512:544], in_=pb[:, 0:256].rearrange("p (k c) -> p k c", c=32)),
                          reads=[b_pb], writes=[b_xT])
                if r + 3 < 5:
                    load(r + 3)
                yield 1.8
            if nxt is not None:
                src2, t02 = nxt
                nblk2 = [(t02 + r * 128, 128) for r in range(3)]
                pf = {}
                for r in range(3):
                    load(r, src2, nblk2, pf)
                x_prefetched[(id(src2), t02)] = pf

        msk = cf[:, CF_MSK:CF_MSK + 2]

        def make_tile(seq, j, xT, b_xT, nxt_tile=None):
          nxt = None if nxt_tile is None else ((xs if nxt_tile[0] == "S" else xp), nxt_tile[1] * T)
          si = 0 if seq == "S" else 1
          gi = j if seq == "S" else NT_S + j
          ntl = NT_S if seq == "S" else NT_P
          src = xs if seq == "S" else xp

          def stageA():
            hT, b_hT = hTa, b_hTa
            yield from transpose_x_tile(xT, b_xT, src, j * T, nxt)
            yield from rms_to_h(xT, b_xT, 0, 0, si, 0, TW, hT=hT, b_hT=b_hT)
            for g in range(4):
                def conv_chunk(cidx):
                    slot, b_slot = ring_next(("w_in0", cidx))
                    w = slot.rearrange("p (k n) -> p k n", n=128)
                    pr, b_pr = bankpair()

                    def fn(e, w=w, pr=pr):
                        last = None
                        for h in range(2):
                            for k in range(KC):
                                last = e.matmul(pr[:, h, 0:HW_], w[:, k, :], hT[:, k, h * HW_:(h + 1) * HW_],
                                                start=(k == 0), stop=(k == KC - 1))
                        return last
                    tr.op("pe", fn, reads=[b_slot, b_hT], writes=b_pr)
                    ring_done()
                    return pr[:, :, 0:HW_], b_pr
                tc_, b_tc = tmpc[g % 2]
                ts_, b_ts = tmps[g % 2]
                pv, b_pv = conv_chunk(4 + g)
                tr.op("act", lambda e, pv=pv, tc_=tc_: e.activation(out=tc_, in_=pv, func=AF.Copy), reads=b_pv, writes=[b_tc])
                pv, b_pv = conv_chunk(8 + g)
                tr.op("dve", lambda e, pv=pv, tc_=tc_, g=g: e.tensor_tensor(
                    out=ca[:, g, :].rearrange("p (h c) -> p h c", c=HW_), in0=pv, in1=tc_, op=ALU.mult),
                    reads=b_pv + [b_tc], writes=[b_ca])
                pv, b_pv = conv_chunk(16 + g)
                tr.op("act", lambda e, pv=pv, ts_=ts_: e.activation(out=ts_, in_=pv, func=AF.Tanh, scale=0.5), reads=b_pv, writes=[b_ts])
                pv, b_pv = conv_chunk(12 + g)
                tr.op("dve", lambda e, pv=pv, ts_=ts_, g=g: e.scalar_tensor_tensor(
                    out=gB[:, g, :].rearrange("p (h c) -> p h c", c=HW_), in0=ts_, scalar=1.0, in1=pv, op0=ALU.add, op1=ALU.mult),
                    reads=b_pv + [b_ts], writes=[b_gB])
                slot, b_slot = ring_next(("w_in0", g))
                w = slot.rearrange("p (k n) -> p k n", n=128)
                pb, b_pb = bank()
                mm_group(pb, [(w[:, k, :], hT[:, k, hsl]) for k in range(KC)], reads=[b_slot, b_hT], writes=[b_pb])
                ring_done()
                tr.op("act", lambda e, pb=pb, g=g: e.activation(out=ab_t[:, g, :], in_=pb, func=AF.Copy), reads=[b_pb], writes=[b_ab])
                yield 11.0
            for (cond_first, c0) in ((True, 0), (False, HALO + T)):
                is_edge = (j == 0) if cond_first else (j == ntl - 1)
                if not is_edge:
                    continue
                for (buf, b_b) in ((ca, b_ca), (gB, b_gB)):
                    if seq == "P":
                        tr.op("dve", lambda e, buf=buf, c0=c0: e.memset(buf[:, :, c0:c0 + HALO], 0.0), writes=[b_b])
                    else:
                        mcol = 0 if cond_first else 1
                        tr.op("dve", lambda e, buf=buf, c0=c0, mcol=mcol: e.tensor_scalar(
                            out=buf[:, :, c0:c0 + HALO], in0=buf[:, :, c0:c0 + HALO], scalar1=msk[:, mcol:mcol + 1], scalar2=None,
                            op0=ALU.mult), reads=[b_b, b_cf], writes=[b_b])
            yield 0.5

          def stageB():
            acc = cz[:, 0:4, :]
            z = cz[:, 4:8, :]
            for g in range(4):
                tr.op("dve", lambda e, g=g: e.tensor_scalar(
                    out=acc[:, g, :], in0=ca[:, g, HALO - 1:HALO - 1 + T], scalar1=wA[:, 0, g:g + 1], scalar2=None, op0=ALU.mult),
                    reads=[b_ca, b_wA], writes=[b_cz])
                for kk in (1, 2):
                    tr.op("dve", lambda e, g=g, kk=kk: e.scalar_tensor_tensor(
                        out=acc[:, g, :], in0=ca[:, g, HALO - 1 + kk:HALO - 1 + kk + T], scalar=wA[:, kk, g:g + 1], in1=acc[:, g, :],
                        op0=ALU.mult, op1=ALU.add), reads=[b_ca, b_wA, b_cz], writes=[b_cz])
                tr.op("dve", lambda e, g=g: e.tensor_tensor(out=yT[:, g, :], in0=acc[:, g, :], in1=ab_t[:, g, :], op=ALU.mult),
                      reads=[b_cz, b_ab], writes=[b_yT])
                yield 2.9
            for g in range(4):
                tr.op("dve", lambda e, g=g: e.tensor_scalar(
                    out=z[:, g, :], in0=gB[:, g, 1:1 + T], scalar1=wB[:, 0, g:g + 1], scalar2=bB[:, g:g + 1], op0=ALU.mult, op1=ALU.add),
                    reads=[b_gB, b_wB, b_bB], writes=[b_cz])
            for kk in range(1, 31):
                for g in range(4):
                    tr.op("dve", lambda e, g=g, kk=kk: e.scalar_tensor_tensor(
                        out=z[:, g, :], in0=gB[:, g, 1 + kk:1 + kk + T], scalar=wB[:, kk, g:g + 1], in1=z[:, g, :],
                        op0=ALU.mult, op1=ALU.add), reads=[b_gB, b_wB, b_cz], writes=[b_cz])
                yield 2.9
            z16 = zsqf[:, 0:1024].bitcast(BF16).rearrange("p (g t) -> p g t", t=T)
            zq16 = zsqf[:, 1024:2048].bitcast(BF16).rearrange("p (g t) -> p g t", t=T)
            tr.op("act", lambda e: e.activation(out=zq16, in_=z, func=AF.Square), reads=[b_cz], writes=[b_zsq])
            tr.op("act", lambda e: e.activation(out=z16, in_=z, func=AF.Copy), reads=[b_cz], writes=[b_zsq])
            p1, b_p1 = bank()
            mm_group(p1, [(ones_bf, z16[:, g, :]) for g in range(4)], reads=[b_zsq, b_cb], writes=[b_p1])
            p2, b_p2 = bank()
            mm_group(p2, [(ones_bf, zq16[:, g, :]) for g in range(4)], reads=[b_zsq, b_cb], writes=[b_p2])
            yield 1.0
            mean, b_mean = lnt[0]
            m2, b_m2 = lnt[1]
            lrs, b_lrs = lnt[2]
            tr.op("act", lambda e: e.activation(out=mean, in_=p1, func=AF.Copy, scale=2.0), reads=[b_p1], writes=[b_mean])
            tr.op("dve", lambda e: e.tensor_tensor(out=m2, in0=mean, in1=mean, op=ALU.mult), reads=[b_mean], writes=[b_m2])
            tr.op("dve", lambda e: e.scalar_tensor_tensor(out=m2, in0=p2, scalar=2.0, in1=m2, op0=ALU.mult, op1=ALU.subtract),
                  reads=[b_p2, b_m2], writes=[b_m2])
            tr.op("act", lambda e: e.activation(out=lrs, in_=m2, func=AF.Ln, bias=epsb[:, 0:1], scale=1.0),
                  reads=[b_m2, b_epsb], writes=[b_lrs])
            tr.op("act", lambda e: e.activation(out=lrs, in_=lrs, func=AF.Exp, scale=-0.5), reads=[b_lrs], writes=[b_lrs])
            yield 3.0
            for g in range(4):
                tr.op("dve", lambda e, g=g: e.tensor_tensor(out=z[:, g, :], in0=z[:, g, :], in1=mean, op=ALU.subtract),
                      reads=[b_cz, b_mean], writes=[b_cz])
                tr.op("dve", lambda e, g=g: e.tensor_tensor(out=z[:, g, :], in0=z[:, g, :], in1=lrs, op=ALU.mult),
                      reads=[b_cz, b_lrs], writes=[b_cz])
                tr.op("act", lambda e, g=g: e.activation(out=yT[:, 4 + g, :], in_=z[:, g, :], func=AF.Silu,
                                                         bias=lnbB[:, g:g + 1], scale=lngB[:, g:g + 1]),
                      reads=[b_cz, b_lngB, b_lnbB], writes=[b_yT])
                yield 2.0
            for n in range(8):
                slot, b_slot = ring_next(("w_out0", n))
                w = slot.rearrange("p (k n) -> p k n", n=128)
                po, b_po = bank()
                mm_group(po, [(w[:, k, :], yT[:, k, :]) for k in range(KC)], reads=[b_slot, b_yT], writes=[b_po])
                ring_done()
                tr.op("dve", lambda e, po=po, n=n: e.scalar_tensor_tensor(
                    out=xT[:, n, hsl], in0=po, scalar=MOD(0, 2, n, si), in1=xT[:, n, hsl], op0=ALU.mult, op1=ALU.add),
                    reads=[b_po, b_xT, b_modr], writes=[b_xT])
                yield 1.8

          def stageN2():
            yield from rms_to_h(xT, b_xT, 0, 1, si, HALO, T, hT=hTa, b_hT=b_hTa)

          def stageC():
            yield from ffn(xT, b_xT, 0, si, hTa, b_hTa, do_norm=False)

          def stageN3():
            tr.dma("sp", "st", X2[gi], xT[:, :, hsl], reads=[b_xT], writes=[b_X2[gi]])
            yield from rms_to_h(xT, b_xT, 1, 0, si, HALO, T, hT=hTb, b_hT=b_hTb)

          def stageD():
            hT, b_hT = hTb, b_hTb
            for n in range(8):
                slot, b_slot = ring_next(("cdu", n))
                w = slot.rearrange("p (k n) -> p k n", n=128)
                pb, b_pb = bank()
                mm_group(pb, [(w[:, k, :], hT[:, k, hsl]) for k in range(KC)], reads=[b_slot, b_hT], writes=[b_pb])
                ring_done()
                if n < 6:
                    tr.op("act", lambda e, pb=pb, n=n: e.activation(out=uT[:, n, :], in_=pb, func=AF.Copy), reads=[b_pb], writes=b_uT)
                else:
                    tr.op("dve", lambda e, pb=pb, n=n: e.tensor_copy(out=fT[:, n - 6, :], in_=pb), reads=[b_pb], writes=b_fT)
            for cc in range(2):
                for ri in range(2):
                    pb, b_pb = bank()
                    mm_group(pb, [(cb[:, CB_D0 + ri * 128:CB_D0 + (ri + 1) * 128], fT[:, cc, :])], reads=[b_cb] + b_fT, writes=[b_pb])
                    pl = cc * 2 + ri
                    if ri:
                        tr.op("act", lambda e, pb=pb, pl=pl: e.activation(out=Gst[:, pl, :], in_=pb, func=AF.Copy), reads=[b_pb], writes=b_Gst)
                    else:
                        tr.op("dve", lambda e, pb=pb, pl=pl: e.tensor_copy(out=Gst[:, pl, :], in_=pb), reads=[b_pb], writes=b_Gst)
            for r_ in range(4):
                if seq == "P":
                    gdst, b_g = G_P[4 * j:4 * j + 4, r_].rearrange("a c b -> c a b"), b_GP
                else:
                    gdst, b_g = G_S[j][:, r_].rearrange("a c b -> c a b"), b_GS[j]
                tr.dma("sp", "g", gdst, Gst[:, r_, :].rearrange("c (a b) -> c a b", b=128), reads=b_Gst, writes=[b_g])
            yield 14.0
            for r in range(4):
                pr, b_pr = bankpair()
                prf = pr.rearrange("p b c -> p (b c)")

                def fnv(e, r=r, pr=pr):
                    last = None
                    for k in range(KC):
                        lh = hT[:, k, HALO + r * 128:HALO + (r + 1) * 128]
                        e.matmul(pr[:, 0, :], lh, vres[:, k, 0:512], start=(k == 0), stop=(k == KC - 1))
                        last = e.matmul(pr[:, 1, 0:256], lh, vres[:, k, 512:768], start=(k == 0), stop=(k == KC - 1))
                    return last
                tr.op("pe", fnv, reads=[b_hT, b_vres], writes=b_pr)
                vf, b_vf = vfs[r % 2]
                st4, b_st4 = st4s[r]
                tr.op("act", lambda e, prf=prf: e.activation(out=vf, in_=prf[:, 0:768], func=AF.Copy), reads=b_pr, writes=b_vf)
                tr.op("dve", lambda e: e.tensor_reduce(out=st4[:, 0:1], in_=vf, axis=mybir.AxisListType.X, op=ALU.add),
                      reads=b_vf, writes=[b_st4])
                tr.op("act", lambda e: e.activation(out=vsq, in_=vf, func=AF.Square), reads=b_vf, writes=b_vsq)
                tr.op("dve", lambda e: e.tensor_reduce(out=st4[:, 1:2], in_=vsq, axis=mybir.AxisListType.X, op=ALU.add),
                      reads=b_vsq, writes=[b_st4])
                tr.op("dve", lambda e: e.tensor_scalar(out=st4[:, 2:4], in0=st4[:, 0:2], scalar1=1.0 / 768.0, scalar2=None, op0=ALU.mult),
                      reads=[b_st4], writes=[b_st4])
                tr.op("dve", lambda e: e.tensor_tensor(out=st4[:, 4:5], in0=st4[:, 2:3], in1=st4[:, 2:3], op=ALU.mult),
                      reads=[b_st4], writes=[b_st4])
                tr.op("dve", lambda e: e.tensor_tensor(out=st4[:, 5:6], in0=st4[:, 3:4], in1=st4[:, 4:5], op=ALU.subtract),
                      reads=[b_st4], writes=[b_st4])
                tr.op("act", lambda e: e.activation(out=st4[:, 6:7], in_=st4[:, 5:6], func=AF.Ln, bias=epsb[:, 0:1], scale=1.0),
                      reads=[b_st4, b_epsb], writes=[b_st4])
                tr.op("act", lambda e: e.activation(out=st4[:, 6:7], in_=st4[:, 6:7], func=AF.Exp, scale=-0.5), reads=[b_st4], writes=[b_st4])
                tr.op("dve", lambda e: e.scalar_tensor_tensor(out=st4[:, 7:8], in0=st4[:, 2:3], scalar=-1.0, in1=st4[:, 6:7],
                                                              op0=ALU.mult, op1=ALU.mult), reads=[b_st4], writes=[b_st4])
                vn_, b_vn = vn[r % 2]
                tr.op("act", lambda e, vn_=vn_: e.activation(out=vn_, in_=vf, func=AF.Identity, bias=st4[:, 7:8], scale=st4[:, 6:7]),
                      reads=b_vf + [b_st4], writes=b_vn)
                yield 7.0
                pr2, b_pr2 = bankpair()
                pr2f = pr2.rearrange("p b c -> p (b c)")

                def fng(e, vn_=vn_, pr2f=pr2f):
                    last = None
                    for h in range(6):
                        last = e.matmul(pr2f[:, h * 128:(h + 1) * 128], vn_[:, h * 128:(h + 1) * 128], wsT[:, h, :], start=True, stop=True)
                    return last
                tr.op("pe", fng, reads=b_vn + [b_wsT], writes=b_pr2)
                for h in range(6):
                    tr.op("dve", lambda e, h=h, pr2f=pr2f: e.scalar_tensor_tensor(
                        out=t6[:, h, :], in0=pr2f[:, h * 128:(h + 1) * 128], scalar=lngC[:, h:h + 1], in1=Cst[:, h, :],
                        op0=ALU.mult, op1=ALU.add), reads=b_pr2 + [b_lngC, b_Cst], writes=b_t6)
                tr.op("dve", lambda e, r=r: e.tensor_tensor(out=ycT[:, :, r * 128:(r + 1) * 128], in0=t6,
                                                             in1=uT[:, :, r * 128:(r + 1) * 128], op=ALU.mult),
                      reads=b_t6 + b_uT, writes=b_ycT)
                yield 8.0
            tr.dma("sp", "st", YC[gi], ycT, reads=b_ycT, writes=[b_YC[gi]])
            yield 1.0

          return stageA, stageB, stageC, stageD, stageN2, stageN3


        def p3_load(idx, tiles3):
            if idx >= len(tiles3):
                return None
            seq, j = tiles3[idx]
            gi = j if seq == "S" else NT_S + j
            xT, b_xT = (xTa, b_xTa) if idx % 2 == 0 else (xTb, b_xTb)
            yc_, b_yc = yc3[idx % 2]
            yd_, b_yd = ydt[idx % 2]
            key = "p3a" if idx % 2 == 0 else "p3b"
            tr.dma("sp", key, xT[:, :, hsl], X2[gi], reads=[b_X2[gi]], writes=[b_xT])
            tr.dma("sp", key, yc_[:], YC[gi], reads=[b_YC[gi]], writes=[b_yc])
            tr.dma("sp", key, yd_[:], YD[:, :, gi * T:(gi + 1) * T].rearrange("c p t -> p c t"), reads=[b_YD], writes=[b_yd])
            return (xT, b_xT, yc_, b_yc, yd_, b_yd)

        def make_tile3(idx, tiles3):
            seq, j = tiles3[idx]
            si = 0 if seq == "S" else 1
            st = {}

            def stageP():
                st["ld"] = p3_load(idx, tiles3)
                xT, b_xT, yc_, b_yc, yd_, b_yd = st["ld"]
                yield 3.0
                for n in range(8):
                    slot, b_slot = ring_next(("w_out1", n))
                    w = slot.rearrange("p (k n) -> p k n", n=128)
                    po, b_po = bank()
                    pairs = [(w[:, k, :], yc_[:, k, :]) for k in range(6)] + [(w[:, 6 + c, :], yd_[:, c, :]) for c in range(2)]
                    mm_group(po, pairs, reads=[b_slot, b_yc, b_yd], writes=[b_po])
                    ring_done()
                    tr.op("dve", lambda e, po=po, n=n: e.scalar_tensor_tensor(
                        out=xT[:, n, hsl], in0=po, scalar=MOD(1, 2, n, si), in1=xT[:, n, hsl], op0=ALU.mult, op1=ALU.add),
                        reads=[b_po, b_xT, b_modr], writes=[b_xT])
                    yield 1.8

            hT3, b_hT3 = (hTa, b_hTa) if idx % 2 == 0 else (hTb, b_hTb)

            def stageN():
                xT, b_xT = st["ld"][0], st["ld"][1]
                yield from rms_to_h(xT, b_xT, 1, 1, si, HALO, T, hT=hT3, b_hT=b_hT3)

            def stageQ():
                xT, b_xT = st["ld"][0], st["ld"][1]
                yield from ffn(xT, b_xT, 1, si, hT3, b_hT3, do_norm=False)

            def stageR():
                xT, b_xT = st["ld"][0], st["ld"][1]
                yield from rms_to_h(xT, b_xT, 1, 0, si, HALO, T, plain_gain=fgm, scr=(sq2, b_sq2, rstd2, b_rstd2))
                ot, b_ot = otok[idx % 2]
                cnt = 0
                for r in range(4):
                    for half in range(2):
                        pb, b_pb = bank()

                        def fn(e, r=r, half=half, pb=pb):
                            last = None
                            for i in range(4):
                                last = e.transpose(pb[:, i * 128:(i + 1) * 128],
                                                   tbuf[:, 4 * half + i, HALO + r * 128:HALO + (r + 1) * 128], ident)
                            return last
                        tr.op("pe", fn, reads=[b_tbuf, b_cf], writes=[b_pb])
                        if cnt % 2:
                            tr.op("act", lambda e, pb=pb, r=r, half=half: e.activation(
                                out=ot[:, r, half * 512:(half + 1) * 512], in_=pb, func=AF.Copy), reads=[b_pb], writes=[b_ot])
                        else:
                            tr.op("dve", lambda e, pb=pb, r=r, half=half: e.tensor_copy(
                                out=ot[:, r, half * 512:(half + 1) * 512], in_=pb), reads=[b_pb], writes=[b_ot])
                        cnt += 1
                    yield 2.0
                ydst = ys if seq == "S" else yp
                tr.dma("sp", "out", ydst[j * T:(j + 1) * T, :].rearrange("(r p) d -> p r d", p=128), ot[:], reads=[b_ot])
                yield 0.5

            return stageP, stageQ, stageR, stageN

        return dict(close=ph.close, make_tile=make_tile if P1 else None, bufs=((xTa, b_xTa), (xTb, b_xTb)),
                    make_tile3=make_tile3)

    env = make_env(1)
    tiles1 = [("S", j) for j in range(NT_S)] + [("P", j) for j in range(NT_P)]
    if debug and "nt1" in debug:
        tiles1 = tiles1[:debug["nt1"]]

    def run_interleaved(gens):
        acc = [0.0] * len(gens)
        alive = [True] * len(gens)
        while any(alive):
            i = min((a, ix) for ix, a in enumerate(acc) if alive[ix])[1]
            try:
                acc[i] += next(gens[i])
            except StopIteration:
                alive[i] = False

    def after_tile(seq, j):
        if seq != "S":
            return
        if debug and debug.get("nocc"):
            for r_ in range(4):
                tr.dma("sp", "ph2", AG[j][r_], G_S[j], reads=[b_GS[j]], writes=[b_AG[j]])
        else:
            tr.custom("pool", "cc", 1, lambda e, j=j: e.collective_compute(
                "AllGather", ALU.bypass, replica_groups=[[0, 1, 2, 3], [4, 5, 6, 7]], ins=[G_S[j].rearrange("a r (c1 c2) b -> (a r c1) (c2 b)", c2=4)],
                outs=[AG[j].rearrange("k a r (c1 c2) b -> (k a r c1) (c2 b)", c2=4)]),
                reads=[b_GS[j]], writes=[b_AG[j]])

    def chain(*gs):
        for g_ in gs:
            yield from g_

    stages = []
    for idx, (seq, j) in enumerate(tiles1):
        xT_, b_xT_ = env["bufs"][idx % 2]
        nxt_tile = tiles1[idx + 1] if idx + 1 < len(tiles1) else None
        stages.append(env["make_tile"](seq, j, xT_, b_xT_, nxt_tile) + (seq, j))
    n1 = len(stages)
    if n1:
        run_interleaved([stages[0][0]()])
        run_interleaved([stages[0][1]()])
    for i in range(1, n1 + 1):
        th = []
        if i < n1:
            th.append(stages[i][0]())
        th.append(stages[i - 1][4]())
        gens = [chain(*th)]
        if i - 2 >= 0:
            gens.insert(0, stages[i - 2][3]())
        run_interleaved(gens)
        if i - 2 >= 0:
            after_tile(stages[i - 2][6], stages[i - 2][7])
        gens = [chain(stages[i - 1][2](), stages[i - 1][5]())]
        if i < n1:
            gens.insert(0, stages[i][1]())
        run_interleaved(gens)
    if n1:
        run_interleaved([stages[n1 - 1][3]()])
        after_tile(stages[n1 - 1][6], stages[n1 - 1][7])
    barrier()
    env["close"]()
    vres_scope.close()

    with ExitStack() as s2:
        def sb2(name, shape, dt=F32):
            return s2.enter_context(nc.sbuf_tensor(name, list(shape), dt)), Buf(name)
        GAs = [sb2("GA%d" % i, [128, 2, 64, 128], BF16) for i in range(2)]
        HB, b_HB = sb2("HB", [128, 2, 128, 128], BF16)
        yst, b_yst = sb2("yst", [128, SP_LEN], BF16)
        tt = [sb2("tt%d" % i, [128, 256]) for i in range(4)]

        def params(seq):
            if seq == "P":
                return dict(Na=64, Nkb=128, ntok=SP_LEN, off=SS_LEN,
                            W11=cb[0:64, CB_W1P1:CB_W1P1 + 128], W12=cb[0:64, CB_W1P2:CB_W1P2 + 128],
                            Tr=cf[:, CF_TP:CF_TP + 64], Ti=cf[:, CF_TP + 64:CF_TP + 128],
                            W2r=cb[:, CB_W2P:CB_W2P + 128], W2i=cb[:, CB_W2P + 128:CB_W2P + 256])
            return dict(Na=128, Nkb=32, ntok=SS_LEN, off=0,
                        W11=cb[:, CB_W1S1:CB_W1S1 + 256], W12=cb[:, CB_W1S2:CB_W1S2 + 256],
                        Tr=cf[:, CF_TS:CF_TS + 128], Ti=cf[:, CF_TS + 128:CF_TS + 256],
                        W2r=cb[:, CB_W2S:CB_W2S + 32], W2i=cb[:, CB_W2S + 32:CB_W2S + 64])

        def fft_load(seq, cc, chh, GA, b_GA):
            cs = slice(chh * 64, (chh + 1) * 64)
            if seq == "P":
                for ri in range(2):
                    tr.dma("sp", "ph2", GA[0:64, ri, :, :], G_P[:, 2 * cc + ri, cs], reads=[b_GP], writes=[b_GA])
            else:
                for rk in range(4):
                    for ri in range(2):
                        tr.dma("sp", "ph2", GA[32 * rk:32 * rk + 32, ri, :, :], AG_all[:, rk, :, 2 * cc + ri, cs],
                               reads=b_AG, writes=[b_GA])

        def fft_stage1(seq, cc, chh, GA, b_GA):
            pr = params(seq)
            Na, W11, W12, Tr, Ti = pr["Na"], pr["W11"], pr["W12"], pr["Tr"], pr["Ti"]
            nch = 512 // (2 * Na)
            for c0 in range(0, 64, nch):
                pb, b_pb = bank()

                def fn1(e, c0=c0, pb=pb):
                    last = None
                    for i in range(nch):
                        o = pb[:, i * 2 * Na:(i + 1) * 2 * Na]
                        e.matmul(o, GA[0:Na, 0, c0 + i, :], W11, start=True, stop=False)
                        last = e.matmul(o, GA[0:Na, 1, c0 + i, :], W12, start=False, stop=True)
                    return last
                tr.op("pe", fn1, reads=[b_GA, b_cb], writes=[b_pb])
                pv = pb.rearrange("p (c r k) -> p c r k", r=2, k=Na)
                Hr, Hi = pv[:, :, 0, :], pv[:, :, 1, :]
                Trb = Tr.unsqueeze(1).broadcast_to([128, nch, Na])
                Tib = Ti.unsqueeze(1).broadcast_to([128, nch, Na])
                tv = [(t_[0][:, 0:nch * Na].rearrange("p (c k) -> p c k", k=Na), t_[1]) for t_ in tt]
                for (ti_, a_, b_) in ((0, Hr, Trb), (1, Hi, Tib), (2, Hr, Tib), (3, Hi, Trb)):
                    tr.op("dve", lambda e, ti_=ti_, a_=a_, b_=b_, tv=tv: e.tensor_tensor(out=tv[ti_][0], in0=a_, in1=b_, op=ALU.mult),
                          reads=[b_pb, b_cf], writes=[tv[ti_][1]])
                ch0 = chh * 64 + c0
                tr.op("dve", lambda e, ch0=ch0, tv=tv: e.tensor_tensor(
                    out=HB[:, 0, ch0:ch0 + nch, 0:Na], in0=tv[0][0], in1=tv[1][0], op=ALU.subtract),
                    reads=[tv[0][1], tv[1][1]], writes=[b_HB])
                tr.op("dve", lambda e, ch0=ch0, tv=tv: e.tensor_tensor(
                    out=HB[:, 1, ch0:ch0 + nch, 0:Na], in0=tv[2][0], in1=tv[3][0], op=ALU.add),
                    reads=[tv[2][1], tv[3][1]], writes=[b_HB])

        def fft_stage2(seq, cc):
            pr = params(seq)
            Na, Nkb, ntok, off, W2r, W2i = pr["Na"], pr["Nkb"], pr["ntok"], pr["off"], pr["W2r"], pr["W2i"]
            nka = 512 // Nkb
            ystv = yst[:, 0:ntok].rearrange("p (kb ka) -> p kb ka", ka=Na)
            for q_, ka0 in enumerate(range(0, Na, nka)):
                pb, b_pb = bank()

                def fn2(e, ka0=ka0, pb=pb):
                    last = None
                    for i in range(nka):
                        o = pb[:, i * Nkb:(i + 1) * Nkb]
                        e.matmul(o, HB[:, 0, :, ka0 + i], W2r, start=True, stop=False)
                        last = e.matmul(o, HB[:, 1, :, ka0 + i], W2i, start=False, stop=True)
                    return last
                tr.op("pe", fn2, reads=[b_HB, b_cb], writes=[b_pb])
                src = pb.rearrange("p (i k) -> p k i", k=Nkb)
                dst = ystv[:, :, ka0:ka0 + nka]
                if q_ % 2 == 0:
                    tr.op("act", lambda e, src=src, dst=dst: e.activation(out=dst, in_=src, func=AF.Copy), reads=[b_pb], writes=[b_yst])
                else:
                    tr.op("dve", lambda e, src=src, dst=dst: e.tensor_copy(out=dst, in_=src), reads=[b_pb], writes=[b_yst])
            tr.dma("sp", "ph2", YD[cc, :, off:off + ntok], yst[:, 0:ntok], reads=[b_yst], writes=[b_YD])

        units = [(seq, cc, chh) for seq in ("S", "P") for cc in range(2) for chh in range(2)]
        if debug and debug.get("nofft"):
            units = []
        for ui in range(min(2, len(units))):
            fft_load(*units[ui], *GAs[ui % 2])
        for ui, (seq, cc, chh) in enumerate(units):
            fft_stage1(seq, cc, chh, *GAs[ui % 2])
            if ui + 2 < len(units):
                fft_load(*units[ui + 2], *GAs[ui % 2])
            if chh == 1:
                fft_stage2(seq, cc)
        barrier()

    env = make_env(3)
    tiles3 = [("S", j) for j in range(NT_S)] + [("P", j) for j in range(NT_P)]
    if debug and "nt3" in debug:
        tiles3 = tiles3[:debug["nt3"]]
    st3 = [env["make_tile3"](idx, tiles3) for idx in range(len(tiles3))]
    n3 = len(st3)
    if n3:
        run_interleaved([chain(st3[0][0](), st3[0][3]())])
    for i in range(n3):
        side = []
        if i >= 1:
            side.append(st3[i - 1][2]())
        if i + 1 < n3:
            side.append(st3[i + 1][0]())
            side.append(st3[i + 1][3]())
        run_interleaved([st3[i][1](), chain(*side)])
    if n3:
        run_interleaved([st3[n3 - 1][2]()])
    barrier()
    tr.final_wait("sp", DMAKEYS)

    with nc.Block() as block:
        @block.tensor
        def _(e):
            for f in tr.q["pe"]:
                f(e)

        @block.scalar
        def _(e):
            for f in tr.q["act"]:
                f(e)

        @block.vector
        def _(e):
            for f in tr.q["dve"]:
                f(e)

        @block.gpsimd
        def _(e):
            for f in tr.q["pool"]:
                f(e)

        @block.sync
        def _(e):
            for f in tr.q["sp"]:
                f(e)
    return nc, tr, recorded


def _consts(q):
    cbm = np.zeros((128, CB_N), np.float64)
    cfm = np.zeros((128, CF_N), np.float64)
    cbm[:, CB_ONES:CB_ONES + 128] = 1.0 / 1024.0
    j = np.arange(64)
    ang = 2 * np.pi * np.outer(j, j) / 64.0
    for g in range(2):
        cbm[g * 64:(g + 1) * 64, CB_D0 + g * 64:CB_D0 + (g + 1) * 64] = np.cos(ang)
        cbm[g * 64:(g + 1) * 64, CB_D0 + 128 + g * 64:CB_D0 + 128 + (g + 1) * 64] = -np.sin(ang)
    a = np.arange(64)
    ang = 2 * np.pi * np.outer(a, a) / 64.0
    cbm[0:64, CB_W1P1:CB_W1P1 + 64] = np.cos(ang)
    cbm[0:64, CB_W1P1 + 64:CB_W1P1 + 128] = -np.sin(ang)
    cbm[0:64, CB_W1P2:CB_W1P2 + 64] = np.sin(ang)
    cbm[0:64, CB_W1P2 + 64:CB_W1P2 + 128] = np.cos(ang)
    a = np.arange(128)
    ang = 2 * np.pi * np.outer(a, a) / 128.0
    cbm[:, CB_W1S1:CB_W1S1 + 128] = np.cos(ang)
    cbm[:, CB_W1S1 + 128:CB_W1S1 + 256] = -np.sin(ang)
    cbm[:, CB_W1S2:CB_W1S2 + 128] = np.sin(ang)
    cbm[:, CB_W1S2 + 128:CB_W1S2 + 256] = np.cos(ang)
    scP = 1.0 / np.sqrt(64.0 * 8192.0)
    scS = 1.0 / np.sqrt(64.0 * 16384.0)
    b = np.arange(128)
    ang = 2 * np.pi * np.outer(b, np.arange(128)) / 128.0
    cbm[:, CB_W2P:CB_W2P + 128] = np.cos(ang) * scP
    cbm[:, CB_W2P + 128:CB_W2P + 256] = np.sin(ang) * scP
    kb = 32 * q + np.arange(32)
    ang = 2 * np.pi * np.outer(b, kb) / 128.0
    cbm[:, CB_W2S:CB_W2S + 32] = np.cos(ang) * scS
    cbm[:, CB_W2S + 32:CB_W2S + 64] = np.sin(ang) * scS
    cfm[:, CF_ID:CF_ID + 128] = np.eye(128)
    cfm[:, CF_ONES512:CF_ONES512 + 128] = 1.0 / 512.0
    ang = 2 * np.pi * np.outer(b, np.arange(64)) / 8192.0
    cfm[:, CF_TP:CF_TP + 64] = np.cos(ang)
    cfm[:, CF_TP + 64:CF_TP + 128] = -np.sin(ang)
    ang = 2 * np.pi * np.outer(b, np.arange(128)) / 16384.0
    cfm[:, CF_TS:CF_TS + 128] = np.cos(ang)
    cfm[:, CF_TS + 128:CF_TS + 256] = -np.sin(ang)
    cfm[:, CF_MSK] = 0.0 if q == 0 else 1.0
    cfm[:, CF_MSK + 1] = 0.0 if q == 3 else 1.0
    return cbm.astype(ml_dtypes.bfloat16), cfm.astype(np.float32)


_CACHE = {}


def make_in_maps(inputs):
    f = lambda k: np.ascontiguousarray(np.asarray(inputs[k], dtype=np.float32))
    x_prompt, x_sample = f("x_prompt"), f("x_sample")
    c_prompt, c_sample = f("c_prompt"), f("c_sample")
    shared = {
        "ada_w": f("ada_w"), "ada_b": f("ada_b"), "mix_norm_g": f("mix_norm_g"), "ffn_norm_g": f("ffn_norm_g"),
        "ab_w_in": f("ab_w_in")[0], "ab_conv_a": f("ab_conv_a")[0], "ab_conv_b_w": f("ab_conv_b_w")[0],
        "ab_conv_b_b": f("ab_conv_b_b")[0], "ab_ln_g": f("ab_ln_g")[0], "ab_ln_b": f("ab_ln_b")[0],
        "ab_w_out": f("ab_w_out")[0], "cd_w_in": f("cd_w_in")[0], "cd_ln_g": f("cd_ln_g")[0], "cd_ln_b": f("cd_ln_b")[0],
        "cd_w_s": f("cd_w_s")[0], "cd_b_s": f("cd_b_s")[0], "cd_w_out": f("cd_w_out")[0],
        "ffn_w_in": f("ffn_w_in"), "ffn_w_out": f("ffn_w_out"), "final_g": f("final_g"),
    }
    in_maps = []
    for c in range(8):
        bq, q = c // 4, c % 4
        xpp = np.zeros((SP_LEN + 2 * HALO, D), np.float32)
        xpp[HALO:HALO + SP_LEN] = x_prompt[c]
        xsp = np.zeros((SS_LEN + 2 * HALO, D), np.float32)
        lo, hi = q * SS_LEN - HALO, (q + 1) * SS_LEN + HALO
        slo, shi = max(lo, 0), min(hi, 4 * SS_LEN)
        xsp[slo - lo:slo - lo + (shi - slo)] = x_sample[bq, slo:shi]
        cbm, cfm = _consts(q)
        m = dict(shared)
        m.update({"xp": xpp, "xs": xsp, "cvec": np.stack([c_sample[bq], c_prompt[c]], 0), "cb": cbm, "cf": cfm})
        in_maps.append(m)
    return in_maps


def kernel(**inputs):
    if "nc" not in _CACHE:
        specs = build_program()[2]
        _CACHE["nc"] = build_program(plan_specs=specs)[0]
    nc = _CACHE["nc"]
    in_maps = make_in_maps(inputs)
    res = run_bass_kernel_spmd(nc, in_maps, core_ids=list(range(8)))
    y_prompt = np.stack([np.asarray(res.results[c]["yp"], dtype=np.float32) for c in range(8)], 0)
    y_sample = np.stack([
        np.concatenate([np.asarray(res.results[b * 4 + q]["ys"], dtype=np.float32) for q in range(4)], 0)
        for b in range(2)], 0)
    return (y_prompt, y_sample)
```

```python
import numpy as np
import ml_dtypes
from contextlib import ExitStack
import concourse.bass as bass
import concourse.mybir as mybir
from concourse.bass_utils import run_bass_kernel_spmd

F32 = mybir.dt.float32
BF16 = mybir.dt.bfloat16
AF = mybir.ActivationFunctionType
ALU = mybir.AluOpType

D = 1024
KC = 8
T = 512
HALO = 16
TW = T + 2 * HALO
HW_ = TW // 2
SP_LEN = 8192
SS_LEN = 4096
NT_S = SS_LEN // T
NT_P = SP_LEN // T
NTILES = NT_S + NT_P
NTOK = SS_LEN + SP_LEN
DFF = 2816
NFF = DFF // 128
EPS = 1e-6
NSLOT = 8

CB_ONES = 0
CB_D0 = 128
CB_W1P1 = 384
CB_W1P2 = 512
CB_W1S1 = 640
CB_W1S2 = 896
CB_W2P = 1152
CB_W2S = 1408
CB_N = 1472
CF_ID = 0
CF_ONES512 = 128
CF_TP = 256
CF_TS = 384
CF_MSK = 640
CF_N = 642


class Buf:
    __slots__ = ("name", "w", "r")

    def __init__(self, name):
        self.name = name
        self.w = None
        self.r = {}


class Tracker:
    ENGS = ("pe", "act", "dve", "pool", "sp")

    def __init__(self, nc, stack):
        self.nc = nc
        self.stack = stack
        self.q = {e: [] for e in self.ENGS}
        self.cnt = {}
        self.semh = {}
        self.isdma = {}
        self.seen = {e: {} for e in self.ENGS}
        for e in ("pe", "act", "dve", "pool"):
            self.newsem(e, False)
        self.nops = 0

    def newsem(self, key, isdma=True):
        self.semh[key] = self.stack.enter_context(self.nc.semaphore("s_" + key))
        self.cnt[key] = 0
        self.isdma[key] = isdma

    def wait(self, eng, dep):
        if dep is None:
            return
        k, v = dep
        if self.isdma[k]:
            v = self.cnt[k]
        elif k == "pe" and eng == "pe":
            return
        if self.seen[eng].get(k, 0) >= v:
            return
        self.seen[eng][k] = v
        sem = self.semh[k]
        self.q[eng].append(lambda e, sem=sem, v=v: e.wait_ge(sem, v))

    def _deps(self, eng, reads, writes):
        for b in reads:
            self.wait(eng, b.w)
        for b in writes:
            self.wait(eng, b.w)
            for k, v in list(b.r.items()):
                self.wait(eng, (k, v))

    def _post(self, dep, reads, writes):
        k, v = dep
        for b in reads:
            if b.r.get(k, 0) < v:
                b.r[k] = v
        for b in writes:
            b.w = dep
            b.r = {}

    def op(self, eng, fn, reads=(), writes=()):
        self._deps(eng, reads, writes)
        self.cnt[eng] += 1
        v = self.cnt[eng]
        sem = self.semh[eng]
        self.q[eng].append(lambda e, fn=fn, sem=sem: fn(e).then_inc(sem, 1))
        self._post((eng, v), reads, writes)
        self.nops += 1

    def dma(self, eng, key, out, in_, reads=(), writes=(), **kw):
        self._deps(eng, reads, writes)
        self.cnt[key] += 16
        v = self.cnt[key]
        sem = self.semh[key]
        self.q[eng].append(lambda e, out=out, in_=in_, sem=sem, kw=kw: e.dma_start(out=out, in_=in_, **kw).then_inc(sem, 16))
        self._post((key, v), reads, writes)

    def custom(self, eng, key, inc, fn, reads=(), writes=()):
        self._deps(eng, reads, writes)
        self.cnt[key] += inc
        v = self.cnt[key]
        sem = self.semh[key]
        self.q[eng].append(lambda e, fn=fn, sem=sem, inc=inc: fn(e).then_inc(sem, inc))
        self._post((key, v), reads, writes)

    def final_wait(self, eng, keys):
        for k in keys:
            if self.cnt[k] > 0:
                self.wait(eng, (k, self.cnt[k]))


def build_program(debug=None, plan_specs=None):
    nc = bass.Bass("TRN2", target_bir_lowering=False)
    stack = ExitStack()
    tr = Tracker(nc, stack)

    def din(name, shape, dt=F32):
        return nc.dram_tensor(name, list(shape), dt, kind="ExternalInput").ap()

    def dscr(name, shape, dt):
        return nc.dram_tensor(name, list(shape), dt).ap()

    xp = din("xp", [SP_LEN + 2 * HALO, D])
    xs = din("xs", [SS_LEN + 2 * HALO, D])
    cvec = din("cvec", [2, D])
    ada_w = din("ada_w", [2, D, 6 * D])
    ada_b = din("ada_b", [2, 6 * D])
    mix_norm_g = din("mix_norm_g", [2, D])
    ffn_norm_g = din("ffn_norm_g", [2, D])
    ab_w_in = din("ab_w_in", [D, 2560])
    ab_conv_a = din("ab_conv_a", [3, 512])
    ab_conv_b_w = din("ab_conv_b_w", [31, 512])
    ab_conv_b_b = din("ab_conv_b_b", [512])
    ab_ln_g = din("ab_ln_g", [512])
    ab_ln_b = din("ab_ln_b", [512])
    ab_w_out = din("ab_w_out", [D, D])
    cd_w_in = din("cd_w_in", [D, 1792])
    cd_ln_g = din("cd_ln_g", [768])
    cd_ln_b = din("cd_ln_b", [768])
    cd_w_s = din("cd_w_s", [6, 128, 128])
    cd_b_s = din("cd_b_s", [6, 128])
    cd_w_out = din("cd_w_out", [D, D])
    ffn_w_in = din("ffn_w_in", [2, D, 2 * DFF])
    ffn_w_out = din("ffn_w_out", [2, DFF, D])
    final_g = din("final_g", [D])
    cb_d = din("cb", [128, CB_N], BF16)
    cf_d = din("cf", [128, CF_N])
    yp = nc.dram_tensor("yp", [SP_LEN, D], F32, kind="ExternalOutput").ap()
    ys = nc.dram_tensor("ys", [SS_LEN, D], F32, kind="ExternalOutput").ap()

    w_in0_s = dscr("w_in0_s", [20, 128, KC, 128], BF16)
    w_out0_s = dscr("w_out0_s", [8, 128, KC, 128], BF16)
    ffi_s = [dscr("ffi0_s", [44, 128, KC, 128], BF16), dscr("ffi1_s", [44, 128, KC, 128], BF16)]
    ffo_s = [dscr("ffo0_s", [8, 128, NFF, 128], BF16), dscr("ffo1_s", [8, 128, NFF, 128], BF16)]
    cdu_s = dscr("cdu_s", [8, 128, KC, 128], BF16)
    cdv_s = dscr("cdv_s", [KC, 128, 768], BF16)
    w_out1_s = dscr("w_out1_s", [8, 128, KC, 128], BF16)
    X2 = dscr("X2", [NTILES, 128, KC, T], F32)
    YC = dscr("YC", [NTILES, 128, 6, T], BF16)
    G_P = dscr("G_P", [64, 4, 128, 128], BF16)
    G_S = [dscr("G_S%d" % j, [4, 4, 128, 128], BF16) for j in range(NT_S)]
    AG_all = dscr("AG_all", [NT_S, 4, 4, 4, 128, 128], BF16)
    AG = [AG_all[j] for j in range(NT_S)]
    YD = dscr("YD", [2, 128, NTOK], BF16)
    b_X2 = [Buf("X2_%d" % i) for i in range(NTILES)]
    b_YC = [Buf("YC_%d" % i) for i in range(NTILES)]
    b_GP, b_YD = Buf("GP"), Buf("YD")
    b_GS = [Buf("GS%d" % j) for j in range(NT_S)]
    b_AG = [Buf("AG%d" % j) for j in range(NT_S)]

    def sbg(name, shape, dt=F32):
        return nc.alloc_sbuf_tensor(name, list(shape), dt), Buf(name)

    cb, b_cb = sbg("cb_sb", [128, CB_N], BF16)
    cf, b_cf = sbg("cf_sb", [128, CF_N])
    ring = nc.alloc_sbuf_tensor("ring", [128, NSLOT, 1024], BF16)
    b_ring = [Buf("ring%d" % i) for i in range(NSLOT)]
    for i in range(NSLOT):
        tr.newsem("ring%d" % i)
    modr, b_modr = sbg("modr", [128, 2, 48, 2])
    gm, b_gm = sbg("gm", [128, 2, 2, KC, 2])
    adab, b_adab = sbg("adab", [128, 2, 48])
    ngm, b_ngm = sbg("ngm", [128, 2, 2, KC])
    fgm, b_fgm = sbg("fgm", [128, KC])
    cT, b_cT = sbg("cT", [128, KC, 2])
    scT, b_scT = sbg("scT", [128, KC, 2], BF16)
    wA, b_wA = sbg("wA", [128, 3, 4])
    wB, b_wB = sbg("wB", [128, 31, 4])
    bB, b_bB = sbg("bB", [128, 4])
    lngB, b_lngB = sbg("lngB", [128, 4])
    lnbB, b_lnbB = sbg("lnbB", [128, 4])
    lngC, b_lngC = sbg("lngC", [128, 6])
    wsT, b_wsT = sbg("wsT", [128, 6, 128], BF16)
    Cst, b_Cst = sbg("Cst", [128, 6, 128])
    st4s = [sbg("st4_%d" % i, [128, 8]) for i in range(4)]
    epsb, b_epsb = sbg("epsb", [128, 1])
    vres_scope = ExitStack()
    vres, b_vres = vres_scope.enter_context(nc.sbuf_tensor("vres", [128, KC, 768], BF16)), Buf("vres")

    pp = nc.alloc_psum_tensor("pp", [128, 8, 512], F32)
    b_pp = [Buf("bank%d" % i) for i in range(8)]
    bank_ctr = [0]

    def bank():
        i = bank_ctr[0] % 8
        bank_ctr[0] += 1
        return pp[:, i, :], b_pp[i]

    def bankpair():
        if bank_ctr[0] % 2:
            bank_ctr[0] += 1
        i = bank_ctr[0] % 8
        bank_ctr[0] += 2
        return pp[:, i:i + 2, :], [b_pp[i], b_pp[i + 1]]

    DMAKEYS = ["ld", "st", "xl0", "xl1", "xl2", "xl3", "xl4", "castA", "castB", "castC", "castD", "castE", "castF",
               "cc", "g", "ph2", "out", "p3a", "p3b", "vres"]
    for k in DMAKEYS:
        tr.newsem(k)
    DMAKEYS += ["ring%d" % i for i in range(NSLOT)]

    def barrier():
        for e in Tracker.ENGS:
            for k in ("pe", "act", "dve", "pool"):
                if tr.cnt[k] > 0:
                    tr.wait(e, (k, tr.cnt[k]))
            for k in DMAKEYS:
                if tr.cnt[k] > 0 and not k.startswith("cast") and k != "vres":
                    tr.wait(e, (k, tr.cnt[k]))

    ident = cf[:, CF_ID:CF_ID + 128]
    ones_bf = cb[:, CB_ONES:CB_ONES + 128]
    ones512 = cf[:, CF_ONES512:CF_ONES512 + 128]

    def mm_group(out_ap, pairs, reads, writes):
        def fn(e, out_ap=out_ap, pairs=pairs):
            n = len(pairs)
            last = None
            for i, (l, r) in enumerate(pairs):
                last = e.matmul(out_ap, l, r, start=(i == 0), stop=(i == n - 1))
            return last
        tr.op("pe", fn, reads, writes)

    tr.op("dve", lambda e: e.memset(epsb[:], EPS), writes=[b_epsb])
    tr.dma("sp", "ld", cb[:], cb_d, writes=[b_cb])
    tr.dma("sp", "ld", cf[:], cf_d, writes=[b_cf])

    SETUP_T_MARK = None

    with ExitStack() as s0:
        def sbt(name, shape, dt=F32):
            return s0.enter_context(nc.sbuf_tensor(name, list(shape), dt)), Buf(name)
        lnbC, b_lnbC = sbt("lnbC", [128, 768])
        ws_nat, b_wsnat = sbt("ws_nat", [128, 6, 128])
        wsTf, b_wsTf = sbt("wsTf", [128, 6, 128])
        bsr, b_bsr = sbt("bsr", [1, 6, 128])
        ones1, b_ones1 = sbt("ones1", [1, 128])
        tr.dma("sp", "ld", lnbC[:], cd_ln_b.partition_broadcast(128), writes=[b_lnbC])
        tr.dma("sp", "ld", ws_nat[:], cd_w_s.rearrange("h p q -> p h q"), writes=[b_wsnat])
        tr.dma("sp", "ld", bsr[:], cd_b_s.rearrange("(o h) p -> o h p", o=1), writes=[b_bsr])
        tr.op("dve", lambda e: e.memset(ones1[:], 1.0), writes=[b_ones1])
        stg, b_stg = sbt("stg", [48, 1024])

        def load_T(src2d, R, C, dst_of_c, b_dst):
            tr.dma("sp", "ld", stg[0:R, 0:C * 128], src2d, writes=[b_stg])
            for c in range(C):
                pb, b_pb = bank()
                tr.op("pe", lambda e, c=c, pb=pb, R=R: e.transpose(pb[:, 0:R], stg[0:R, c * 128:(c + 1) * 128], ident[0:R, 0:R]),
                      reads=[b_stg, b_cf], writes=[b_pb])
                tr.op("dve", lambda e, c=c, pb=pb, R=R: e.tensor_copy(out=dst_of_c(c), in_=pb[:, 0:R]), reads=[b_pb], writes=[b_dst])

        load_T(cvec, 2, 8, lambda c: cT[:, c, :], b_cT)
        for l in range(2):
            load_T(ada_b[l].rearrange("(n p) -> n p", p=128), 48, 1, lambda c, l=l: adab[:, l, :], b_adab)
            load_T(mix_norm_g[l].rearrange("(k p) -> k p", p=128), 8, 1, lambda c, l=l: ngm[:, l, 0, :], b_ngm)
            load_T(ffn_norm_g[l].rearrange("(k p) -> k p", p=128), 8, 1, lambda c, l=l: ngm[:, l, 1, :], b_ngm)
        load_T(final_g.rearrange("(k p) -> k p", p=128), 8, 1, lambda c: fgm[:], b_fgm)
        load_T(ab_conv_a, 3, 4, lambda c: wA[:, :, c], b_wA)
        load_T(ab_conv_b_w, 31, 4, lambda c: wB[:, :, c], b_wB)
        tr.op("dve", lambda e: e.tensor_scalar(out=wB[:], in0=wB[:], scalar1=0.5, scalar2=None, op0=ALU.mult), reads=[b_wB], writes=[b_wB])
        load_T(ab_conv_b_b.rearrange("(g p) -> g p", p=128), 4, 1, lambda c: bB[:], b_bB)
        load_T(ab_ln_g.rearrange("(g p) -> g p", p=128), 4, 1, lambda c: lngB[:], b_lngB)
        load_T(ab_ln_b.rearrange("(g p) -> g p", p=128), 4, 1, lambda c: lnbB[:], b_lnbB)
        load_T(cd_ln_g.rearrange("(g p) -> g p", p=128), 6, 1, lambda c: lngC[:], b_lngC)
        for h in range(6):
            pb, b_pb = bank()
            tr.op("pe", lambda e, h=h, pb=pb: e.transpose(pb[:, 0:128], ws_nat[:, h, :], ident),
                  reads=[b_wsnat, b_cf], writes=[b_pb])
            tr.op("dve", lambda e, h=h, pb=pb: e.tensor_copy(out=wsT[:, h, :], in_=pb[:, 0:128]), reads=[b_pb], writes=[b_wsT])
            tr.op("act", lambda e, h=h, pb=pb: e.activation(out=wsTf[:, h, :], in_=pb[:, 0:128], func=AF.Copy), reads=[b_pb], writes=[b_wsTf])
        for h in range(6):
            pb, b_pb = bank()
            mm_group(pb[:, 0:128], [(lnbC[:, h * 128:(h + 1) * 128], wsTf[:, h, :]), (ones1[:], bsr[:, h, :])],
                     reads=[b_lnbC, b_wsTf, b_ones1, b_bsr], writes=[b_pb])
            tr.op("dve", lambda e, h=h, pb=pb: e.tensor_copy(out=Cst[:, h, :], in_=pb[:, 0:128]), reads=[b_pb], writes=[b_Cst])
        barrier()


    b_cast = {k: Buf(k) for k in ("castA", "castB", "castC", "castD", "castE", "castF")}

    def cast_units(key, dst, src, ncols0, nunits):
        for u in range(nunits):
            c0 = ncols0 + u * 128
            tr.dma("pool", key, dst[u], src[:, c0:c0 + 128].rearrange("(k p) n -> p k n", p=128), writes=[b_cast[key]])

    cast_units("castA", w_in0_s, ab_w_in, 0, 20)
    cast_units("castB", w_out0_s, ab_w_out, 0, 8)
    cast_units("castC", ffi_s[0], ffn_w_in[0], 0, 44)
    cast_units("castD", ffo_s[0], ffn_w_out[0], 0, 8)
    cast_units("castE", cdu_s[0:6], cd_w_in, 0, 6)
    cast_units("castE", cdu_s[6:8], cd_w_in, 1536, 2)
    for k in range(KC):
        tr.dma("pool", "castE", cdv_s[k], cd_w_in[k * 128:(k + 1) * 128, 768:1536], writes=[b_cast["castE"]])
    tr.dma("pool", "vres", vres[:], cdv_s.rearrange("k p n -> p k n"), reads=[b_cast["castE"]], writes=[b_vres])
    cast_units("castF", w_out1_s, cd_w_out, 0, 8)
    cast_units("castF", ffi_s[1], ffn_w_in[1], 0, 44)
    cast_units("castF", ffo_s[1], ffn_w_out[1], 0, 8)

    def resolve(spec):
        kind = spec[0]
        if kind == "w_in0":
            return w_in0_s[spec[1]], b_cast["castA"]
        if kind == "w_out0":
            return w_out0_s[spec[1]], b_cast["castB"]
        if kind == "ffi":
            return ffi_s[spec[1]][spec[2]], b_cast["castC" if spec[1] == 0 else "castF"]
        if kind == "ffo":
            return ffo_s[spec[1]][spec[2]][:, spec[3]:spec[4], :], b_cast["castD" if spec[1] == 0 else "castF"]
        if kind == "cdu":
            return cdu_s[spec[1]], b_cast["castE"]
        if kind == "cdv":
            return cdv_s[spec[1]], b_cast["castE"]
        if kind == "w_out1":
            return w_out1_s[spec[1]], b_cast["castF"]
        raise ValueError(spec)

    recorded = []
    ring_state = {"issued": 0, "used": 0}

    def ring_issue():
        if plan_specs is None:
            return
        u = ring_state["issued"]
        if u >= len(plan_specs):
            return
        ring_state["issued"] += 1
        src, cbuf = resolve(plan_specs[u])
        s_ = u % NSLOT
        shp = list(src.shape)
        n = 1
        for d_ in shp[1:]:
            n *= d_
        dst = ring[:, s_, 0:n]
        if len(shp) == 3:
            dst = dst.rearrange("p (k n) -> p k n", n=shp[2])
        tr.dma("sp", "ring%d" % s_, dst, src, reads=[cbuf], writes=[b_ring[s_]])

    def ring_next(spec):
        u = ring_state["used"]
        ring_state["used"] += 1
        recorded.append(spec)
        if plan_specs is not None:
            assert plan_specs[u] == spec, (u, plan_specs[u], spec)
        s_ = u % NSLOT
        return ring[:, s_, :], b_ring[s_]

    def ring_done():
        ring_issue()

    tr.op("act", lambda e: e.activation(out=scT[:], in_=cT[:], func=AF.Silu), reads=[b_cT], writes=[b_scT])
    with ExitStack() as s1:
        adaf = [(s1.enter_context(nc.sbuf_tensor("adaf%d" % i, [128, KC, 512], F32)), Buf("adaf%d" % i)) for i in range(2)]
        adab16 = [(s1.enter_context(nc.sbuf_tensor("adab16_%d" % i, [128, KC, 512], BF16)), Buf("adab16_%d" % i)) for i in range(2)]
        for i in range(2):
            tr.newsem("ada%d" % i)
            DMAKEYS.append("ada%d" % i)
        for l in range(2):
            pb, b_pb = bank()
            for nb_ in range(12):
                i = (l * 12 + nb_) % 2
                af, b_af = adaf[i]
                a16, b_a16 = adab16[i]
                tr.dma("sp", "ada%d" % i, af[:], ada_w[l][:, nb_ * 512:(nb_ + 1) * 512].rearrange("(k p) n -> p k n", p=128), writes=[b_af])
                tr.op("act", lambda e, af=af, a16=a16: e.activation(out=a16[:, 0:4], in_=af[:, 0:4], func=AF.Copy), reads=[b_af], writes=[b_a16])
                tr.op("dve", lambda e, af=af, a16=a16: e.tensor_copy(out=a16[:, 4:8], in_=af[:, 4:8]), reads=[b_af], writes=[b_a16])
                for q_ in range(4):
                    n = nb_ * 4 + q_
                    mm_group(pb[:, 2 * n:2 * n + 2], [(a16[:, k, q_ * 128:(q_ + 1) * 128], scT[:, k, :]) for k in range(KC)],
                             reads=[b_a16, b_scT], writes=[b_pb])
            tr.op("dve", lambda e, l=l, pb=pb: e.tensor_tensor(
                out=modr[:, l], in0=pb[:, 0:96].rearrange("p (n s) -> p n s", s=2),
                in1=adab[:, l, :].unsqueeze(2).broadcast_to([128, 48, 2]), op=ALU.add),
                reads=[b_pb, b_adab], writes=[b_modr])
        barrier()
    for _ in range(NSLOT):
        ring_issue()

    for l in range(2):
        for wch, kind in ((0, 1), (1, 4)):
            tr.op("dve", lambda e, l=l, wch=wch, kind=kind: e.scalar_tensor_tensor(
                out=gm[:, l, wch], in0=modr[:, l, kind * 8:(kind + 1) * 8, :], scalar=1.0,
                in1=ngm[:, l, wch, :].unsqueeze(2).broadcast_to([128, KC, 2]), op0=ALU.add, op1=ALU.mult),
                reads=[b_modr, b_ngm], writes=[b_gm])

    def MOD(l, kind, kc, si):
        return modr[:, l, kind * 8 + kc, si:si + 1]

    def GM(l, wch, kc, si):
        return gm[:, l, wch, kc, si:si + 1]

    def make_env(phase):
        ph = ExitStack()
        P1 = (phase == 1)

        def sbp(name, shape, dt=F32):
            return ph.enter_context(nc.sbuf_tensor("%s_p%d" % (name, phase), list(shape), dt)), Buf(name)

        xTa, b_xTa = sbp("xTa", [128, KC, TW])
        sq, b_sq = sbp("sq", [128, KC, TW], BF16)
        hTa, b_hTa = sbp("hTa", [128, KC, TW], BF16)
        hTb, b_hTb = sbp("hTb", [128, KC, TW], BF16)
        rstd, b_rstd = sbp("rstd", [128, TW])
        hid, b_hid = sbp("hid", [128, NFF, T], BF16)
        tmpg = [sbp("tmpg%d" % i, [128, T]) for i in range(2)]
        xTb, b_xTb = sbp("xTb", [128, KC, TW])
        nts = [sbp("nt%d" % i, [128, TW]) for i in range(2)]
        if P1:
            xtok = [sbp("xtok%d" % i, [128, D]) for i in range(3)]
            mix, _ = sbp("mix", [128, 9984])
            cz, b_cz = sbp("cz", [128, 8, T])
            zsq, b_zsq = sbp("zsq", [128, 4, T])
        else:
            otok = [sbp("otok%d" % i, [128, 4, D]) for i in range(2)]
            yc3 = [sbp("yc3_%d" % i, [128, 6, T], BF16) for i in range(2)]
            ydt = [sbp("ydt%d" % i, [128, 2, T], BF16) for i in range(2)]
            mix, _ = sbp("mix", [128, 16])
            tbuf3, b_tbuf3 = sbp("tbuf3", [128, KC, TW])
            sq2, b_sq2 = sbp("sq2", [128, KC, TW], BF16)
            rstd2, b_rstd2 = sbp("rstd2", [128, TW])
        if P1:
            tbuf, b_tbuf = None, None
        else:
            tbuf, b_tbuf = tbuf3, b_tbuf3

        if P1:
            def mixv(a, b, dt=F32):
                v = mix[:, a:b]
                return v.bitcast(BF16) if dt is BF16 else v
            ab_t, b_ab = mixv(0, 2048).rearrange("p (g t) -> p g t", t=T), Buf("ab")
            ca, b_ca = mixv(2048, 3136, BF16).rearrange("p (g t) -> p g t", t=TW), Buf("ca")
            gB, b_gB = mixv(3136, 4224, BF16).rearrange("p (g t) -> p g t", t=TW), Buf("gB")
            tmpc = [(mixv(4224 + i * 544, 4768 + i * 544).rearrange("p (h c) -> p h c", c=HW_), Buf("tmpc%d" % i)) for i in range(2)]
            tmps = [(mixv(5312 + i * 544, 5856 + i * 544).rearrange("p (h c) -> p h c", c=HW_), Buf("tmps%d" % i)) for i in range(2)]
            lnt = [(mixv(6400 + i * 512, 6912 + i * 512), Buf("lnt%d" % i)) for i in range(3)]
            yT, b_yT = mixv(7936, 9984, BF16).rearrange("p (k t) -> p k t", t=T), Buf("yT")
            czf = cz[:].rearrange("p k t -> p (k t)")
            zsqf = zsq[:].rearrange("p g t -> p (g t)")
            uT, b_uT = czf[:, 0:3072].rearrange("p (k t) -> p k t", t=T), [b_cz]
            Gst, b_Gst = czf[:, 3072:4096].bitcast(BF16).rearrange("p (r t) -> p r t", t=T), [b_cz]
            ycT, b_ycT = zsqf[:, 0:1536].bitcast(BF16).rearrange("p (k t) -> p k t", t=T), [b_zsq]
            fT, b_fT = zsqf[:, 1536:2048].bitcast(BF16).rearrange("p (c t) -> p c t", t=T), [b_zsq]
            vfs = [(mixv(7936, 8704), [b_yT]), (mixv(7936, 8704), [b_yT])]
            t6, b_t6 = mixv(8704, 9472).rearrange("p (h c) -> p h c", c=128), [b_yT]
            vn = [(mixv(9472, 9856, BF16), [b_yT]), (mixv(7552, 7936, BF16), [lnt[2][1]])]
            vsq, b_vsq = mixv(6400, 7168), [lnt[0][1], lnt[1][1]]

        def rms_to_h(xT, b_xT, l, wch, si, c0, ncols, plain_gain=None, hT=None, b_hT=None, yielding=True, scr=None):
            sq_, b_sq_, rstd_, b_rstd_ = scr if scr is not None else (sq, b_sq, rstd, b_rstd)
            c1 = c0 + ncols
            tr.op("act", lambda e: e.activation(out=sq_[:, :, c0:c1], in_=xT[:, :, c0:c1], func=AF.Square),
                  reads=[b_xT], writes=[b_sq_])
            if yielding:
                yield 4.0
            pieces = [(c0, c1)] if ncols <= 512 else [(c0, c0 + ncols // 2), (c0 + ncols // 2, c1)]
            for (a0, a1) in pieces:
                pb, b_pb = bank()
                mm_group(pb[:, 0:a1 - a0], [(ones_bf, sq_[:, k, a0:a1]) for k in range(KC)], reads=[b_sq_, b_cb], writes=[b_pb])
                tr.op("act", lambda e, pb=pb, a0=a0, a1=a1: e.activation(
                    out=rstd_[:, a0:a1], in_=pb[:, 0:a1 - a0], func=AF.Ln, bias=epsb[:, 0:1], scale=1.0),
                    reads=[b_pb, b_epsb], writes=[b_rstd_])
                tr.op("act", lambda e, a0=a0, a1=a1: e.activation(
                    out=rstd_[:, a0:a1], in_=rstd_[:, a0:a1], func=AF.Exp, scale=-0.5),
                    reads=[b_rstd_], writes=[b_rstd_])
            if yielding:
                yield 3.0
            if plain_gain is not None:
                for k in range(KC):
                    tr.op("dve", lambda e, k=k: e.scalar_tensor_tensor(
                        out=tbuf[:, k, c0:c1], in0=xT[:, k, c0:c1], scalar=plain_gain[:, k:k + 1], in1=rstd_[:, c0:c1],
                        op0=ALU.mult, op1=ALU.mult), reads=[b_xT, b_rstd_, b_fgm], writes=[b_tbuf])
                if yielding:
                    yield 5.0
                return
            kind_sh = 0 if wch == 0 else 3
            for k in range(KC):
                nt_, b_nt = nts[k % 2]
                tr.op("dve", lambda e, k=k, nt_=nt_: e.scalar_tensor_tensor(
                    out=nt_[:, c0:c1], in0=xT[:, k, c0:c1], scalar=GM(l, wch, k, si), in1=rstd_[:, c0:c1],
                    op0=ALU.mult, op1=ALU.mult), reads=[b_xT, b_rstd_, b_gm], writes=[b_nt])
                tr.op("act", lambda e, k=k, nt_=nt_: e.activation(
                    out=hT[:, k, c0:c1], in_=nt_[:, c0:c1], func=AF.Identity, bias=MOD(l, kind_sh, k, si), scale=1.0),
                    reads=[b_nt, b_modr], writes=[b_hT])
            if yielding:
                yield 5.0

        hsl = slice(HALO, HALO + T)

        def ffn(xT, b_xT, l, si, hT, b_hT, do_norm=True):
            if do_norm:
                yield from rms_to_h(xT, b_xT, l, 1, si, HALO, T, hT=hT, b_hT=b_hT)
            for j in range(NFF):
                slot, b_slot = ring_next(("ffi", l, j))
                w = slot.rearrange("p (k n) -> p k n", n=128)
                pg, b_pg = bank()
                mm_group(pg, [(w[:, k, :], hT[:, k, hsl]) for k in range(KC)], reads=[b_slot, b_hT], writes=[b_pg])
                ring_done()
                tg, b_tg = tmpg[j % 2]
                tr.op("act", lambda e, pg=pg, tg=tg: e.activation(out=tg[:], in_=pg, func=AF.Silu), reads=[b_pg], writes=[b_tg])
                slot, b_slot = ring_next(("ffi", l, NFF + j))
                w = slot.rearrange("p (k n) -> p k n", n=128)
                pu, b_pu = bank()
                mm_group(pu, [(w[:, k, :], hT[:, k, hsl]) for k in range(KC)], reads=[b_slot, b_hT], writes=[b_pu])
                ring_done()
                tr.op("dve", lambda e, pu=pu, tg=tg, j=j: e.tensor_tensor(out=hid[:, j, :], in0=pu, in1=tg[:], op=ALU.mult),
                      reads=[b_pu, b_tg], writes=[b_hid])
                yield 3.0
            for n in range(8):
                po, b_po = bank()
                for (k0, k1) in ((0, 8), (8, 16), (16, 22)):
                    slot, b_slot = ring_next(("ffo", l, n, k0, k1))
                    w = slot[:, 0:(k1 - k0) * 128].rearrange("p (k n) -> p k n", n=128)

                    def fn(e, w=w, po=po, k0=k0, k1=k1):
                        last = None
                        for k in range(k0, k1):
                            last = e.matmul(po, w[:, k - k0, :], hid[:, k, :], start=(k == 0), stop=(k == NFF - 1))
                        return last
                    tr.op("pe", fn, reads=[b_slot, b_hid], writes=[b_po])
                    ring_done()
                tr.op("dve", lambda e, po=po, n=n: e.scalar_tensor_tensor(
                    out=xT[:, n, hsl], in0=po, scalar=MOD(l, 5, n, si), in1=xT[:, n, hsl], op0=ALU.mult, op1=ALU.add),
                    reads=[b_po, b_xT, b_modr], writes=[b_xT])
                yield 4.2

        xl_ctr = [0]

        def load_x_tile(src, t0):
            nblk = [(t0 + r * 128, 128) for r in range(4)] + [(t0 + 512, 32)]
            staged = []
            for (r0, nr) in nblk:
                i = xl_ctr[0] % 3
                xl_ctr[0] += 1
                xt, b_xt = xtok[i]
                tr.dma("sp", "xl%d" % i, xt[0:nr, :], src[r0:r0 + nr, :], writes=[b_xt])
                staged.append((xt, b_xt, nr))
            return staged

        x_prefetched = {}

        def transpose_x_tile(xT, b_xT, src, t0, nxt=None):
            nblk = [(t0 + r * 128, 128) for r in range(4)] + [(t0 + 512, 32)]
            cnt = 0
            stg_ = {}

            def load(r, src_=src, nblk_=nblk, dst_=stg_):
                r0, nr = nblk_[r]
                i = xl_ctr[0] % 3
                xl_ctr[0] += 1
                xt, b_xt = xtok[i]
                tr.dma("sp", "xl%d" % i, xt[0:nr, :], src_[r0:r0 + nr, :], writes=[b_xt])
                dst_[r] = (xt, b_xt)
            key = (id(src), t0)
            if key in x_prefetched:
                stg_.update(x_prefetched.pop(key))
            else:
                for r in range(3):
                    load(r)
            yield 2.0
            for r in range(5):
                xt, b_xt = stg_[r]
                if r < 4:
                    for h in range(2):
                        pb, b_pb = bank()

                        def fn(e, h=h, pb=pb, xt=xt):
                            last = None
                            for kk in range(4):
                                k = 4 * h + kk
                                last = e.transpose(pb[:, kk * 128:(kk + 1) * 128], xt[:, k * 128:(k + 1) * 128], ident)
                            return last
                        tr.op("pe", fn, reads=[b_xt, b_cf], writes=[b_pb])
                        dst = xT[:, 4 * h:4 * h + 4, r * 128:(r + 1) * 128]
                        srcv = pb.rearrange("p (k c) -> p k c", c=128)
                        if cnt % 2:
                            tr.op("act", lambda e, dst=dst, srcv=srcv: e.activation(out=dst, in_=srcv, func=AF.Copy), reads=[b_pb], writes=[b_xT])
                        else:
                            tr.op("dve", lambda e, dst=dst, srcv=srcv: e.tensor_copy(out=dst, in_=srcv), reads=[b_pb], writes=[b_xT])
                        cnt += 1
                else:
                    pb, b_pb = bank()

                    def fn2(e, pb=pb, xt=xt):
                        last = None
                        for k in range(KC):
                            last = e.transpose(pb[:, k * 32:(k + 1) * 32], xt[0:32, k * 128:(k + 1) * 128], ident[0:32, 0:32])
                        return last
                    tr.op("pe", fn2, reads=[b_xt, b_cf], writes=[b_pb])
                    tr.op("dve", lambda e, pb=pb: e.tensor_copy(out=xT[:, :, 512:544], in_=pb[:, 0:256].rearrange("p (k c) -> p k c", c=32)),
                          reads=[b_pb], writes=[b_xT])
                if r + 3 < 5:
                    load(r + 3)
                yield 1.8
            if nxt is not None:
                src2, t02 = nxt
                nblk2 = [(t02 + r * 128, 128) for r in range(3)]
                pf = {}
                for r in range(3):
                    load(r, src2, nblk2, pf)
                x_prefetched[(id(src2), t02)] = pf

        msk = cf[:, CF_MSK:CF_MSK + 2]

        def make_tile(seq, j, xT, b_xT, nxt_tile=None):
          nxt = None if nxt_tile is None else ((xs if nxt_tile[0] == "S" else xp), nxt_tile[1] * T)
          si = 0 if seq == "S" else 1
          gi = j if seq == "S" else NT_S + j
          ntl = NT_S if seq == "S" else NT_P
          src = xs if seq == "S" else xp

          def stageA():
            hT, b_hT = hTa, b_hTa
            yield from transpose_x_tile(xT, b_xT, src, j * T, nxt)
            yield from rms_to_h(xT, b_xT, 0, 0, si, 0, TW, hT=hT, b_hT=b_hT)
            for g in range(4):
                def conv_chunk(cidx):
                    slot, b_slot = ring_next(("w_in0", cidx))
                    w = slot.rearrange("p (k n) -> p k n", n=128)
                    pr, b_pr = bankpair()

                    def fn(e, w=w, pr=pr):
                        last = None
                        for h in range(2):
                            for k in range(KC):
                                last = e.matmul(pr[:, h, 0:HW_], w[:, k, :], hT[:, k, h * HW_:(h + 1) * HW_],
                                                start=(k == 0), stop=(k == KC - 1))
                        return last
                    tr.op("pe", fn, reads=[b_slot, b_hT], writes=b_pr)
                    ring_done()
                    return pr[:, :, 0:HW_], b_pr
                tc_, b_tc = tmpc[g % 2]
                ts_, b_ts = tmps[g % 2]
                pv, b_pv = conv_chunk(4 + g)
                tr.op("act", lambda e, pv=pv, tc_=tc_: e.activation(out=tc_, in_=pv, func=AF.Copy), reads=b_pv, writes=[b_tc])
                pv, b_pv = conv_chunk(8 + g)
                tr.op("dve", lambda e, pv=pv, tc_=tc_, g=g: e.tensor_tensor(
                    out=ca[:, g, :].rearrange("p (h c) -> p h c", c=HW_), in0=pv, in1=tc_, op=ALU.mult),
                    reads=b_pv + [b_tc], writes=[b_ca])
                pv, b_pv = conv_chunk(16 + g)
                tr.op("act", lambda e, pv=pv, ts_=ts_: e.activation(out=ts_, in_=pv, func=AF.Tanh, scale=0.5), reads=b_pv, writes=[b_ts])
                pv, b_pv = conv_chunk(12 + g)
                tr.op("dve", lambda e, pv=pv, ts_=ts_, g=g: e.scalar_tensor_tensor(
                    out=gB[:, g, :].rearrange("p (h c) -> p h c", c=HW_), in0=ts_, scalar=1.0, in1=pv, op0=ALU.add, op1=ALU.mult),
                    reads=b_pv + [b_ts], writes=[b_gB])
                slot, b_slot = ring_next(("w_in0", g))
                w = slot.rearrange("p (k n) -> p k n", n=128)
                pb, b_pb = bank()
                mm_group(pb, [(w[:, k, :], hT[:, k, hsl]) for k in range(KC)], reads=[b_slot, b_hT], writes=[b_pb])
                ring_done()
                tr.op("act", lambda e, pb=pb, g=g: e.activation(out=ab_t[:, g, :], in_=pb, func=AF.Copy), reads=[b_pb], writes=[b_ab])
                yield 11.0
            for (cond_first, c0) in ((True, 0), (False, HALO + T)):
                is_edge = (j == 0) if cond_first else (j == ntl - 1)
                if not is_edge:
                    continue
                for (buf, b_b) in ((ca, b_ca), (gB, b_gB)):
                    if seq == "P":
                        tr.op("dve", lambda e, buf=buf, c0=c0: e.memset(buf[:, :, c0:c0 + HALO], 0.0), writes=[b_b])
                    else:
                        mcol = 0 if cond_first else 1
                        tr.op("dve", lambda e, buf=buf, c0=c0, mcol=mcol: e.tensor_scalar(
                            out=buf[:, :, c0:c0 + HALO], in0=buf[:, :, c0:c0 + HALO], scalar1=msk[:, mcol:mcol + 1], scalar2=None,
                            op0=ALU.mult), reads=[b_b, b_cf], writes=[b_b])
            yield 0.5

          def stageB():
            acc = cz[:, 0:4, :]
            z = cz[:, 4:8, :]
            for g in range(4):
                tr.op("dve", lambda e, g=g: e.tensor_scalar(
                    out=acc[:, g, :], in0=ca[:, g, HALO - 1:HALO - 1 + T], scalar1=wA[:, 0, g:g + 1], scalar2=None, op0=ALU.mult),
                    reads=[b_ca, b_wA], writes=[b_cz])
                for kk in (1, 2):
                    tr.op("dve", lambda e, g=g, kk=kk: e.scalar_tensor_tensor(
                        out=acc[:, g, :], in0=ca[:, g, HALO - 1 + kk:HALO - 1 + kk + T], scalar=wA[:, kk, g:g + 1], in1=acc[:, g, :],
                        op0=ALU.mult, op1=ALU.add), reads=[b_ca, b_wA, b_cz], writes=[b_cz])
                tr.op("dve", lambda e, g=g: e.tensor_tensor(out=yT[:, g, :], in0=acc[:, g, :], in1=ab_t[:, g, :], op=ALU.mult),
                      reads=[b_cz, b_ab], writes=[b_yT])
                yield 2.9
            for g in range(4):
                tr.op("dve", lambda e, g=g: e.tensor_scalar(
                    out=z[:, g, :], in0=gB[:, g, 1:1 + T], scalar1=wB[:, 0, g:g + 1], scalar2=bB[:, g:g + 1], op0=ALU.mult, op1=ALU.add),
                    reads=[b_gB, b_wB, b_bB], writes=[b_cz])
            for kk in range(1, 31):
                for g in range(4):
                    tr.op("dve", lambda e, g=g, kk=kk: e.scalar_tensor_tensor(
                        out=z[:, g, :], in0=gB[:, g, 1 + kk:1 + kk + T], scalar=wB[:, kk, g:g + 1], in1=z[:, g, :],
                        op0=ALU.mult, op1=ALU.add), reads=[b_gB, b_wB, b_cz], writes=[b_cz])
                yield 2.9
            z16 = zsqf[:, 0:1024].bitcast(BF16).rearrange("p (g t) -> p g t", t=T)
            zq16 = zsqf[:, 1024:2048].bitcast(BF16).rearrange("p (g t) -> p g t", t=T)
            tr.op("act", lambda e: e.activation(out=zq16, in_=z, func=AF.Square), reads=[b_cz], writes=[b_zsq])
            tr.op("act", lambda e: e.activation(out=z16, in_=z, func=AF.Copy), reads=[b_cz], writes=[b_zsq])
            p1, b_p1 = bank()
            mm_group(p1, [(ones_bf, z16[:, g, :]) for g in range(4)], reads=[b_zsq, b_cb], writes=[b_p1])
            p2, b_p2 = bank()
            mm_group(p2, [(ones_bf, zq16[:, g, :]) for g in range(4)], reads=[b_zsq, b_cb], writes=[b_p2])
            yield 1.0
            mean, b_mean = lnt[0]
            m2, b_m2 = lnt[1]
            lrs, b_lrs = lnt[2]
            tr.op("act", lambda e: e.activation(out=mean, in_=p1, func=AF.Copy, scale=2.0), reads=[b_p1], writes=[b_mean])
            tr.op("dve", lambda e: e.tensor_tensor(out=m2, in0=mean, in1=mean, op=ALU.mult), reads=[b_mean], writes=[b_m2])
            tr.op("dve", lambda e: e.scalar_tensor_tensor(out=m2, in0=p2, scalar=2.0, in1=m2, op0=ALU.mult, op1=ALU.subtract),
                  reads=[b_p2, b_m2], writes=[b_m2])
            tr.op("act", lambda e: e.activation(out=lrs, in_=m2, func=AF.Ln, bias=epsb[:, 0:1], scale=1.0),
                  reads=[b_m2, b_epsb], writes=[b_lrs])
            tr.op("act", lambda e: e.activation(out=lrs, in_=lrs, func=AF.Exp, scale=-0.5), reads=[b_lrs], writes=[b_lrs])
            yield 3.0
            for g in range(4):
                tr.op("dve", lambda e, g=g: e.tensor_tensor(out=z[:, g, :], in0=z[:, g, :], in1=mean, op=ALU.subtract),
                      reads=[b_cz, b_mean], writes=[b_cz])
                tr.op("dve", lambda e, g=g: e.tensor_tensor(out=z[:, g, :], in0=z[:, g, :], in1=lrs, op=ALU.mult),
                      reads=[b_cz, b_lrs], writes=[b_cz])
                tr.op("act", lambda e, g=g: e.activation(out=yT[:, 4 + g, :], in_=z[:, g, :], func=AF.Silu,
                                                         bias=lnbB[:, g:g + 1], scale=lngB[:, g:g + 1]),
                      reads=[b_cz, b_lngB, b_lnbB], writes=[b_yT])
                yield 2.0
            for n in range(8):
                slot, b_slot = ring_next(("w_out0", n))
                w = slot.rearrange("p (k n) -> p k n", n=128)
                po, b_po = bank()
                mm_group(po, [(w[:, k, :], yT[:, k, :]) for k in range(KC)], reads=[b_slot, b_yT], writes=[b_po])
                ring_done()
                tr.op("dve", lambda e, po=po, n=n: e.scalar_tensor_tensor(
                    out=xT[:, n, hsl], in0=po, scalar=MOD(0, 2, n, si), in1=xT[:, n, hsl], op0=ALU.mult, op1=ALU.add),
                    reads=[b_po, b_xT, b_modr], writes=[b_xT])
                yield 1.8

          def stageN2():
            yield from rms_to_h(xT, b_xT, 0, 1, si, HALO, T, hT=hTa, b_hT=b_hTa)

          def stageC():
            yield from ffn(xT, b_xT, 0, si, hTa, b_hTa, do_norm=False)

          def stageN3():
            tr.dma("sp", "st", X2[gi], xT[:, :, hsl], reads=[b_xT], writes=[b_X2[gi]])
            yield from rms_to_h(xT, b_xT, 1, 0, si, HALO, T, hT=hTb, b_hT=b_hTb)

          def stageD():
            hT, b_hT = hTb, b_hTb
            for n in range(8):
                slot, b_slot = ring_next(("cdu", n))
                w = slot.rearrange("p (k n) -> p k n", n=128)
                pb, b_pb = bank()
                mm_group(pb, [(w[:, k, :], hT[:, k, hsl]) for k in range(KC)], reads=[b_slot, b_hT], writes=[b_pb])
                ring_done()
                if n < 6:
                    tr.op("act", lambda e, pb=pb, n=n: e.activation(out=uT[:, n, :], in_=pb, func=AF.Copy), reads=[b_pb], writes=b_uT)
                else:
                    tr.op("dve", lambda e, pb=pb, n=n: e.tensor_copy(out=fT[:, n - 6, :], in_=pb), reads=[b_pb], writes=b_fT)
            for cc in range(2):
                for ri in range(2):
                    pb, b_pb = bank()
                    mm_group(pb, [(cb[:, CB_D0 + ri * 128:CB_D0 + (ri + 1) * 128], fT[:, cc, :])], reads=[b_cb] + b_fT, writes=[b_pb])
                    pl = cc * 2 + ri
                    if ri:
                        tr.op("act", lambda e, pb=pb, pl=pl: e.activation(out=Gst[:, pl, :], in_=pb, func=AF.Copy), reads=[b_pb], writes=b_Gst)
                    else:
                        tr.op("dve", lambda e, pb=pb, pl=pl: e.tensor_copy(out=Gst[:, pl, :], in_=pb), reads=[b_pb], writes=b_Gst)
            for r_ in range(4):
                if seq == "P":
                    gdst, b_g = G_P[4 * j:4 * j + 4, r_].rearrange("a c b -> c a b"), b_GP
                else:
                    gdst, b_g = G_S[j][:, r_].rearrange("a c b -> c a b"), b_GS[j]
                tr.dma("sp", "g", gdst, Gst[:, r_, :].rearrange("c (a b) -> c a b", b=128), reads=b_Gst, writes=[b_g])
            yield 14.0
            for r in range(4):
                pr, b_pr = bankpair()
                prf = pr.rearrange("p b c -> p (b c)")

                def fnv(e, r=r, pr=pr):
                    last = None
                    for k in range(KC):
                        lh = hT[:, k, HALO + r * 128:HALO + (r + 1) * 128]
                        e.matmul(pr[:, 0, :], lh, vres[:, k, 0:512], start=(k == 0), stop=(k == KC - 1))
                        last = e.matmul(pr[:, 1, 0:256], lh, vres[:, k, 512:768], start=(k == 0), stop=(k == KC - 1))
                    return last
                tr.op("pe", fnv, reads=[b_hT, b_vres], writes=b_pr)
                vf, b_vf = vfs[r % 2]
                st4, b_st4 = st4s[r]
                tr.op("act", lambda e, prf=prf: e.activation(out=vf, in_=prf[:, 0:768], func=AF.Copy), reads=b_pr, writes=b_vf)
                tr.op("dve", lambda e: e.tensor_reduce(out=st4[:, 0:1], in_=vf, axis=mybir.AxisListType.X, op=ALU.add),
                      reads=b_vf, writes=[b_st4])
                tr.op("act", lambda e: e.activation(out=vsq, in_=vf, func=AF.Square), reads=b_vf, writes=b_vsq)
                tr.op("dve", lambda e: e.tensor_reduce(out=st4[:, 1:2], in_=vsq, axis=mybir.AxisListType.X, op=ALU.add),
                      reads=b_vsq, writes=[b_st4])
                tr.op("dve", lambda e: e.tensor_scalar(out=st4[:, 2:4], in0=st4[:, 0:2], scalar1=1.0 / 768.0, scalar2=None, op0=ALU.mult),
                      reads=[b_st4], writes=[b_st4])
                tr.op("dve", lambda e: e.tensor_tensor(out=st4[:, 4:5], in0=st4[:, 2:3], in1=st4[:, 2:3], op=ALU.mult),
                      reads=[b_st4], writes=[b_st4])
                tr.op("dve", lambda e: e.tensor_tensor(out=st4[:, 5:6], in0=st4[:, 3:4], in1=st4[:, 4:5], op=ALU.subtract),
                      reads=[b_st4], writes=[b_st4])
                tr.op("act", lambda e: e.activation(out=st4[:, 6:7], in_=st4[:, 5:6], func=AF.Ln, bias=epsb[:, 0:1], scale=1.0),
                      reads=[b_st4, b_epsb], writes=[b_st4])
                tr.op("act", lambda e: e.activation(out=st4[:, 6:7], in_=st4[:, 6:7], func=AF.Exp, scale=-0.5), reads=[b_st4], writes=[b_st4])
                tr.op("dve", lambda e: e.scalar_tensor_tensor(out=st4[:, 7:8], in0=st4[:, 2:3], scalar=-1.0, in1=st4[:, 6:7],
                                                              op0=ALU.mult, op1=ALU.mult), reads=[b_st4], writes=[b_st4])
                vn_, b_vn = vn[r % 2]
                tr.op("act", lambda e, vn_=vn_: e.activation(out=vn_, in_=vf, func=AF.Identity, bias=st4[:, 7:8], scale=st4[:, 6:7]),
                      reads=b_vf + [b_st4], writes=b_vn)
                yield 7.0
                pr2, b_pr2 = bankpair()
                pr2f = pr2.rearrange("p b c -> p (b c)")

                def fng(e, vn_=vn_, pr2f=pr2f):
                    last = None
                    for h in range(6):
                        last = e.matmul(pr2f[:, h * 128:(h + 1) * 128], vn_[:, h * 128:(h + 1) * 128], wsT[:, h, :], start=True, stop=True)
                    return last
                tr.op("pe", fng, reads=b_vn + [b_wsT], writes=b_pr2)
                for h in range(6):
                    tr.op("dve", lambda e, h=h, pr2f=pr2f: e.scalar_tensor_tensor(
                        out=t6[:, h, :], in0=pr2f[:, h * 128:(h + 1) * 128], scalar=lngC[:, h:h + 1], in1=Cst[:, h, :],
                        op0=ALU.mult, op1=ALU.add), reads=b_pr2 + [b_lngC, b_Cst], writes=b_t6)
                tr.op("dve", lambda e, r=r: e.tensor_tensor(out=ycT[:, :, r * 128:(r + 1) * 128], in0=t6,
                                                             in1=uT[:, :, r * 128:(r + 1) * 128], op=ALU.mult),
                      reads=b_t6 + b_uT, writes=b_ycT)
                yield 8.0
            tr.dma("sp", "st", YC[gi], ycT, reads=b_ycT, writes=[b_YC[gi]])
            yield 1.0

          return stageA, stageB, stageC, stageD, stageN2, stageN3


        def p3_load(idx, tiles3):
            if idx >= len(tiles3):
                return None
            seq, j = tiles3[idx]
            gi = j if seq == "S" else NT_S + j
            xT, b_xT = (xTa, b_xTa) if idx % 2 == 0 else (xTb, b_xTb)
            yc_, b_yc = yc3[idx % 2]
            yd_, b_yd = ydt[idx % 2]
            key = "p3a" if idx % 2 == 0 else "p3b"
            tr.dma("sp", key, xT[:, :, hsl], X2[gi], reads=[b_X2[gi]], writes=[b_xT])
            tr.dma("sp", key, yc_[:], YC[gi], reads=[b_YC[gi]], writes=[b_yc])
            tr.dma("sp", key, yd_[:], YD[:, :, gi * T:(gi + 1) * T].rearrange("c p t -> p c t"), reads=[b_YD], writes=[b_yd])
            return (xT, b_xT, yc_, b_yc, yd_, b_yd)

        def make_tile3(idx, tiles3):
            seq, j = tiles3[idx]
            si = 0 if seq == "S" else 1
            st = {}

            def stageP():
                st["ld"] = p3_load(idx, tiles3)
                xT, b_xT, yc_, b_yc, yd_, b_yd = st["ld"]
                yield 3.0
                for n in range(8):
                    slot, b_slot = ring_next(("w_out1", n))
                    w = slot.rearrange("p (k n) -> p k n", n=128)
                    po, b_po = bank()
                    pairs = [(w[:, k, :], yc_[:, k, :]) for k in range(6)] + [(w[:, 6 + c, :], yd_[:, c, :]) for c in range(2)]
                    mm_group(po, pairs, reads=[b_slot, b_yc, b_yd], writes=[b_po])
                    ring_done()
                    tr.op("dve", lambda e, po=po, n=n: e.scalar_tensor_tensor(
                        out=xT[:, n, hsl], in0=po, scalar=MOD(1, 2, n, si), in1=xT[:, n, hsl], op0=ALU.mult, op1=ALU.add),
                        reads=[b_po, b_xT, b_modr], writes=[b_xT])
                    yield 1.8

            hT3, b_hT3 = (hTa, b_hTa) if idx % 2 == 0 else (hTb, b_hTb)

            def stageN():
                xT, b_xT = st["ld"][0], st["ld"][1]
                yield from rms_to_h(xT, b_xT, 1, 1, si, HALO, T, hT=hT3, b_hT=b_hT3)

            def stageQ():
                xT, b_xT = st["ld"][0], st["ld"][1]
                yield from ffn(xT, b_xT, 1, si, hT3, b_hT3, do_norm=False)

            def stageR():
                xT, b_xT = st["ld"][0], st["ld"][1]
                yield from rms_to_h(xT, b_xT, 1, 0, si, HALO, T, plain_gain=fgm, scr=(sq2, b_sq2, rstd2, b_rstd2))
                ot, b_ot = otok[idx % 2]
                cnt = 0
                for r in range(4):
                    for half in range(2):
                        pb, b_pb = bank()

                        def fn(e, r=r, half=half, pb=pb):
                            last = None
                            for i in range(4):
                                last = e.transpose(pb[:, i * 128:(i + 1) * 128],
                                                   tbuf[:, 4 * half + i, HALO + r * 128:HALO + (r + 1) * 128], ident)
                            return last
                        tr.op("pe", fn, reads=[b_tbuf, b_cf], writes=[b_pb])
                        if cnt % 2:
                            tr.op("act", lambda e, pb=pb, r=r, half=half: e.activation(
                                out=ot[:, r, half * 512:(half + 1) * 512], in_=pb, func=AF.Copy), reads=[b_pb], writes=[b_ot])
                        else:
                            tr.op("dve", lambda e, pb=pb, r=r, half=half: e.tensor_copy(
                                out=ot[:, r, half * 512:(half + 1) * 512], in_=pb), reads=[b_pb], writes=[b_ot])
                        cnt += 1
                    yield 2.0
                ydst = ys if seq == "S" else yp
                tr.dma("sp", "out", ydst[j * T:(j + 1) * T, :].rearrange("(r p) d -> p r d", p=128), ot[:], reads=[b_ot])
                yield 0.5

            return stageP, stageQ, stageR, stageN

        return dict(close=ph.close, make_tile=make_tile if P1 else None, bufs=((xTa, b_xTa), (xTb, b_xTb)),
                    make_tile3=make_tile3)

    env = make_env(1)
    tiles1 = [("S", j) for j in range(NT_S)] + [("P", j) for j in range(NT_P)]
    if debug and "nt1" in debug:
        tiles1 = tiles1[:debug["nt1"]]

    def run_interleaved(gens):
        acc = [0.0] * len(gens)
        alive = [True] * len(gens)
        while any(alive):
            i = min((a, ix) for ix, a in enumerate(acc) if alive[ix])[1]
            try:
                acc[i] += next(gens[i])
            except StopIteration:
                alive[i] = False

    def after_tile(seq, j):
        if seq != "S":
            return
        if debug and debug.get("nocc"):
            for r_ in range(4):
                tr.dma("sp", "ph2", AG[j][r_], G_S[j], reads=[b_GS[j]], writes=[b_AG[j]])
        else:
            tr.custom("pool", "cc", 1, lambda e, j=j: e.collective_compute(
                "AllGather", ALU.bypass, replica_groups=[[0, 1, 2, 3], [4, 5, 6, 7]], ins=[G_S[j].rearrange("a r (c1 c2) b -> (a r c1) (c2 b)", c2=4)],
                outs=[AG[j].rearrange("k a r (c1 c2) b -> (k a r c1) (c2 b)", c2=4)]),
                reads=[b_GS[j]], writes=[b_AG[j]])

    def chain(*gs):
        for g_ in gs:
            yield from g_

    stages = []
    for idx, (seq, j) in enumerate(tiles1):
        xT_, b_xT_ = env["bufs"][idx % 2]
        nxt_tile = tiles1[idx + 1] if idx + 1 < len(tiles1) else None
        stages.append(env["make_tile"](seq, j, xT_, b_xT_, nxt_tile) + (seq, j))
    n1 = len(stages)
    if n1:
        run_interleaved([stages[0][0]()])
        run_interleaved([stages[0][1]()])
    for i in range(1, n1 + 1):
        th = []
        if i < n1:
            th.append(stages[i][0]())
        th.append(stages[i - 1][4]())
        gens = [chain(*th)]
        if i - 2 >= 0:
            gens.insert(0, stages[i - 2][3]())
        run_interleaved(gens)
        if i - 2 >= 0:
            after_tile(stages[i - 2][6], stages[i - 2][7])
        gens = [chain(stages[i - 1][2](), stages[i - 1][5]())]
        if i < n1:
            gens.insert(0, stages[i][1]())
        run_interleaved(gens)
    if n1:
        run_interleaved([stages[n1 - 1][3]()])
        after_tile(stages[n1 - 1][6], stages[n1 - 1][7])
    barrier()
    env["close"]()
    vres_scope.close()

    with ExitStack() as s2:
        def sb2(name, shape, dt=F32):
            return s2.enter_context(nc.sbuf_tensor(name, list(shape), dt)), Buf(name)
        GAs = [sb2("GA%d" % i, [128, 2, 64, 128], BF16) for i in range(2)]
        HB, b_HB = sb2("HB", [128, 2, 128, 128], BF16)
        yst, b_yst = sb2("yst", [128, SP_LEN], BF16)
        tt = [sb2("tt%d" % i, [128, 256], BF16) for i in range(4)]

        def params(seq):
            if seq == "P":
                return dict(Na=64, Nkb=128, ntok=SP_LEN, off=SS_LEN,
                            W11=cb[0:64, CB_W1P1:CB_W1P1 + 128], W12=cb[0:64, CB_W1P2:CB_W1P2 + 128],
                            Tr=cf[:, CF_TP:CF_TP + 64], Ti=cf[:, CF_TP + 64:CF_TP + 128],
                            W2r=cb[:, CB_W2P:CB_W2P + 128], W2i=cb[:, CB_W2P + 128:CB_W2P + 256])
            return dict(Na=128, Nkb=32, ntok=SS_LEN, off=0,
                        W11=cb[:, CB_W1S1:CB_W1S1 + 256], W12=cb[:, CB_W1S2:CB_W1S2 + 256],
                        Tr=cf[:, CF_TS:CF_TS + 128], Ti=cf[:, CF_TS + 128:CF_TS + 256],
                        W2r=cb[:, CB_W2S:CB_W2S + 32], W2i=cb[:, CB_W2S + 32:CB_W2S + 64])

        def fft_load(seq, cc, chh, GA, b_GA):
            cs = slice(chh * 64, (chh + 1) * 64)
            if seq == "P":
                for ri in range(2):
                    tr.dma("sp", "ph2", GA[0:64, ri, :, :], G_P[:, 2 * cc + ri, cs], reads=[b_GP], writes=[b_GA])
            else:
                for rk in range(4):
                    for ri in range(2):
                        tr.dma("sp", "ph2", GA[32 * rk:32 * rk + 32, ri, :, :], AG_all[:, rk, :, 2 * cc + ri, cs],
                               reads=b_AG, writes=[b_GA])

        def fft_stage1(seq, cc, chh, GA, b_GA):
            pr = params(seq)
            Na, W11, W12, Tr, Ti = pr["Na"], pr["W11"], pr["W12"], pr["Tr"], pr["Ti"]
            nch = 512 // (2 * Na)
            for c0 in range(0, 64, nch):
                pb, b_pb = bank()

                def fn1(e, c0=c0, pb=pb):
                    last = None
                    for i in range(nch):
                        o = pb[:, i * 2 * Na:(i + 1) * 2 * Na]
                        e.matmul(o, GA[0:Na, 0, c0 + i, :], W11, start=True, stop=False)
                        last = e.matmul(o, GA[0:Na, 1, c0 + i, :], W12, start=False, stop=True)
                    return last
                tr.op("pe", fn1, reads=[b_GA, b_cb], writes=[b_pb])
                pv = pb.rearrange("p (c r k) -> p c r k", r=2, k=Na)
                Hr, Hi = pv[:, :, 0, :], pv[:, :, 1, :]
                Trb = Tr.unsqueeze(1).broadcast_to([128, nch, Na])
                Tib = Ti.unsqueeze(1).broadcast_to([128, nch, Na])
                tv = [(t_[0][:, 0:nch * Na].rearrange("p (c k) -> p c k", k=Na), t_[1]) for t_ in tt]
                for (ti_, a_, b_) in ((0, Hr, Trb), (1, Hi, Tib), (2, Hr, Tib), (3, Hi, Trb)):
                    tr.op("dve", lambda e, ti_=ti_, a_=a_, b_=b_, tv=tv: e.tensor_tensor(out=tv[ti_][0], in0=a_, in1=b_, op=ALU.mult),
                          reads=[b_pb, b_cf], writes=[tv[ti_][1]])
                ch0 = chh * 64 + c0
                tr.op("dve", lambda e, ch0=ch0, tv=tv: e.tensor_tensor(
                    out=HB[:, 0, ch0:ch0 + nch, 0:Na], in0=tv[0][0], in1=tv[1][0], op=ALU.subtract),
                    reads=[tv[0][1], tv[1][1]], writes=[b_HB])
                tr.op("dve", lambda e, ch0=ch0, tv=tv: e.tensor_tensor(
                    out=HB[:, 1, ch0:ch0 + nch, 0:Na], in0=tv[2][0], in1=tv[3][0], op=ALU.add),
                    reads=[tv[2][1], tv[3][1]], writes=[b_HB])

        def fft_stage2(seq, cc):
            pr = params(seq)
            Na, Nkb, ntok, off, W2r, W2i = pr["Na"], pr["Nkb"], pr["ntok"], pr["off"], pr["W2r"], pr["W2i"]
            nka = 512 // Nkb
            ystv = yst[:, 0:ntok].rearrange("p (kb ka) -> p kb ka", ka=Na)
            for q_, ka0 in enumerate(range(0, Na, nka)):
                pb, b_pb = bank()

                def fn2(e, ka0=ka0, pb=pb):
                    last = None
                    for i in range(nka):
                        o = pb[:, i * Nkb:(i + 1) * Nkb]
                        e.matmul(o, HB[:, 0, :, ka0 + i], W2r, start=True, stop=False)
                        last = e.matmul(o, HB[:, 1, :, ka0 + i], W2i, start=False, stop=True)
                    return last
                tr.op("pe", fn2, reads=[b_HB, b_cb], writes=[b_pb])
                src = pb.rearrange("p (i k) -> p k i", k=Nkb)
                dst = ystv[:, :, ka0:ka0 + nka]
                if q_ % 2 == 0:
                    tr.op("act", lambda e, src=src, dst=dst: e.activation(out=dst, in_=src, func=AF.Copy), reads=[b_pb], writes=[b_yst])
                else:
                    tr.op("dve", lambda e, src=src, dst=dst: e.tensor_copy(out=dst, in_=src), reads=[b_pb], writes=[b_yst])
            tr.dma("sp", "ph2", YD[cc, :, off:off + ntok], yst[:, 0:ntok], reads=[b_yst], writes=[b_YD])

        units = [(seq, cc, chh) for seq in ("S", "P") for cc in range(2) for chh in range(2)]
        if debug and debug.get("nofft"):
            units = []
        for ui in range(min(2, len(units))):
            fft_load(*units[ui], *GAs[ui % 2])
        for ui, (seq, cc, chh) in enumerate(units):
            fft_stage1(seq, cc, chh, *GAs[ui % 2])
            if ui + 2 < len(units):
                fft_load(*units[ui + 2], *GAs[ui % 2])
            if chh == 1:
                fft_stage2(seq, cc)
        barrier()

    env = make_env(3)
    tiles3 = [("S", j) for j in range(NT_S)] + [("P", j) for j in range(NT_P)]
    if debug and "nt3" in debug:
        tiles3 = tiles3[:debug["nt3"]]
    st3 = [env["make_tile3"](idx, tiles3) for idx in range(len(tiles3))]
    n3 = len(st3)
    if n3:
        run_interleaved([chain(st3[0][0](), st3[0][3]())])
    for i in range(n3):
        side = []
        if i >= 1:
            side.append(st3[i - 1][2]())
        if i + 1 < n3:
            side.append(st3[i + 1][0]())
            side.append(st3[i + 1][3]())
        run_interleaved([st3[i][1](), chain(*side)])
    if n3:
        run_interleaved([st3[n3 - 1][2]()])
    barrier()
    tr.final_wait("sp", DMAKEYS)

    with nc.Block() as block:
        @block.tensor
        def _(e):
            for f in tr.q["pe"]:
                f(e)

        @block.scalar
        def _(e):
            for f in tr.q["act"]:
                f(e)

        @block.vector
        def _(e):
            for f in tr.q["dve"]:
                f(e)

        @block.gpsimd
        def _(e):
            for f in tr.q["pool"]:
                f(e)

        @block.sync
        def _(e):
            for f in tr.q["sp"]:
                f(e)
    return nc, tr, recorded


def _consts(q):
    cbm = np.zeros((128, CB_N), np.float64)
    cfm = np.zeros((128, CF_N), np.float64)
    cbm[:, CB_ONES:CB_ONES + 128] = 1.0 / 1024.0
    j = np.arange(64)
    ang = 2 * np.pi * np.outer(j, j) / 64.0
    for g in range(2):
        cbm[g * 64:(g + 1) * 64, CB_D0 + g * 64:CB_D0 + (g + 1) * 64] = np.cos(ang)
        cbm[g * 64:(g + 1) * 64, CB_D0 + 128 + g * 64:CB_D0 + 128 + (g + 1) * 64] = -np.sin(ang)
    a = np.arange(64)
    ang = 2 * np.pi * np.outer(a, a) / 64.0
    cbm[0:64, CB_W1P1:CB_W1P1 + 64] = np.cos(ang)
    cbm[0:64, CB_W1P1 + 64:CB_W1P1 + 128] = -np.sin(ang)
    cbm[0:64, CB_W1P2:CB_W1P2 + 64] = np.sin(ang)
    cbm[0:64, CB_W1P2 + 64:CB_W1P2 + 128] = np.cos(ang)
    a = np.arange(128)
    ang = 2 * np.pi * np.outer(a, a) / 128.0
    cbm[:, CB_W1S1:CB_W1S1 + 128] = np.cos(ang)
    cbm[:, CB_W1S1 + 128:CB_W1S1 + 256] = -np.sin(ang)
    cbm[:, CB_W1S2:CB_W1S2 + 128] = np.sin(ang)
    cbm[:, CB_W1S2 + 128:CB_W1S2 + 256] = np.cos(ang)
    scP = 1.0 / np.sqrt(64.0 * 8192.0)
    scS = 1.0 / np.sqrt(64.0 * 16384.0)
    b = np.arange(128)
    ang = 2 * np.pi * np.outer(b, np.arange(128)) / 128.0
    cbm[:, CB_W2P:CB_W2P + 128] = np.cos(ang) * scP
    cbm[:, CB_W2P + 128:CB_W2P + 256] = np.sin(ang) * scP
    kb = 32 * q + np.arange(32)
    ang = 2 * np.pi * np.outer(b, kb) / 128.0
    cbm[:, CB_W2S:CB_W2S + 32] = np.cos(ang) * scS
    cbm[:, CB_W2S + 32:CB_W2S + 64] = np.sin(ang) * scS
    cfm[:, CF_ID:CF_ID + 128] = np.eye(128)
    cfm[:, CF_ONES512:CF_ONES512 + 128] = 1.0 / 512.0
    ang = 2 * np.pi * np.outer(b, np.arange(64)) / 8192.0
    cfm[:, CF_TP:CF_TP + 64] = np.cos(ang)
    cfm[:, CF_TP + 64:CF_TP + 128] = -np.sin(ang)
    ang = 2 * np.pi * np.outer(b, np.arange(128)) / 16384.0
    cfm[:, CF_TS:CF_TS + 128] = np.cos(ang)
    cfm[:, CF_TS + 128:CF_TS + 256] = -np.sin(ang)
    cfm[:, CF_MSK] = 0.0 if q == 0 else 1.0
    cfm[:, CF_MSK + 1] = 0.0 if q == 3 else 1.0
    return cbm.astype(ml_dtypes.bfloat16), cfm.astype(np.float32)


_CACHE = {}


def make_in_maps(inputs):
    f = lambda k: np.ascontiguousarray(np.asarray(inputs[k], dtype=np.float32))
    x_prompt, x_sample = f("x_prompt"), f("x_sample")
    c_prompt, c_sample = f("c_prompt"), f("c_sample")
    shared = {
        "ada_w": f("ada_w"), "ada_b": f("ada_b"), "mix_norm_g": f("mix_norm_g"), "ffn_norm_g": f("ffn_norm_g"),
        "ab_w_in": f("ab_w_in")[0], "ab_conv_a": f("ab_conv_a")[0], "ab_conv_b_w": f("ab_conv_b_w")[0],
        "ab_conv_b_b": f("ab_conv_b_b")[0], "ab_ln_g": f("ab_ln_g")[0], "ab_ln_b": f("ab_ln_b")[0],
        "ab_w_out": f("ab_w_out")[0], "cd_w_in": f("cd_w_in")[0], "cd_ln_g": f("cd_ln_g")[0], "cd_ln_b": f("cd_ln_b")[0],
        "cd_w_s": f("cd_w_s")[0], "cd_b_s": f("cd_b_s")[0], "cd_w_out": f("cd_w_out")[0],
        "ffn_w_in": f("ffn_w_in"), "ffn_w_out": f("ffn_w_out"), "final_g": f("final_g"),
    }
    in_maps = []
    for c in range(8):
        bq, q = c // 4, c % 4
        xpp = np.zeros((SP_LEN + 2 * HALO, D), np.float32)
        xpp[HALO:HALO + SP_LEN] = x_prompt[c]
        xsp = np.zeros((SS_LEN + 2 * HALO, D), np.float32)
        lo, hi = q * SS_LEN - HALO, (q + 1) * SS_LEN + HALO
        slo, shi = max(lo, 0), min(hi, 4 * SS_LEN)
        xsp[slo - lo:slo - lo + (shi - slo)] = x_sample[bq, slo:shi]
        cbm, cfm = _consts(q)
        m = dict(shared)
        m.update({"xp": xpp, "xs": xsp, "cvec": np.stack([c_sample[bq], c_prompt[c]], 0), "cb": cbm, "cf": cfm})
        in_maps.append(m)
    return in_maps


def kernel(**inputs):
    if "nc" not in _CACHE:
        specs = build_program()[2]
        _CACHE["nc"] = build_program(plan_specs=specs)[0]
    nc = _CACHE["nc"]
    in_maps = make_in_maps(inputs)
    res = run_bass_kernel_spmd(nc, in_maps, core_ids=list(range(8)))
    y_prompt = np.stack([np.asarray(res.results[c]["yp"], dtype=np.float32) for c in range(8)], 0)
    y_sample = np.stack([
        np.concatenate([np.asarray(res.results[b * 4 + q]["ys"], dtype=np.float32) for q in range(4)], 0)
        for b in range(2)], 0)
    return (y_prompt, y_sample)
```

```python
import numpy as np
import ml_dtypes
from contextlib import ExitStack
import concourse.bass as bass
import concourse.mybir as mybir
from concourse.bass_utils import run_bass_kernel_spmd

F32 = mybir.dt.float32
BF16 = mybir.dt.bfloat16
AF = mybir.ActivationFunctionType
ALU = mybir.AluOpType

D = 1024
KC = 8
T = 512
HALO = 16
TW = T + 2 * HALO
HW_ = TW // 2
SP_LEN = 8192
SS_LEN = 4096
NT_S = SS_LEN // T
NT_P = SP_LEN // T
NTILES = NT_S + NT_P
NTOK = SS_LEN + SP_LEN
DFF = 2816
NFF = DFF // 128
EPS = 1e-6
NSLOT = 8

CB_ONES = 0
CB_D0 = 128
CB_W1P1 = 384
CB_W1P2 = 512
CB_W1S1 = 640
CB_W1S2 = 896
CB_W2P = 1152
CB_W2S = 1408
CB_N = 1472
CF_ID = 0
CF_ONES512 = 128
CF_TP = 256
CF_TS = 384
CF_MSK = 640
CF_N = 642


class Buf:
    __slots__ = ("name", "w", "r")

    def __init__(self, name):
        self.name = name
        self.w = None
        self.r = {}


class Tracker:
    ENGS = ("pe", "act", "dve", "pool", "sp")

    def __init__(self, nc, stack):
        self.nc = nc
        self.stack = stack
        self.q = {e: [] for e in self.ENGS}
        self.cnt = {}
        self.semh = {}
        self.isdma = {}
        self.seen = {e: {} for e in self.ENGS}
        for e in ("pe", "act", "dve", "pool"):
            self.newsem(e, False)
        self.nops = 0

    def newsem(self, key, isdma=True):
        self.semh[key] = self.stack.enter_context(self.nc.semaphore("s_" + key))
        self.cnt[key] = 0
        self.isdma[key] = isdma

    def wait(self, eng, dep):
        if dep is None:
            return
        k, v = dep
        if self.isdma[k]:
            v = self.cnt[k]
        elif k == "pe" and eng == "pe":
            return
        if self.seen[eng].get(k, 0) >= v:
            return
        self.seen[eng][k] = v
        sem = self.semh[k]
        self.q[eng].append(lambda e, sem=sem, v=v: e.wait_ge(sem, v))

    def _deps(self, eng, reads, writes):
        for b in reads:
            self.wait(eng, b.w)
        for b in writes:
            self.wait(eng, b.w)
            for k, v in list(b.r.items()):
                self.wait(eng, (k, v))

    def _post(self, dep, reads, writes):
        k, v = dep
        for b in reads:
            if b.r.get(k, 0) < v:
                b.r[k] = v
        for b in writes:
            b.w = dep
            b.r = {}

    def op(self, eng, fn, reads=(), writes=()):
        self._deps(eng, reads, writes)
        self.cnt[eng] += 1
        v = self.cnt[eng]
        sem = self.semh[eng]
        self.q[eng].append(lambda e, fn=fn, sem=sem: fn(e).then_inc(sem, 1))
        self._post((eng, v), reads, writes)
        self.nops += 1

    def dma(self, eng, key, out, in_, reads=(), writes=(), **kw):
        self._deps(eng, reads, writes)
        self.cnt[key] += 16
        v = self.cnt[key]
        sem = self.semh[key]
        self.q[eng].append(lambda e, out=out, in_=in_, sem=sem, kw=kw: e.dma_start(out=out, in_=in_, **kw).then_inc(sem, 16))
        self._post((key, v), reads, writes)

    def custom(self, eng, key, inc, fn, reads=(), writes=()):
        self._deps(eng, reads, writes)
        self.cnt[key] += inc
        v = self.cnt[key]
        sem = self.semh[key]
        self.q[eng].append(lambda e, fn=fn, sem=sem, inc=inc: fn(e).then_inc(sem, inc))
        self._post((key, v), reads, writes)

    def final_wait(self, eng, keys):
        for k in keys:
            if self.cnt[k] > 0:
                self.wait(eng, (k, self.cnt[k]))


def build_program(debug=None, plan_specs=None):
    nc = bass.Bass("TRN2", target_bir_lowering=False)
    stack = ExitStack()
    tr = Tracker(nc, stack)

    def din(name, shape, dt=F32):
        return nc.dram_tensor(name, list(shape), dt, kind="ExternalInput").ap()

    def dscr(name, shape, dt):
        return nc.dram_tensor(name, list(shape), dt).ap()

    xp = din("xp", [SP_LEN + 2 * HALO, D])
    xs = din("xs", [SS_LEN + 2 * HALO, D])
    cvec = din("cvec", [2, D])
    ada_w = din("ada_w", [2, D, 6 * D])
    ada_b = din("ada_b", [2, 6 * D])
    mix_norm_g = din("mix_norm_g", [2, D])
    ffn_norm_g = din("ffn_norm_g", [2, D])
    ab_w_in = din("ab_w_in", [D, 2560])
    ab_conv_a = din("ab_conv_a", [3, 512])
    ab_conv_b_w = din("ab_conv_b_w", [31, 512])
    ab_conv_b_b = din("ab_conv_b_b", [512])
    ab_ln_g = din("ab_ln_g", [512])
    ab_ln_b = din("ab_ln_b", [512])
    ab_w_out = din("ab_w_out", [D, D])
    cd_w_in = din("cd_w_in", [D, 1792])
    cd_ln_g = din("cd_ln_g", [768])
    cd_ln_b = din("cd_ln_b", [768])
    cd_w_s = din("cd_w_s", [6, 128, 128])
    cd_b_s = din("cd_b_s", [6, 128])
    cd_w_out = din("cd_w_out", [D, D])
    ffn_w_in = din("ffn_w_in", [2, D, 2 * DFF])
    ffn_w_out = din("ffn_w_out", [2, DFF, D])
    final_g = din("final_g", [D])
    cb_d = din("cb", [128, CB_N], BF16)
    cf_d = din("cf", [128, CF_N])
    yp = nc.dram_tensor("yp", [SP_LEN, D], F32, kind="ExternalOutput").ap()
    ys = nc.dram_tensor("ys", [SS_LEN, D], F32, kind="ExternalOutput").ap()

    w_in0_s = dscr("w_in0_s", [20, 128, KC, 128], BF16)
    w_out0_s = dscr("w_out0_s", [8, 128, KC, 128], BF16)
    ffi_s = [dscr("ffi0_s", [44, 128, KC, 128], BF16), dscr("ffi1_s", [44, 128, KC, 128], BF16)]
    ffo_s = [dscr("ffo0_s", [8, 128, NFF, 128], BF16), dscr("ffo1_s", [8, 128, NFF, 128], BF16)]
    cdu_s = dscr("cdu_s", [8, 128, KC, 128], BF16)
    cdv_s = dscr("cdv_s", [KC, 128, 768], BF16)
    w_out1_s = dscr("w_out1_s", [8, 128, KC, 128], BF16)
    X2 = dscr("X2", [NTILES, 128, KC, T], F32)
    YC = dscr("YC", [NTILES, 128, 6, T], BF16)
    G_P = dscr("G_P", [64, 4, 128, 128], BF16)
    G_S = [dscr("G_S%d" % j, [4, 4, 128, 128], BF16) for j in range(NT_S)]
    AG_all = dscr("AG_all", [NT_S, 4, 4, 4, 128, 128], BF16)
    AG = [AG_all[j] for j in range(NT_S)]
    YD = dscr("YD", [2, 128, NTOK], BF16)
    b_X2 = [Buf("X2_%d" % i) for i in range(NTILES)]
    b_YC = [Buf("YC_%d" % i) for i in range(NTILES)]
    b_GP, b_YD = Buf("GP"), Buf("YD")
    b_GS = [Buf("GS%d" % j) for j in range(NT_S)]
    b_AG = [Buf("AG%d" % j) for j in range(NT_S)]

    def sbg(name, shape, dt=F32):
        return nc.alloc_sbuf_tensor(name, list(shape), dt), Buf(name)

    cb, b_cb = sbg("cb_sb", [128, CB_N], BF16)
    cf, b_cf = sbg("cf_sb", [128, CF_N])
    ring = nc.alloc_sbuf_tensor("ring", [128, NSLOT, 1024], BF16)
    b_ring = [Buf("ring%d" % i) for i in range(NSLOT)]
    for i in range(NSLOT):
        tr.newsem("ring%d" % i)
    modr, b_modr = sbg("modr", [128, 2, 48, 2])
    gm, b_gm = sbg("gm", [128, 2, 2, KC, 2])
    adab, b_adab = sbg("adab", [128, 2, 48])
    ngm, b_ngm = sbg("ngm", [128, 2, 2, KC])
    fgm, b_fgm = sbg("fgm", [128, KC])
    cT, b_cT = sbg("cT", [128, KC, 2])
    scT, b_scT = sbg("scT", [128, KC, 2], BF16)
    wA, b_wA = sbg("wA", [128, 3, 4])
    wB, b_wB = sbg("wB", [128, 31, 4])
    bB, b_bB = sbg("bB", [128, 4])
    lngB, b_lngB = sbg("lngB", [128, 4])
    lnbB, b_lnbB = sbg("lnbB", [128, 4])
    lngC, b_lngC = sbg("lngC", [128, 6])
    wsT, b_wsT = sbg("wsT", [128, 6, 128], BF16)
    Cst, b_Cst = sbg("Cst", [128, 6, 128])
    st4s = [sbg("st4_%d" % i, [128, 8]) for i in range(4)]
    epsb, b_epsb = sbg("epsb", [128, 1])
    vres_scope = ExitStack()
    vres, b_vres = vres_scope.enter_context(nc.sbuf_tensor("vres", [128, KC, 768], BF16)), Buf("vres")

    pp = nc.alloc_psum_tensor("pp", [128, 8, 512], F32)
    b_pp = [Buf("bank%d" % i) for i in range(8)]
    bank_ctr = [0]

    def bank():
        i = bank_ctr[0] % 8
        bank_ctr[0] += 1
        return pp[:, i, :], b_pp[i]

    def bankpair():
        if bank_ctr[0] % 2:
            bank_ctr[0] += 1
        i = bank_ctr[0] % 8
        bank_ctr[0] += 2
        return pp[:, i:i + 2, :], [b_pp[i], b_pp[i + 1]]

    DMAKEYS = ["ld", "st", "xl0", "xl1", "xl2", "xl3", "xl4", "castA", "castB", "castC", "castD", "castE", "castF",
               "cc", "g", "ph2", "out", "p3a", "p3b", "vres"]
    for k in DMAKEYS:
        tr.newsem(k)
    DMAKEYS += ["ring%d" % i for i in range(NSLOT)]

    def barrier():
        for e in Tracker.ENGS:
            for k in ("pe", "act", "dve", "pool"):
                if tr.cnt[k] > 0:
                    tr.wait(e, (k, tr.cnt[k]))
            for k in DMAKEYS:
                if tr.cnt[k] > 0 and not k.startswith("cast") and k != "vres":
                    tr.wait(e, (k, tr.cnt[k]))

    ident = cf[:, CF_ID:CF_ID + 128]
    ones_bf = cb[:, CB_ONES:CB_ONES + 128]
    ones512 = cf[:, CF_ONES512:CF_ONES512 + 128]

    def mm_group(out_ap, pairs, reads, writes):
        def fn(e, out_ap=out_ap, pairs=pairs):
            n = len(pairs)
            last = None
            for i, (l, r) in enumerate(pairs):
                last = e.matmul(out_ap, l, r, start=(i == 0), stop=(i == n - 1))
            return last
        tr.op("pe", fn, reads, writes)

    tr.op("dve", lambda e: e.memset(epsb[:], EPS), writes=[b_epsb])
    tr.dma("sp", "ld", cb[:], cb_d, writes=[b_cb])
    tr.dma("sp", "ld", cf[:], cf_d, writes=[b_cf])

    SETUP_T_MARK = None

    with ExitStack() as s0:
        def sbt(name, shape, dt=F32):
            return s0.enter_context(nc.sbuf_tensor(name, list(shape), dt)), Buf(name)
        lnbC, b_lnbC = sbt("lnbC", [128, 768])
        ws_nat, b_wsnat = sbt("ws_nat", [128, 6, 128])
        wsTf, b_wsTf = sbt("wsTf", [128, 6, 128])
        bsr, b_bsr = sbt("bsr", [1, 6, 128])
        ones1, b_ones1 = sbt("ones1", [1, 128])
        tr.dma("sp", "ld", lnbC[:], cd_ln_b.partition_broadcast(128), writes=[b_lnbC])
        tr.dma("sp", "ld", ws_nat[:], cd_w_s.rearrange("h p q -> p h q"), writes=[b_wsnat])
        tr.dma("sp", "ld", bsr[:], cd_b_s.rearrange("(o h) p -> o h p", o=1), writes=[b_bsr])
        tr.op("dve", lambda e: e.memset(ones1[:], 1.0), writes=[b_ones1])
        stg, b_stg = sbt("stg", [48, 1024])

        def load_T(src2d, R, C, dst_of_c, b_dst):
            tr.dma("sp", "ld", stg[0:R, 0:C * 128], src2d, writes=[b_stg])
            for c in range(C):
                pb, b_pb = bank()
                tr.op("pe", lambda e, c=c, pb=pb, R=R: e.transpose(pb[:, 0:R], stg[0:R, c * 128:(c + 1) * 128], ident[0:R, 0:R]),
                      reads=[b_stg, b_cf], writes=[b_pb])
                tr.op("dve", lambda e, c=c, pb=pb, R=R: e.tensor_copy(out=dst_of_c(c), in_=pb[:, 0:R]), reads=[b_pb], writes=[b_dst])

        load_T(cvec, 2, 8, lambda c: cT[:, c, :], b_cT)
        for l in range(2):
            load_T(ada_b[l].rearrange("(n p) -> n p", p=128), 48, 1, lambda c, l=l: adab[:, l, :], b_adab)
            load_T(mix_norm_g[l].rearrange("(k p) -> k p", p=128), 8, 1, lambda c, l=l: ngm[:, l, 0, :], b_ngm)
            load_T(ffn_norm_g[l].rearrange("(k p) -> k p", p=128), 8, 1, lambda c, l=l: ngm[:, l, 1, :], b_ngm)
        load_T(final_g.rearrange("(k p) -> k p", p=128), 8, 1, lambda c: fgm[:], b_fgm)
        load_T(ab_conv_a, 3, 4, lambda c: wA[:, :, c], b_wA)
        load_T(ab_conv_b_w, 31, 4, lambda c: wB[:, :, c], b_wB)
        tr.op("dve", lambda e: e.tensor_scalar(out=wB[:], in0=wB[:], scalar1=0.5, scalar2=None, op0=ALU.mult), reads=[b_wB], writes=[b_wB])
        load_T(ab_conv_b_b.rearrange("(g p) -> g p", p=128), 4, 1, lambda c: bB[:], b_bB)
        load_T(ab_ln_g.rearrange("(g p) -> g p", p=128), 4, 1, lambda c: lngB[:], b_lngB)
        load_T(ab_ln_b.rearrange("(g p) -> g p", p=128), 4, 1, lambda c: lnbB[:], b_lnbB)
        load_T(cd_ln_g.rearrange("(g p) -> g p", p=128), 6, 1, lambda c: lngC[:], b_lngC)
        for h in range(6):
            pb, b_pb = bank()
            tr.op("pe", lambda e, h=h, pb=pb: e.transpose(pb[:, 0:128], ws_nat[:, h, :], ident),
                  reads=[b_wsnat, b_cf], writes=[b_pb])
            tr.op("dve", lambda e, h=h, pb=pb: e.tensor_copy(out=wsT[:, h, :], in_=pb[:, 0:128]), reads=[b_pb], writes=[b_wsT])
            tr.op("act", lambda e, h=h, pb=pb: e.activation(out=wsTf[:, h, :], in_=pb[:, 0:128], func=AF.Copy), reads=[b_pb], writes=[b_wsTf])
        for h in range(6):
            pb, b_pb = bank()
            mm_group(pb[:, 0:128], [(lnbC[:, h * 128:(h + 1) * 128], wsTf[:, h, :]), (ones1[:], bsr[:, h, :])],
                     reads=[b_lnbC, b_wsTf, b_ones1, b_bsr], writes=[b_pb])
            tr.op("dve", lambda e, h=h, pb=pb: e.tensor_copy(out=Cst[:, h, :], in_=pb[:, 0:128]), reads=[b_pb], writes=[b_Cst])
        barrier()


    b_cast = {k: Buf(k) for k in ("castA", "castB", "castC", "castD", "castE", "castF")}

    def cast_units(key, dst, src, ncols0, nunits):
        for u in range(nunits):
            c0 = ncols0 + u * 128
            tr.dma("pool", key, dst[u], src[:, c0:c0 + 128].rearrange("(k p) n -> p k n", p=128), writes=[b_cast[key]])

    cast_units("castA", w_in0_s, ab_w_in, 0, 20)
    cast_units("castB", w_out0_s, ab_w_out, 0, 8)
    cast_units("castC", ffi_s[0], ffn_w_in[0], 0, 44)
    cast_units("castD", ffo_s[0], ffn_w_out[0], 0, 8)
    cast_units("castE", cdu_s[0:6], cd_w_in, 0, 6)
    cast_units("castE", cdu_s[6:8], cd_w_in, 1536, 2)
    for k in range(KC):
        tr.dma("pool", "castE", cdv_s[k], cd_w_in[k * 128:(k + 1) * 128, 768:1536], writes=[b_cast["castE"]])
    tr.dma("pool", "vres", vres[:], cdv_s.rearrange("k p n -> p k n"), reads=[b_cast["castE"]], writes=[b_vres])
    cast_units("castF", w_out1_s, cd_w_out, 0, 8)
    cast_units("castF", ffi_s[1], ffn_w_in[1], 0, 44)
    cast_units("castF", ffo_s[1], ffn_w_out[1], 0, 8)

    def resolve(spec):
        kind = spec[0]
        if kind == "w_in0":
            return w_in0_s[spec[1]], b_cast["castA"]
        if kind == "w_out0":
            return w_out0_s[spec[1]], b_cast["castB"]
        if kind == "ffi":
            return ffi_s[spec[1]][spec[2]], b_cast["castC" if spec[1] == 0 else "castF"]
        if kind == "ffo":
            return ffo_s[spec[1]][spec[2]][:, spec[3]:spec[4], :], b_cast["castD" if spec[1] == 0 else "castF"]
        if kind == "cdu":
            return cdu_s[spec[1]], b_cast["castE"]
        if kind == "cdv":
            return cdv_s[spec[1]], b_cast["castE"]
        if kind == "w_out1":
            return w_out1_s[spec[1]], b_cast["castF"]
        raise ValueError(spec)

    recorded = []
    ring_state = {"issued": 0, "used": 0}

    def ring_issue():
        if plan_specs is None:
            return
        u = ring_state["issued"]
        if u >= len(plan_specs):
            return
        ring_state["issued"] += 1
        src, cbuf = resolve(plan_specs[u])
        s_ = u % NSLOT
        shp = list(src.shape)
        n = 1
        for d_ in shp[1:]:
            n *= d_
        dst = ring[:, s_, 0:n]
        if len(shp) == 3:
            dst = dst.rearrange("p (k n) -> p k n", n=shp[2])
        tr.dma("sp", "ring%d" % s_, dst, src, reads=[cbuf], writes=[b_ring[s_]])

    def ring_next(spec):
        u = ring_state["used"]
        ring_state["used"] += 1
        recorded.append(spec)
        if plan_specs is not None:
            assert plan_specs[u] == spec, (u, plan_specs[u], spec)
        s_ = u % NSLOT
        return ring[:, s_, :], b_ring[s_]

    def ring_done():
        ring_issue()

    tr.op("act", lambda e: e.activation(out=scT[:], in_=cT[:], func=AF.Silu), reads=[b_cT], writes=[b_scT])
    with ExitStack() as s1:
        adaf = [(s1.enter_context(nc.sbuf_tensor("adaf%d" % i, [128, KC, 512], F32)), Buf("adaf%d" % i)) for i in range(2)]
        adab16 = [(s1.enter_context(nc.sbuf_tensor("adab16_%d" % i, [128, KC, 512], BF16)), Buf("adab16_%d" % i)) for i in range(2)]
        for i in range(2):
            tr.newsem("ada%d" % i)
            DMAKEYS.append("ada%d" % i)
        for l in range(2):
            pb, b_pb = bank()
            for nb_ in range(12):
                i = (l * 12 + nb_) % 2
                af, b_af = adaf[i]
                a16, b_a16 = adab16[i]
                tr.dma("sp", "ada%d" % i, af[:], ada_w[l][:, nb_ * 512:(nb_ + 1) * 512].rearrange("(k p) n -> p k n", p=128), writes=[b_af])
                tr.op("act", lambda e, af=af, a16=a16: e.activation(out=a16[:, 0:4], in_=af[:, 0:4], func=AF.Copy), reads=[b_af], writes=[b_a16])
                tr.op("dve", lambda e, af=af, a16=a16: e.tensor_copy(out=a16[:, 4:8], in_=af[:, 4:8]), reads=[b_af], writes=[b_a16])
                for q_ in range(4):
                    n = nb_ * 4 + q_
                    mm_group(pb[:, 2 * n:2 * n + 2], [(a16[:, k, q_ * 128:(q_ + 1) * 128], scT[:, k, :]) for k in range(KC)],
                             reads=[b_a16, b_scT], writes=[b_pb])
            tr.op("dve", lambda e, l=l, pb=pb: e.tensor_tensor(
                out=modr[:, l], in0=pb[:, 0:96].rearrange("p (n s) -> p n s", s=2),
                in1=adab[:, l, :].unsqueeze(2).broadcast_to([128, 48, 2]), op=ALU.add),
                reads=[b_pb, b_adab], writes=[b_modr])
        barrier()
    for _ in range(NSLOT):
        ring_issue()

    for l in range(2):
        for wch, kind in ((0, 1), (1, 4)):
            tr.op("dve", lambda e, l=l, wch=wch, kind=kind: e.scalar_tensor_tensor(
                out=gm[:, l, wch], in0=modr[:, l, kind * 8:(kind + 1) * 8, :], scalar=1.0,
                in1=ngm[:, l, wch, :].unsqueeze(2).broadcast_to([128, KC, 2]), op0=ALU.add, op1=ALU.mult),
                reads=[b_modr, b_ngm], writes=[b_gm])

    def MOD(l, kind, kc, si):
        return modr[:, l, kind * 8 + kc, si:si + 1]

    def GM(l, wch, kc, si):
        return gm[:, l, wch, kc, si:si + 1]

    def make_env(phase):
        ph = ExitStack()
        P1 = (phase == 1)

        def sbp(name, shape, dt=F32):
            return ph.enter_context(nc.sbuf_tensor("%s_p%d" % (name, phase), list(shape), dt)), Buf(name)

        xTa, b_xTa = sbp("xTa", [128, KC, TW])
        sq, b_sq = sbp("sq", [128, KC, TW], BF16)
        hTa, b_hTa = sbp("hTa", [128, KC, TW], BF16)
        hTb, b_hTb = sbp("hTb", [128, KC, TW], BF16)
        rstd, b_rstd = sbp("rstd", [128, TW])
        hid, b_hid = sbp("hid", [128, NFF, T], BF16)
        tmpg = [sbp("tmpg%d" % i, [128, T]) for i in range(2)]
        xTb, b_xTb = sbp("xTb", [128, KC, TW])
        nts = [sbp("nt%d" % i, [128, TW], BF16) for i in range(2)]
        if P1:
            xtok = [sbp("xtok%d" % i, [128, D]) for i in range(3)]
            mix, _ = sbp("mix", [128, 9984])
            cz, b_cz = sbp("cz", [128, 8, T])
            zsq, b_zsq = sbp("zsq", [128, 4, T])
        else:
            otok = [sbp("otok%d" % i, [128, 4, D]) for i in range(2)]
            yc3 = [sbp("yc3_%d" % i, [128, 6, T], BF16) for i in range(2)]
            ydt = [sbp("ydt%d" % i, [128, 2, T], BF16) for i in range(2)]
            mix, _ = sbp("mix", [128, 16])
            tbuf3, b_tbuf3 = sbp("tbuf3", [128, KC, TW])
            sq2, b_sq2 = sbp("sq2", [128, KC, TW], BF16)
            rstd2, b_rstd2 = sbp("rstd2", [128, TW])
        if P1:
            tbuf, b_tbuf = None, None
        else:
            tbuf, b_tbuf = tbuf3, b_tbuf3

        if P1:
            def mixv(a, b, dt=F32):
                v = mix[:, a:b]
                return v.bitcast(BF16) if dt is BF16 else v
            ab_t, b_ab = mixv(0, 2048).rearrange("p (g t) -> p g t", t=T), Buf("ab")
            ca, b_ca = mixv(2048, 3136, BF16).rearrange("p (g t) -> p g t", t=TW), Buf("ca")
            gB, b_gB = mixv(3136, 4224, BF16).rearrange("p (g t) -> p g t", t=TW), Buf("gB")
            tmpc = [(mixv(4224 + i * 544, 4768 + i * 544).rearrange("p (h c) -> p h c", c=HW_), Buf("tmpc%d" % i)) for i in range(2)]
            tmps = [(mixv(5312 + i * 544, 5856 + i * 544).rearrange("p (h c) -> p h c", c=HW_), Buf("tmps%d" % i)) for i in range(2)]
            lnt = [(mixv(6400 + i * 512, 6912 + i * 512), Buf("lnt%d" % i)) for i in range(3)]
            yT, b_yT = mixv(7936, 9984, BF16).rearrange("p (k t) -> p k t", t=T), Buf("yT")
            czf = cz[:].rearrange("p k t -> p (k t)")
            zsqf = zsq[:].rearrange("p g t -> p (g t)")
            uT, b_uT = czf[:, 0:3072].rearrange("p (k t) -> p k t", t=T), [b_cz]
            Gst, b_Gst = czf[:, 3072:4096].bitcast(BF16).rearrange("p (r t) -> p r t", t=T), [b_cz]
            ycT, b_ycT = zsqf[:, 0:1536].bitcast(BF16).rearrange("p (k t) -> p k t", t=T), [b_zsq]
            fT, b_fT = zsqf[:, 1536:2048].bitcast(BF16).rearrange("p (c t) -> p c t", t=T), [b_zsq]
            vfs = [(mixv(7936, 8704), [b_yT]), (mixv(7936, 8704), [b_yT])]
            t6, b_t6 = mixv(8704, 9472).rearrange("p (h c) -> p h c", c=128), [b_yT]
            vn = [(mixv(9472, 9856, BF16), [b_yT]), (mixv(7552, 7936, BF16), [lnt[2][1]])]
            vsq, b_vsq = mixv(6400, 7168), [lnt[0][1], lnt[1][1]]

        def rms_to_h(xT, b_xT, l, wch, si, c0, ncols, plain_gain=None, hT=None, b_hT=None, yielding=True, scr=None):
            sq_, b_sq_, rstd_, b_rstd_ = scr if scr is not None else (sq, b_sq, rstd, b_rstd)
            c1 = c0 + ncols
            tr.op("act", lambda e: e.activation(out=sq_[:, :, c0:c1], in_=xT[:, :, c0:c1], func=AF.Square),
                  reads=[b_xT], writes=[b_sq_])
            if yielding:
                yield 4.0
            pieces = [(c0, c1)] if ncols <= 512 else [(c0, c0 + ncols // 2), (c0 + ncols // 2, c1)]
            for (a0, a1) in pieces:
                pb, b_pb = bank()
                mm_group(pb[:, 0:a1 - a0], [(ones_bf, sq_[:, k, a0:a1]) for k in range(KC)], reads=[b_sq_, b_cb], writes=[b_pb])
                tr.op("act", lambda e, pb=pb, a0=a0, a1=a1: e.activation(
                    out=rstd_[:, a0:a1], in_=pb[:, 0:a1 - a0], func=AF.Ln, bias=epsb[:, 0:1], scale=1.0),
                    reads=[b_pb, b_epsb], writes=[b_rstd_])
                tr.op("act", lambda e, a0=a0, a1=a1: e.activation(
                    out=rstd_[:, a0:a1], in_=rstd_[:, a0:a1], func=AF.Exp, scale=-0.5),
                    reads=[b_rstd_], writes=[b_rstd_])
            if yielding:
                yield 3.0
            if plain_gain is not None:
                for k in range(KC):
                    tr.op("dve", lambda e, k=k: e.scalar_tensor_tensor(
                        out=tbuf[:, k, c0:c1], in0=xT[:, k, c0:c1], scalar=plain_gain[:, k:k + 1], in1=rstd_[:, c0:c1],
                        op0=ALU.mult, op1=ALU.mult), reads=[b_xT, b_rstd_, b_fgm], writes=[b_tbuf])
                if yielding:
                    yield 5.0
                return
            kind_sh = 0 if wch == 0 else 3
            for k in range(KC):
                nt_, b_nt = nts[k % 2]
                tr.op("dve", lambda e, k=k, nt_=nt_: e.scalar_tensor_tensor(
                    out=nt_[:, c0:c1], in0=xT[:, k, c0:c1], scalar=GM(l, wch, k, si), in1=rstd_[:, c0:c1],
                    op0=ALU.mult, op1=ALU.mult), reads=[b_xT, b_rstd_, b_gm], writes=[b_nt])
                tr.op("act", lambda e, k=k, nt_=nt_: e.activation(
                    out=hT[:, k, c0:c1], in_=nt_[:, c0:c1], func=AF.Identity, bias=MOD(l, kind_sh, k, si), scale=1.0),
                    reads=[b_nt, b_modr], writes=[b_hT])
            if yielding:
                yield 5.0

        hsl = slice(HALO, HALO + T)

        def ffn(xT, b_xT, l, si, hT, b_hT, do_norm=True):
            if do_norm:
                yield from rms_to_h(xT, b_xT, l, 1, si, HALO, T, hT=hT, b_hT=b_hT)
            for j in range(NFF):
                slot, b_slot = ring_next(("ffi", l, j))
                w = slot.rearrange("p (k n) -> p k n", n=128)
                pg, b_pg = bank()
                mm_group(pg, [(w[:, k, :], hT[:, k, hsl]) for k in range(KC)], reads=[b_slot, b_hT], writes=[b_pg])
                ring_done()
                tg, b_tg = tmpg[j % 2]
                tr.op("act", lambda e, pg=pg, tg=tg: e.activation(out=tg[:], in_=pg, func=AF.Silu), reads=[b_pg], writes=[b_tg])
                slot, b_slot = ring_next(("ffi", l, NFF + j))
                w = slot.rearrange("p (k n) -> p k n", n=128)
                pu, b_pu = bank()
                mm_group(pu, [(w[:, k, :], hT[:, k, hsl]) for k in range(KC)], reads=[b_slot, b_hT], writes=[b_pu])
                ring_done()
                tr.op("dve", lambda e, pu=pu, tg=tg, j=j: e.tensor_tensor(out=hid[:, j, :], in0=pu, in1=tg[:], op=ALU.mult),
                      reads=[b_pu, b_tg], writes=[b_hid])
                yield 3.0
            for n in range(8):
                po, b_po = bank()
                for (k0, k1) in ((0, 8), (8, 16), (16, 22)):
                    slot, b_slot = ring_next(("ffo", l, n, k0, k1))
                    w = slot[:, 0:(k1 - k0) * 128].rearrange("p (k n) -> p k n", n=128)

                    def fn(e, w=w, po=po, k0=k0, k1=k1):
                        last = None
                        for k in range(k0, k1):
                            last = e.matmul(po, w[:, k - k0, :], hid[:, k, :], start=(k == 0), stop=(k == NFF - 1))
                        return last
                    tr.op("pe", fn, reads=[b_slot, b_hid], writes=[b_po])
                    ring_done()
                tr.op("dve", lambda e, po=po, n=n: e.scalar_tensor_tensor(
                    out=xT[:, n, hsl], in0=po, scalar=MOD(l, 5, n, si), in1=xT[:, n, hsl], op0=ALU.mult, op1=ALU.add),
                    reads=[b_po, b_xT, b_modr], writes=[b_xT])
                yield 4.2

        xl_ctr = [0]

        def load_x_tile(src, t0):
            nblk = [(t0 + r * 128, 128) for r in range(4)] + [(t0 + 512, 32)]
            staged = []
            for (r0, nr) in nblk:
                i = xl_ctr[0] % 3
                xl_ctr[0] += 1
                xt, b_xt = xtok[i]
                tr.dma("sp", "xl%d" % i, xt[0:nr, :], src[r0:r0 + nr, :], writes=[b_xt])
                staged.append((xt, b_xt, nr))
            return staged

        x_prefetched = {}

        def transpose_x_tile(xT, b_xT, src, t0, nxt=None):
            nblk = [(t0 + r * 128, 128) for r in range(4)] + [(t0 + 512, 32)]
            cnt = 0
            stg_ = {}

            def load(r, src_=src, nblk_=nblk, dst_=stg_):
                r0, nr = nblk_[r]
                i = xl_ctr[0] % 3
                xl_ctr[0] += 1
                xt, b_xt = xtok[i]
                tr.dma("sp", "xl%d" % i, xt[0:nr, :], src_[r0:r0 + nr, :], writes=[b_xt])
                dst_[r] = (xt, b_xt)
            key = (id(src), t0)
            if key in x_prefetched:
                stg_.update(x_prefetched.pop(key))
            else:
                for r in range(3):
                    load(r)
            yield 2.0
            for r in range(5):
                xt, b_xt = stg_[r]
                if r < 4:
                    for h in range(2):
                        pb, b_pb = bank()

                        def fn(e, h=h, pb=pb, xt=xt):
                            last = None
                            for kk in range(4):
                                k = 4 * h + kk
                                last = e.transpose(pb[:, kk * 128:(kk + 1) * 128], xt[:, k * 128:(k + 1) * 128], ident)
                            return last
                        tr.op("pe", fn, reads=[b_xt, b_cf], writes=[b_pb])
                        dst = xT[:, 4 * h:4 * h + 4, r * 128:(r + 1) * 128]
                        srcv = pb.rearrange("p (k c) -> p k c", c=128)
                        if cnt % 2:
                            tr.op("act", lambda e, dst=dst, srcv=srcv: e.activation(out=dst, in_=srcv, func=AF.Copy), reads=[b_pb], writes=[b_xT])
                        else:
                            tr.op("dve", lambda e, dst=dst, srcv=srcv: e.tensor_copy(out=dst, in_=srcv), reads=[b_pb], writes=[b_xT])
                        cnt += 1
                else:
                    pb, b_pb = bank()

                    def fn2(e, pb=pb, xt=xt):
                        last = None
                        for k in range(KC):
                            last = e.transpose(pb[:, k * 32:(k + 1) * 32], xt[0:32, k * 128:(k + 1) * 128], ident[0:32, 0:32])
                        return last
                    tr.op("pe", fn2, reads=[b_xt, b_cf], writes=[b_pb])
                    tr.op("dve", lambda e, pb=pb: e.tensor_copy(out=xT[:, :, 512:544], in_=pb[:, 0:256].rearrange("p (k c) -> p k c", c=32)),
                          reads=[b_pb], writes=[b_xT])
                if r + 3 < 5:
                    load(r + 3)
                yield 1.8
            if nxt is not None:
                src2, t02 = nxt
                nblk2 = [(t02 + r * 128, 128) for r in range(3)]
                pf = {}
                for r in range(3):
                    load(r, src2, nblk2, pf)
                x_prefetched[(id(src2), t02)] = pf

        msk = cf[:, CF_MSK:CF_MSK + 2]

        def make_tile(seq, j, xT, b_xT, nxt_tile=None):
          nxt = None if nxt_tile is None else ((xs if nxt_tile[0] == "S" else xp), nxt_tile[1] * T)
          si = 0 if seq == "S" else 1
          gi = j if seq == "S" else NT_S + j
          ntl = NT_S if seq == "S" else NT_P
          src = xs if seq == "S" else xp

          def stageA():
            hT, b_hT = hTa, b_hTa
            yield from transpose_x_tile(xT, b_xT, src, j * T, nxt)
            yield from rms_to_h(xT, b_xT, 0, 0, si, 0, TW, hT=hT, b_hT=b_hT)
            for g in range(4):
                def conv_chunk(cidx):
                    slot, b_slot = ring_next(("w_in0", cidx))
                    w = slot.rearrange("p (k n) -> p k n", n=128)
                    pr, b_pr = bankpair()

                    def fn(e, w=w, pr=pr):
                        last = None
                        for h in range(2):
                            for k in range(KC):
                                last = e.matmul(pr[:, h, 0:HW_], w[:, k, :], hT[:, k, h * HW_:(h + 1) * HW_],
                                                start=(k == 0), stop=(k == KC - 1))
                        return last
                    tr.op("pe", fn, reads=[b_slot, b_hT], writes=b_pr)
                    ring_done()
                    return pr[:, :, 0:HW_], b_pr
                tc_, b_tc = tmpc[g % 2]
                ts_, b_ts = tmps[g % 2]
                pv, b_pv = conv_chunk(4 + g)
                tr.op("act", lambda e, pv=pv, tc_=tc_: e.activation(out=tc_, in_=pv, func=AF.Copy), reads=b_pv, writes=[b_tc])
                pv, b_pv = conv_chunk(8 + g)
                tr.op("dve", lambda e, pv=pv, tc_=tc_, g=g: e.tensor_tensor(
                    out=ca[:, g, :].rearrange("p (h c) -> p h c", c=HW_), in0=pv, in1=tc_, op=ALU.mult),
                    reads=b_pv + [b_tc], writes=[b_ca])
                pv, b_pv = conv_chunk(16 + g)
                tr.op("act", lambda e, pv=pv, ts_=ts_: e.activation(out=ts_, in_=pv, func=AF.Tanh, scale=0.5), reads=b_pv, writes=[b_ts])
                pv, b_pv = conv_chunk(12 + g)
                tr.op("dve", lambda e, pv=pv, ts_=ts_, g=g: e.scalar_tensor_tensor(
                    out=gB[:, g, :].rearrange("p (h c) -> p h c", c=HW_), in0=ts_, scalar=1.0, in1=pv, op0=ALU.add, op1=ALU.mult),
                    reads=b_pv + [b_ts], writes=[b_gB])
                slot, b_slot = ring_next(("w_in0", g))
                w = slot.rearrange("p (k n) -> p k n", n=128)
                pb, b_pb = bank()
                mm_group(pb, [(w[:, k, :], hT[:, k, hsl]) for k in range(KC)], reads=[b_slot, b_hT], writes=[b_pb])
                ring_done()
                tr.op("act", lambda e, pb=pb, g=g: e.activation(out=ab_t[:, g, :], in_=pb, func=AF.Copy), reads=[b_pb], writes=[b_ab])
                yield 11.0
            for (cond_first, c0) in ((True, 0), (False, HALO + T)):
                is_edge = (j == 0) if cond_first else (j == ntl - 1)
                if not is_edge:
                    continue
                for (buf, b_b) in ((ca, b_ca), (gB, b_gB)):
                    if seq == "P":
                        tr.op("dve", lambda e, buf=buf, c0=c0: e.memset(buf[:, :, c0:c0 + HALO], 0.0), writes=[b_b])
                    else:
                        mcol = 0 if cond_first else 1
                        tr.op("dve", lambda e, buf=buf, c0=c0, mcol=mcol: e.tensor_scalar(
                            out=buf[:, :, c0:c0 + HALO], in0=buf[:, :, c0:c0 + HALO], scalar1=msk[:, mcol:mcol + 1], scalar2=None,
                            op0=ALU.mult), reads=[b_b, b_cf], writes=[b_b])
            yield 0.5

          def stageB():
            acc = cz[:, 0:4, :]
            z = cz[:, 4:8, :]
            for g in range(4):
                tr.op("dve", lambda e, g=g: e.tensor_scalar(
                    out=acc[:, g, :], in0=ca[:, g, HALO - 1:HALO - 1 + T], scalar1=wA[:, 0, g:g + 1], scalar2=None, op0=ALU.mult),
                    reads=[b_ca, b_wA], writes=[b_cz])
                for kk in (1, 2):
                    tr.op("dve", lambda e, g=g, kk=kk: e.scalar_tensor_tensor(
                        out=acc[:, g, :], in0=ca[:, g, HALO - 1 + kk:HALO - 1 + kk + T], scalar=wA[:, kk, g:g + 1], in1=acc[:, g, :],
                        op0=ALU.mult, op1=ALU.add), reads=[b_ca, b_wA, b_cz], writes=[b_cz])
                tr.op("dve", lambda e, g=g: e.tensor_tensor(out=yT[:, g, :], in0=acc[:, g, :], in1=ab_t[:, g, :], op=ALU.mult),
                      reads=[b_cz, b_ab], writes=[b_yT])
                yield 2.9
            for g in range(4):
                tr.op("dve", lambda e, g=g: e.tensor_scalar(
                    out=z[:, g, :], in0=gB[:, g, 1:1 + T], scalar1=wB[:, 0, g:g + 1], scalar2=bB[:, g:g + 1], op0=ALU.mult, op1=ALU.add),
                    reads=[b_gB, b_wB, b_bB], writes=[b_cz])
            for kk in range(1, 31):
                for g in range(4):
                    tr.op("dve", lambda e, g=g, kk=kk: e.scalar_tensor_tensor(
                        out=z[:, g, :], in0=gB[:, g, 1 + kk:1 + kk + T], scalar=wB[:, kk, g:g + 1], in1=z[:, g, :],
                        op0=ALU.mult, op1=ALU.add), reads=[b_gB, b_wB, b_cz], writes=[b_cz])
                yield 2.9
            z16 = zsqf[:, 0:1024].bitcast(BF16).rearrange("p (g t) -> p g t", t=T)
            zq16 = zsqf[:, 1024:2048].bitcast(BF16).rearrange("p (g t) -> p g t", t=T)
            tr.op("act", lambda e: e.activation(out=zq16, in_=z, func=AF.Square), reads=[b_cz], writes=[b_zsq])
            tr.op("act", lambda e: e.activation(out=z16, in_=z, func=AF.Copy), reads=[b_cz], writes=[b_zsq])
            p1, b_p1 = bank()
            mm_group(p1, [(ones_bf, z16[:, g, :]) for g in range(4)], reads=[b_zsq, b_cb], writes=[b_p1])
            p2, b_p2 = bank()
            mm_group(p2, [(ones_bf, zq16[:, g, :]) for g in range(4)], reads=[b_zsq, b_cb], writes=[b_p2])
            yield 1.0
            mean, b_mean = lnt[0]
            m2, b_m2 = lnt[1]
            lrs, b_lrs = lnt[2]
            tr.op("act", lambda e: e.activation(out=mean, in_=p1, func=AF.Copy, scale=2.0), reads=[b_p1], writes=[b_mean])
            tr.op("dve", lambda e: e.tensor_tensor(out=m2, in0=mean, in1=mean, op=ALU.mult), reads=[b_mean], writes=[b_m2])
            tr.op("dve", lambda e: e.scalar_tensor_tensor(out=m2, in0=p2, scalar=2.0, in1=m2, op0=ALU.mult, op1=ALU.subtract),
                  reads=[b_p2, b_m2], writes=[b_m2])
            tr.op("act", lambda e: e.activation(out=lrs, in_=m2, func=AF.Ln, bias=epsb[:, 0:1], scale=1.0),
                  reads=[b_m2, b_epsb], writes=[b_lrs])
            tr.op("act", lambda e: e.activation(out=lrs, in_=lrs, func=AF.Exp, scale=-0.5), reads=[b_lrs], writes=[b_lrs])
            yield 3.0
            for g in range(4):
                tr.op("dve", lambda e, g=g: e.tensor_tensor(out=z[:, g, :], in0=z[:, g, :], in1=mean, op=ALU.subtract),
                      reads=[b_cz, b_mean], writes=[b_cz])
                tr.op("dve", lambda e, g=g: e.tensor_tensor(out=z[:, g, :], in0=z[:, g, :], in1=lrs, op=ALU.mult),
                      reads=[b_cz, b_lrs], writes=[b_cz])
                tr.op("act", lambda e, g=g: e.activation(out=yT[:, 4 + g, :], in_=z[:, g, :], func=AF.Silu,
                                                         bias=lnbB[:, g:g + 1], scale=lngB[:, g:g + 1]),
                      reads=[b_cz, b_lngB, b_lnbB], writes=[b_yT])
                yield 2.0
            for n in range(8):
                slot, b_slot = ring_next(("w_out0", n))
                w = slot.rearrange("p (k n) -> p k n", n=128)
                po, b_po = bank()
                mm_group(po, [(w[:, k, :], yT[:, k, :]) for k in range(KC)], reads=[b_slot, b_yT], writes=[b_po])
                ring_done()
                tr.op("dve", lambda e, po=po, n=n: e.scalar_tensor_tensor(
                    out=xT[:, n, hsl], in0=po, scalar=MOD(0, 2, n, si), in1=xT[:, n, hsl], op0=ALU.mult, op1=ALU.add),
                    reads=[b_po, b_xT, b_modr], writes=[b_xT])
                yield 1.8

          def stageN2():
            yield from rms_to_h(xT, b_xT, 0, 1, si, HALO, T, hT=hTa, b_hT=b_hTa)

          def stageC():
            yield from ffn(xT, b_xT, 0, si, hTa, b_hTa, do_norm=False)

          def stageN3():
            tr.dma("sp", "st", X2[gi], xT[:, :, hsl], reads=[b_xT], writes=[b_X2[gi]])
            yield from rms_to_h(xT, b_xT, 1, 0, si, HALO, T, hT=hTb, b_hT=b_hTb)

          def stageD():
            hT, b_hT = hTb, b_hTb
            for n in range(8):
                slot, b_slot = ring_next(("cdu", n))
                w = slot.rearrange("p (k n) -> p k n", n=128)
                pb, b_pb = bank()
                mm_group(pb, [(w[:, k, :], hT[:, k, hsl]) for k in range(KC)], reads=[b_slot, b_hT], writes=[b_pb])
                ring_done()
                if n < 6:
                    tr.op("act", lambda e, pb=pb, n=n: e.activation(out=uT[:, n, :], in_=pb, func=AF.Copy), reads=[b_pb], writes=b_uT)
                else:
                    tr.op("dve", lambda e, pb=pb, n=n: e.tensor_copy(out=fT[:, n - 6, :], in_=pb), reads=[b_pb], writes=b_fT)
            for cc in range(2):
                for ri in range(2):
                    pb, b_pb = bank()
                    mm_group(pb, [(cb[:, CB_D0 + ri * 128:CB_D0 + (ri + 1) * 128], fT[:, cc, :])], reads=[b_cb] + b_fT, writes=[b_pb])
                    pl = cc * 2 + ri
                    if ri:
                        tr.op("act", lambda e, pb=pb, pl=pl: e.activation(out=Gst[:, pl, :], in_=pb, func=AF.Copy), reads=[b_pb], writes=b_Gst)
                    else:
                        tr.op("dve", lambda e, pb=pb, pl=pl: e.tensor_copy(out=Gst[:, pl, :], in_=pb), reads=[b_pb], writes=b_Gst)
            for r_ in range(4):
                if seq == "P":
                    gdst, b_g = G_P[4 * j:4 * j + 4, r_].rearrange("a c b -> c a b"), b_GP
                else:
                    gdst, b_g = G_S[j][:, r_].rearrange("a c b -> c a b"), b_GS[j]
                tr.dma("sp", "g", gdst, Gst[:, r_, :].rearrange("c (a b) -> c a b", b=128), reads=b_Gst, writes=[b_g])
            yield 14.0
            for r in range(4):
                pr, b_pr = bankpair()
                prf = pr.rearrange("p b c -> p (b c)")

                def fnv(e, r=r, pr=pr):
                    last = None
                    for k in range(KC):
                        lh = hT[:, k, HALO + r * 128:HALO + (r + 1) * 128]
                        e.matmul(pr[:, 0, :], lh, vres[:, k, 0:512], start=(k == 0), stop=(k == KC - 1))
                        last = e.matmul(pr[:, 1, 0:256], lh, vres[:, k, 512:768], start=(k == 0), stop=(k == KC - 1))
                    return last
                tr.op("pe", fnv, reads=[b_hT, b_vres], writes=b_pr)
                vf, b_vf = vfs[r % 2]
                st4, b_st4 = st4s[r]
                tr.op("act", lambda e, prf=prf: e.activation(out=vf, in_=prf[:, 0:768], func=AF.Copy), reads=b_pr, writes=b_vf)
                tr.op("dve", lambda e: e.tensor_reduce(out=st4[:, 0:1], in_=vf, axis=mybir.AxisListType.X, op=ALU.add),
                      reads=b_vf, writes=[b_st4])
                tr.op("act", lambda e: e.activation(out=vsq, in_=vf, func=AF.Square), reads=b_vf, writes=b_vsq)
                tr.op("dve", lambda e: e.tensor_reduce(out=st4[:, 1:2], in_=vsq, axis=mybir.AxisListType.X, op=ALU.add),
                      reads=b_vsq, writes=[b_st4])
                tr.op("dve", lambda e: e.tensor_scalar(out=st4[:, 2:4], in0=st4[:, 0:2], scalar1=1.0 / 768.0, scalar2=None, op0=ALU.mult),
                      reads=[b_st4], writes=[b_st4])
                tr.op("dve", lambda e: e.tensor_tensor(out=st4[:, 4:5], in0=st4[:, 2:3], in1=st4[:, 2:3], op=ALU.mult),
                      reads=[b_st4], writes=[b_st4])
                tr.op("dve", lambda e: e.tensor_tensor(out=st4[:, 5:6], in0=st4[:, 3:4], in1=st4[:, 4:5], op=ALU.subtract),
                      reads=[b_st4], writes=[b_st4])
                tr.op("act", lambda e: e.activation(out=st4[:, 6:7], in_=st4[:, 5:6], func=AF.Ln, bias=epsb[:, 0:1], scale=1.0),
                      reads=[b_st4, b_epsb], writes=[b_st4])
                tr.op("act", lambda e: e.activation(out=st4[:, 6:7], in_=st4[:, 6:7], func=AF.Exp, scale=-0.5), reads=[b_st4], writes=[b_st4])
                tr.op("dve", lambda e: e.scalar_tensor_tensor(out=st4[:, 7:8], in0=st4[:, 2:3], scalar=-1.0, in1=st4[:, 6:7],
                                                              op0=ALU.mult, op1=ALU.mult), reads=[b_st4], writes=[b_st4])
                vn_, b_vn = vn[r % 2]
                tr.op("act", lambda e, vn_=vn_: e.activation(out=vn_, in_=vf, func=AF.Identity, bias=st4[:, 7:8], scale=st4[:, 6:7]),
                      reads=b_vf + [b_st4], writes=b_vn)
                yield 7.0
                pr2, b_pr2 = bankpair()
                pr2f = pr2.rearrange("p b c -> p (b c)")

                def fng(e, vn_=vn_, pr2f=pr2f):
                    last = None
                    for h in range(6):
                        last = e.matmul(pr2f[:, h * 128:(h + 1) * 128], vn_[:, h * 128:(h + 1) * 128], wsT[:, h, :], start=True, stop=True)
                    return last
                tr.op("pe", fng, reads=b_vn + [b_wsT], writes=b_pr2)
                for h in range(6):
                    tr.op("dve", lambda e, h=h, pr2f=pr2f: e.scalar_tensor_tensor(
                        out=t6[:, h, :], in0=pr2f[:, h * 128:(h + 1) * 128], scalar=lngC[:, h:h + 1], in1=Cst[:, h, :],
                        op0=ALU.mult, op1=ALU.add), reads=b_pr2 + [b_lngC, b_Cst], writes=b_t6)
                tr.op("dve", lambda e, r=r: e.tensor_tensor(out=ycT[:, :, r * 128:(r + 1) * 128], in0=t6,
                                                             in1=uT[:, :, r * 128:(r + 1) * 128], op=ALU.mult),
                      reads=b_t6 + b_uT, writes=b_ycT)
                yield 8.0
            tr.dma("sp", "st", YC[gi], ycT, reads=b_ycT, writes=[b_YC[gi]])
            yield 1.0

          return stageA, stageB, stageC, stageD, stageN2, stageN3


        def p3_load(idx, tiles3):
            if idx >= len(tiles3):
                return None
            seq, j = tiles3[idx]
            gi = j if seq == "S" else NT_S + j
            xT, b_xT = (xTa, b_xTa) if idx % 2 == 0 else (xTb, b_xTb)
            yc_, b_yc = yc3[idx % 2]
            yd_, b_yd = ydt[idx % 2]
            key = "p3a" if idx % 2 == 0 else "p3b"
            tr.dma("sp", key, xT[:, :, hsl], X2[gi], reads=[b_X2[gi]], writes=[b_xT])
            tr.dma("sp", key, yc_[:], YC[gi], reads=[b_YC[gi]], writes=[b_yc])
            tr.dma("sp", key, yd_[:], YD[:, :, gi * T:(gi + 1) * T].rearrange("c p t -> p c t"), reads=[b_YD], writes=[b_yd])
            return (xT, b_xT, yc_, b_yc, yd_, b_yd)

        def make_tile3(idx, tiles3):
            seq, j = tiles3[idx]
            si = 0 if seq == "S" else 1
            st = {}

            def stageP():
                st["ld"] = p3_load(idx, tiles3)
                xT, b_xT, yc_, b_yc, yd_, b_yd = st["ld"]
                yield 3.0
                for n in range(8):
                    slot, b_slot = ring_next(("w_out1", n))
                    w = slot.rearrange("p (k n) -> p k n", n=128)
                    po, b_po = bank()
                    pairs = [(w[:, k, :], yc_[:, k, :]) for k in range(6)] + [(w[:, 6 + c, :], yd_[:, c, :]) for c in range(2)]
                    mm_group(po, pairs, reads=[b_slot, b_yc, b_yd], writes=[b_po])
                    ring_done()
                    tr.op("dve", lambda e, po=po, n=n: e.scalar_tensor_tensor(
                        out=xT[:, n, hsl], in0=po, scalar=MOD(1, 2, n, si), in1=xT[:, n, hsl], op0=ALU.mult, op1=ALU.add),
                        reads=[b_po, b_xT, b_modr], writes=[b_xT])
                    yield 1.8

            hT3, b_hT3 = (hTa, b_hTa) if idx % 2 == 0 else (hTb, b_hTb)

            def stageN():
                xT, b_xT = st["ld"][0], st["ld"][1]
                yield from rms_to_h(xT, b_xT, 1, 1, si, HALO, T, hT=hT3, b_hT=b_hT3)

            def stageQ():
                xT, b_xT = st["ld"][0], st["ld"][1]
                yield from ffn(xT, b_xT, 1, si, hT3, b_hT3, do_norm=False)

            def stageR():
                xT, b_xT = st["ld"][0], st["ld"][1]
                yield from rms_to_h(xT, b_xT, 1, 0, si, HALO, T, plain_gain=fgm, scr=(sq2, b_sq2, rstd2, b_rstd2))
                ot, b_ot = otok[idx % 2]
                cnt = 0
                for r in range(4):
                    for half in range(2):
                        pb, b_pb = bank()

                        def fn(e, r=r, half=half, pb=pb):
                            last = None
                            for i in range(4):
                                last = e.transpose(pb[:, i * 128:(i + 1) * 128],
                                                   tbuf[:, 4 * half + i, HALO + r * 128:HALO + (r + 1) * 128], ident)
                            return last
                        tr.op("pe", fn, reads=[b_tbuf, b_cf], writes=[b_pb])
                        if cnt % 2:
                            tr.op("act", lambda e, pb=pb, r=r, half=half: e.activation(
                                out=ot[:, r, half * 512:(half + 1) * 512], in_=pb, func=AF.Copy), reads=[b_pb], writes=[b_ot])
                        else:
                            tr.op("dve", lambda e, pb=pb, r=r, half=half: e.tensor_copy(
                                out=ot[:, r, half * 512:(half + 1) * 512], in_=pb), reads=[b_pb], writes=[b_ot])
                        cnt += 1
                    yield 2.0
                ydst = ys if seq == "S" else yp
                tr.dma("sp", "out", ydst[j * T:(j + 1) * T, :].rearrange("(r p) d -> p r d", p=128), ot[:], reads=[b_ot])
                yield 0.5

            return stageP, stageQ, stageR, stageN

        return dict(close=ph.close, make_tile=make_tile if P1 else None, bufs=((xTa, b_xTa), (xTb, b_xTb)),
                    make_tile3=make_tile3)

    env = make_env(1)
    tiles1 = [("S", j) for j in range(NT_S)] + [("P", j) for j in range(NT_P)]
    if debug and "nt1" in debug:
        tiles1 = tiles1[:debug["nt1"]]

    def run_interleaved(gens):
        acc = [0.0] * len(gens)
        alive = [True] * len(gens)
        while any(alive):
            i = min((a, ix) for ix, a in enumerate(acc) if alive[ix])[1]
            try:
                acc[i] += next(gens[i])
            except StopIteration:
                alive[i] = False

    def after_tile(seq, j):
        if seq != "S":
            return
        if debug and debug.get("nocc"):
            for r_ in range(4):
                tr.dma("sp", "ph2", AG[j][r_], G_S[j], reads=[b_GS[j]], writes=[b_AG[j]])
        else:
            tr.custom("pool", "cc", 1, lambda e, j=j: e.collective_compute(
                "AllGather", ALU.bypass, replica_groups=[[0, 1, 2, 3], [4, 5, 6, 7]], ins=[G_S[j].rearrange("a r (c1 c2) b -> (a r c1) (c2 b)", c2=4)],
                outs=[AG[j].rearrange("k a r (c1 c2) b -> (k a r c1) (c2 b)", c2=4)]),
                reads=[b_GS[j]], writes=[b_AG[j]])

    def chain(*gs):
        for g_ in gs:
            yield from g_

    stages = []
    for idx, (seq, j) in enumerate(tiles1):
        xT_, b_xT_ = env["bufs"][idx % 2]
        nxt_tile = tiles1[idx + 1] if idx + 1 < len(tiles1) else None
        stages.append(env["make_tile"](seq, j, xT_, b_xT_, nxt_tile) + (seq, j))
    n1 = len(stages)
    if n1:
        run_interleaved([stages[0][0]()])
        run_interleaved([stages[0][1]()])
    for i in range(1, n1 + 1):
        th = []
        if i < n1:
            th.append(stages[i][0]())
        th.append(stages[i - 1][4]())
        gens = [chain(*th)]
        if i - 2 >= 0:
            gens.insert(0, stages[i - 2][3]())
        run_interleaved(gens)
        if i - 2 >= 0:
            after_tile(stages[i - 2][6], stages[i - 2][7])
        gens = [chain(stages[i - 1][2](), stages[i - 1][5]())]
        if i < n1:
            gens.insert(0, stages[i][1]())
        run_interleaved(gens)
    if n1:
        run_interleaved([stages[n1 - 1][3]()])
        after_tile(stages[n1 - 1][6], stages[n1 - 1][7])
    barrier()
    env["close"]()
    vres_scope.close()

    with ExitStack() as s2:
        def sb2(name, shape, dt=F32):
            return s2.enter_context(nc.sbuf_tensor(name, list(shape), dt)), Buf(name)
        GAs = [sb2("GA%d" % i, [128, 2, 64, 128], BF16) for i in range(2)]
        HB, b_HB = sb2("HB", [128, 2, 128, 128], BF16)
        yst, b_yst = sb2("yst", [128, SP_LEN], BF16)
        tt = [sb2("tt%d" % i, [128, 256], BF16) for i in range(4)]

        def params(seq):
            if seq == "P":
                return dict(Na=64, Nkb=128, ntok=SP_LEN, off=SS_LEN,
                            W11=cb[0:64, CB_W1P1:CB_W1P1 + 128], W12=cb[0:64, CB_W1P2:CB_W1P2 + 128],
                            Tr=cf[:, CF_TP:CF_TP + 64], Ti=cf[:, CF_TP + 64:CF_TP + 128],
                            W2r=cb[:, CB_W2P:CB_W2P + 128], W2i=cb[:, CB_W2P + 128:CB_W2P + 256])
            return dict(Na=128, Nkb=32, ntok=SS_LEN, off=0,
                        W11=cb[:, CB_W1S1:CB_W1S1 + 256], W12=cb[:, CB_W1S2:CB_W1S2 + 256],
                        Tr=cf[:, CF_TS:CF_TS + 128], Ti=cf[:, CF_TS + 128:CF_TS + 256],
                        W2r=cb[:, CB_W2S:CB_W2S + 32], W2i=cb[:, CB_W2S + 32:CB_W2S + 64])

        def fft_load(seq, cc, chh, GA, b_GA):
            cs = slice(chh * 64, (chh + 1) * 64)
            if seq == "P":
                for ri in range(2):
                    tr.dma("sp", "ph2", GA[0:64, ri, :, :], G_P[:, 2 * cc + ri, cs], reads=[b_GP], writes=[b_GA])
            else:
                for rk in range(4):
                    for ri in range(2):
                        tr.dma("sp", "ph2", GA[32 * rk:32 * rk + 32, ri, :, :], AG_all[:, rk, :, 2 * cc + ri, cs],
                               reads=b_AG, writes=[b_GA])

        def fft_stage1(seq, cc, chh, GA, b_GA):
            pr = params(seq)
            Na, W11, W12, Tr, Ti = pr["Na"], pr["W11"], pr["W12"], pr["Tr"], pr["Ti"]
            nch = 512 // (2 * Na)
            for c0 in range(0, 64, nch):
                pb, b_pb = bank()

                def fn1(e, c0=c0, pb=pb):
                    last = None
                    for i in range(nch):
                        o = pb[:, i * 2 * Na:(i + 1) * 2 * Na]
                        e.matmul(o, GA[0:Na, 0, c0 + i, :], W11, start=True, stop=False)
                        last = e.matmul(o, GA[0:Na, 1, c0 + i, :], W12, start=False, stop=True)
                    return last
                tr.op("pe", fn1, reads=[b_GA, b_cb], writes=[b_pb])
                pv = pb.rearrange("p (c r k) -> p c r k", r=2, k=Na)
                Hr, Hi = pv[:, :, 0, :], pv[:, :, 1, :]
                Trb = Tr.unsqueeze(1).broadcast_to([128, nch, Na])
                Tib = Ti.unsqueeze(1).broadcast_to([128, nch, Na])
                tv = [(t_[0][:, 0:nch * Na].rearrange("p (c k) -> p c k", k=Na), t_[1]) for t_ in tt]
                for (ti_, a_, b_) in ((0, Hr, Trb), (1, Hi, Tib), (2, Hr, Tib), (3, Hi, Trb)):
                    tr.op("dve", lambda e, ti_=ti_, a_=a_, b_=b_, tv=tv: e.tensor_tensor(out=tv[ti_][0], in0=a_, in1=b_, op=ALU.mult),
                          reads=[b_pb, b_cf], writes=[tv[ti_][1]])
                ch0 = chh * 64 + c0
                tr.op("dve", lambda e, ch0=ch0, tv=tv: e.tensor_tensor(
                    out=HB[:, 0, ch0:ch0 + nch, 0:Na], in0=tv[0][0], in1=tv[1][0], op=ALU.subtract),
                    reads=[tv[0][1], tv[1][1]], writes=[b_HB])
                tr.op("dve", lambda e, ch0=ch0, tv=tv: e.tensor_tensor(
                    out=HB[:, 1, ch0:ch0 + nch, 0:Na], in0=tv[2][0], in1=tv[3][0], op=ALU.add),
                    reads=[tv[2][1], tv[3][1]], writes=[b_HB])

        def fft_stage2(seq, cc):
            pr = params(seq)
            Na, Nkb, ntok, off, W2r, W2i = pr["Na"], pr["Nkb"], pr["ntok"], pr["off"], pr["W2r"], pr["W2i"]
            nka = 512 // Nkb
            ystv = yst[:, 0:ntok].rearrange("p (kb ka) -> p kb ka", ka=Na)
            for q_, ka0 in enumerate(range(0, Na, nka)):
                pb, b_pb = bank()

                def fn2(e, ka0=ka0, pb=pb):
                    last = None
                    for i in range(nka):
                        o = pb[:, i * Nkb:(i + 1) * Nkb]
                        e.matmul(o, HB[:, 0, :, ka0 + i], W2r, start=True, stop=False)
                        last = e.matmul(o, HB[:, 1, :, ka0 + i], W2i, start=False, stop=True)
                    return last
                tr.op("pe", fn2, reads=[b_HB, b_cb], writes=[b_pb])
                src = pb.rearrange("p (i k) -> p k i", k=Nkb)
                dst = ystv[:, :, ka0:ka0 + nka]
                if q_ % 2 == 0:
                    tr.op("act", lambda e, src=src, dst=dst: e.activation(out=dst, in_=src, func=AF.Copy), reads=[b_pb], writes=[b_yst])
                else:
                    tr.op("dve", lambda e, src=src, dst=dst: e.tensor_copy(out=dst, in_=src), reads=[b_pb], writes=[b_yst])
            tr.dma("sp", "ph2", YD[cc, :, off:off + ntok], yst[:, 0:ntok], reads=[b_yst], writes=[b_YD])

        units = [(seq, cc, chh) for seq in ("S", "P") for cc in range(2) for chh in range(2)]
        if debug and debug.get("nofft"):
            units = []
        for ui in range(min(2, len(units))):
            fft_load(*units[ui], *GAs[ui % 2])
        for ui, (seq, cc, chh) in enumerate(units):
            fft_stage1(seq, cc, chh, *GAs[ui % 2])
            if ui + 2 < len(units):
                fft_load(*units[ui + 2], *GAs[ui % 2])
            if chh == 1:
                fft_stage2(seq, cc)
        barrier()

    env = make_env(3)
    tiles3 = [("S", j) for j in range(NT_S)] + [("P", j) for j in range(NT_P)]
    if debug and "nt3" in debug:
        tiles3 = tiles3[:debug["nt3"]]
    st3 = [env["make_tile3"](idx, tiles3) for idx in range(len(tiles3))]
    n3 = len(st3)
    if n3:
        run_interleaved([chain(st3[0][0](), st3[0][3]())])
    for i in range(n3):
        side = []
        if i >= 1:
            side.append(st3[i - 1][2]())
        if i + 1 < n3:
            side.append(st3[i + 1][0]())
            side.append(st3[i + 1][3]())
        run_interleaved([st3[i][1](), chain(*side)])
    if n3:
        run_interleaved([st3[n3 - 1][2]()])
    barrier()
    tr.final_wait("sp", DMAKEYS)

    with nc.Block() as block:
        @block.tensor
        def _(e):
            for f in tr.q["pe"]:
                f(e)

        @block.scalar
        def _(e):
            for f in tr.q["act"]:
                f(e)

        @block.vector
        def _(e):
            for f in tr.q["dve"]:
                f(e)

        @block.gpsimd
        def _(e):
            for f in tr.q["pool"]:
                f(e)

        @block.sync
        def _(e):
            for f in tr.q["sp"]:
                f(e)
    return nc, tr, recorded


def _consts(q):
    cbm = np.zeros((128, CB_N), np.float64)
    cfm = np.zeros((128, CF_N), np.float64)
    cbm[:, CB_ONES:CB_ONES + 128] = 1.0 / 1024.0
    j = np.arange(64)
    ang = 2 * np.pi * np.outer(j, j) / 64.0
    for g in range(2):
        cbm[g * 64:(g + 1) * 64, CB_D0 + g * 64:CB_D0 + (g + 1) * 64] = np.cos(ang)
        cbm[g * 64:(g + 1) * 64, CB_D0 + 128 + g * 64:CB_D0 + 128 + (g + 1) * 64] = -np.sin(ang)
    a = np.arange(64)
    ang = 2 * np.pi * np.outer(a, a) / 64.0
    cbm[0:64, CB_W1P1:CB_W1P1 + 64] = np.cos(ang)
    cbm[0:64, CB_W1P1 + 64:CB_W1P1 + 128] = -np.sin(ang)
    cbm[0:64, CB_W1P2:CB_W1P2 + 64] = np.sin(ang)
    cbm[0:64, CB_W1P2 + 64:CB_W1P2 + 128] = np.cos(ang)
    a = np.arange(128)
    ang = 2 * np.pi * np.outer(a, a) / 128.0
    cbm[:, CB_W1S1:CB_W1S1 + 128] = np.cos(ang)
    cbm[:, CB_W1S1 + 128:CB_W1S1 + 256] = -np.sin(ang)
    cbm[:, CB_W1S2:CB_W1S2 + 128] = np.sin(ang)
    cbm[:, CB_W1S2 + 128:CB_W1S2 + 256] = np.cos(ang)
    scP = 1.0 / np.sqrt(64.0 * 8192.0)
    scS = 1.0 / np.sqrt(64.0 * 16384.0)
    b = np.arange(128)
    ang = 2 * np.pi * np.outer(b, np.arange(128)) / 128.0
    cbm[:, CB_W2P:CB_W2P + 128] = np.cos(ang) * scP
    cbm[:, CB_W2P + 128:CB_W2P + 256] = np.sin(ang) * scP
    kb = 32 * q + np.arange(32)
    ang = 2 * np.pi * np.outer(b, kb) / 128.0
    cbm[:, CB_W2S:CB_W2S + 32] = np.cos(ang) * scS
    cbm[:, CB_W2S + 32:CB_W2S + 64] = np.sin(ang) * scS
    cfm[:, CF_ID:CF_ID + 128] = np.eye(128)
    cfm[:, CF_ONES512:CF_ONES512 + 128] = 1.0 / 512.0
    ang = 2 * np.pi * np.outer(b, np.arange(64)) / 8192.0
    cfm[:, CF_TP:CF_TP + 64] = np.cos(ang)
    cfm[:, CF_TP + 64:CF_TP + 128] = -np.sin(ang)
    ang = 2 * np.pi * np.outer(b, np.arange(128)) / 16384.0
    cfm[:, CF_TS:CF_TS + 128] = np.cos(ang)
    cfm[:, CF_TS + 128:CF_TS + 256] = -np.sin(ang)
    cfm[:, CF_MSK] = 0.0 if q == 0 else 1.0
    cfm[:, CF_MSK + 1] = 0.0 if q == 3 else 1.0
    return cbm.astype(ml_dtypes.bfloat16), cfm.astype(np.float32)


_CACHE = {}


def make_in_maps(inputs):
    f = lambda k: np.ascontiguousarray(np.asarray(inputs[k], dtype=np.float32))
    x_prompt, x_sample = f("x_prompt"), f("x_sample")
    c_prompt, c_sample = f("c_prompt"), f("c_sample")
    shared = {
        "ada_w": f("ada_w"), "ada_b": f("ada_b"), "mix_norm_g": f("mix_norm_g"), "ffn_norm_g": f("ffn_norm_g"),
        "ab_w_in": f("ab_w_in")[0], "ab_conv_a": f("ab_conv_a")[0], "ab_conv_b_w": f("ab_conv_b_w")[0],
        "ab_conv_b_b": f("ab_conv_b_b")[0], "ab_ln_g": f("ab_ln_g")[0], "ab_ln_b": f("ab_ln_b")[0],
        "ab_w_out": f("ab_w_out")[0], "cd_w_in": f("cd_w_in")[0], "cd_ln_g": f("cd_ln_g")[0], "cd_ln_b": f("cd_ln_b")[0],
        "cd_w_s": f("cd_w_s")[0], "cd_b_s": f("cd_b_s")[0], "cd_w_out": f("cd_w_out")[0],
        "ffn_w_in": f("ffn_w_in"), "ffn_w_out": f("ffn_w_out"), "final_g": f("final_g"),
    }
    in_maps = []
    for c in range(8):
        bq, q = c // 4, c % 4
        xpp = np.zeros((SP_LEN + 2 * HALO, D), np.float32)
        xpp[HALO:HALO + SP_LEN] = x_prompt[c]
        xsp = np.zeros((SS_LEN + 2 * HALO, D), np.float32)
        lo, hi = q * SS_LEN - HALO, (q + 1) * SS_LEN + HALO
        slo, shi = max(lo, 0), min(hi, 4 * SS_LEN)
        xsp[slo - lo:slo - lo + (shi - slo)] = x_sample[bq, slo:shi]
        cbm, cfm = _consts(q)
        m = dict(shared)
        m.update({"xp": xpp, "xs": xsp, "cvec": np.stack([c_sample[bq], c_prompt[c]], 0), "cb": cbm, "cf": cfm})
        in_maps.append(m)
    return in_maps


def kernel(**inputs):
    if "nc" not in _CACHE:
        specs = build_program()[2]
        _CACHE["nc"] = build_program(plan_specs=specs)[0]
    nc = _CACHE["nc"]
    in_maps = make_in_maps(inputs)
    res = run_bass_kernel_spmd(nc, in_maps, core_ids=list(range(8)))
    y_prompt = np.stack([np.asarray(res.results[c]["yp"], dtype=np.float32) for c in range(8)], 0)
    y_sample = np.stack([
        np.concatenate([np.asarray(res.results[b * 4 + q]["ys"], dtype=np.float32) for q in range(4)], 0)
        for b in range(2)], 0)
    return (y_prompt, y_sample)
```
